# Optimizing a Trainium2 kernel written in Bass

```python
import math
import jax, jax.numpy as jnp
from jax import lax
import numpy as np

D_MODEL = 1024
BATCH = 8
SEQ = 8192
DEPTH = 1
DEC_BATCH = 32
DEC_SEQ = 32
PAST_LEN = 1024

CHUNK = 64
Q_BLOCK = 128
N_MEM = 256
EPS = 1e-6
DA_HEADS = 4
DA_HEAD_DIM = D_MODEL // 16
DA_VDIM = 2 * DA_HEAD_DIM
DA_QK = DA_HEADS * 2 * DA_HEAD_DIM
DA_WIDTH = DA_HEADS * DA_VDIM
ROPE_THETA = 500000.0
ROT_DIM = DA_HEAD_DIM // 4
LRU_WIDTH = D_MODEL // 4
LRU_BLOCKS = 4
LRU_BLOCK = LRU_WIDTH // LRU_BLOCKS
CONV_W = 4
LRU_C = 8.0
MX_HEADS = 4
MX_HEAD_DIM = D_MODEL // 16
MX_WIDTH = MX_HEADS * MX_HEAD_DIM
MIX_WIDTH = DA_WIDTH + LRU_WIDTH + MX_WIDTH
IN_SIZES = (DA_QK, DA_QK, DA_WIDTH, DA_WIDTH, LRU_WIDTH, LRU_WIDTH, MX_WIDTH, MX_WIDTH)
IN_WIDTH = 2 * DA_QK + 2 * DA_WIDTH + 2 * LRU_WIDTH + 2 * MX_WIDTH

kernel_name = "hybrid_diffattn_rglru_memxattn_stream_step"


def rms_norm(x, g):
    xf = x.astype(jnp.float32)
    y = xf * lax.rsqrt(jnp.mean(xf * xf, axis=-1, keepdims=True) + EPS)
    return (y * g.astype(jnp.float32)).astype(x.dtype)


def rope(x, pos):
    half = ROT_DIM // 2
    inv = ROPE_THETA ** (-jnp.arange(0, ROT_DIM, 2, dtype=jnp.float32) / ROT_DIM)
    ang = pos.astype(jnp.float32)[:, None] * inv[None, :]
    shape = (1, pos.shape[0]) + (1,) * (x.ndim - 3) + (half,)
    cos = jnp.cos(ang).reshape(shape)
    sin = jnp.sin(ang).reshape(shape)
    xr = x[..., :ROT_DIM].astype(jnp.float32)
    x1, x2 = xr[..., :half], xr[..., half:]
    rot = jnp.concatenate([x1 * cos - x2 * sin, x2 * cos + x1 * sin], axis=-1).astype(x.dtype)
    return jnp.concatenate([rot, x[..., ROT_DIM:]], axis=-1)


def split_cols(t, sizes):
    idx, acc = [], 0
    for s in sizes[:-1]:
        acc += s
        idx.append(acc)
    return jnp.split(t, idx, axis=-1)


def diff_attend(q, k, v, q_pos, k_pos, lam):
    s = jnp.einsum('bqhcd,bkhcd->bhcqk', q, k).astype(jnp.float32) * (DA_HEAD_DIM ** -0.5)
    mask = (k_pos[None, :] // CHUNK) <= (q_pos[:, None] // CHUNK)
    s = jnp.where(mask[None, None, None], s, jnp.float32(-1e30))
    p = jax.nn.softmax(s, axis=-1)
    w = p[:, :, 0] - lam * p[:, :, 1]
    return jnp.einsum('bhqk,bkhe->bqhe', w.astype(v.dtype), v)


def lru_branch(xb, conv_buf, h0, conv_w, conv_b, w_a, b_a, w_x, b_x, lru_lambda):
    B, T, W = xb.shape
    xp = jnp.concatenate([conv_buf.astype(xb.dtype), xb], axis=1)
    xc = conv_b
    for j in range(CONV_W):
        xc = xc + xp[:, j:j + T] * conv_w[j]
    new_buf = xp[:, -(CONV_W - 1):]
    xbk = xc.reshape(B, T, LRU_BLOCKS, LRU_BLOCK)
    r = jax.nn.sigmoid(jnp.einsum('btni,nij->btnj', xbk, w_a).reshape(B, T, W) + b_a)
    i = jax.nn.sigmoid(jnp.einsum('btni,nij->btnj', xbk, w_x).reshape(B, T, W) + b_x)
    log_a = -LRU_C * r.astype(jnp.float32) * jax.nn.softplus(-lru_lambda.astype(jnp.float32))
    a = jnp.exp(log_a)
    b = jnp.sqrt(-jnp.expm1(2.0 * log_a)) * (i * xc).astype(jnp.float32)

    def combine(e, l):
        return (e[0] * l[0], l[0] * e[1] + l[1])

    a_cum, h = lax.associative_scan(combine, (a, b), axis=1)
    h = h + a_cum * h0.astype(jnp.float32)[:, None, :]
    return h.astype(xb.dtype), new_buf, h[:, -1].astype(xb.dtype)


def mem_kv(mem, mem_norm_g, w_mem_kv, mx_k_norm_g):
    B, N, _ = mem.shape
    kv = rms_norm(mem, mem_norm_g) @ w_mem_kv
    k, v = split_cols(kv, (MX_WIDTH, MX_WIDTH))
    k = rms_norm(k.reshape(B, N, MX_HEADS, MX_HEAD_DIM), mx_k_norm_g)
    return k, v.reshape(B, N, MX_HEADS, MX_HEAD_DIM)


def layer(x, q_pos, past_k, past_v, conv_buf, h0, m_k, m_v, p, lambda_init, blocked):
    B, T, _ = x.shape
    hn = rms_norm(x, p['norm_g'])
    proj = hn @ p['w_in']
    dq, dk, dv, dg, lx, lg, mq, mg = split_cols(proj, IN_SIZES)
    q = rope(rms_norm(dq.reshape(B, T, DA_HEADS, 2, DA_HEAD_DIM), p['da_q_norm_g']), q_pos)
    k_new = rope(rms_norm(dk.reshape(B, T, DA_HEADS, 2, DA_HEAD_DIM), p['da_k_norm_g']), q_pos)
    v_new = dv.reshape(B, T, DA_HEADS, DA_VDIM)
    if past_k is None:
        k_all, v_all, k_pos = k_new, v_new, q_pos
    else:
        k_all = jnp.concatenate([past_k.astype(k_new.dtype), k_new], axis=1)
        v_all = jnp.concatenate([past_v.astype(v_new.dtype), v_new], axis=1)
        k_pos = jnp.arange(past_k.shape[1] + T, dtype=jnp.int32)
    f32 = jnp.float32
    lam = (jnp.exp(jnp.sum(p['lambda_q1'].astype(f32) * p['lambda_k1'].astype(f32)))
           - jnp.exp(jnp.sum(p['lambda_q2'].astype(f32) * p['lambda_k2'].astype(f32))) + lambda_init)
    if blocked:
        nb = T // Q_BLOCK
        qb = q.reshape(B, nb, Q_BLOCK, DA_HEADS, 2, DA_HEAD_DIM).transpose(1, 0, 2, 3, 4, 5)
        pb = q_pos.reshape(nb, Q_BLOCK)
        ob = lax.map(lambda a: diff_attend(a[0], k_all, v_all, a[1], k_pos, lam), (qb, pb))
        o = ob.transpose(1, 0, 2, 3, 4).reshape(B, T, DA_HEADS, DA_VDIM)
    else:
        o = diff_attend(q, k_all, v_all, q_pos, k_pos, lam)
    o = rms_norm(o, p['da_subln_g']) * (1.0 - lambda_init)
    out_a = o.reshape(B, T, DA_WIDTH) * jax.nn.silu(dg)
    hl, new_buf, h_last = lru_branch(lx, conv_buf, h0, p['lru_conv_w'], p['lru_conv_b'], p['lru_w_a'],
                                     p['lru_b_a'], p['lru_w_x'], p['lru_b_x'], p['lru_lambda'])
    out_b = hl * jax.nn.silu(lg)
    qm = rms_norm(mq.reshape(B, T, MX_HEADS, MX_HEAD_DIM), p['mx_q_norm_g'])
    sm = jnp.einsum('bqhd,bkhd->bhqk', qm, m_k.astype(qm.dtype)).astype(f32) * (MX_HEAD_DIM ** -0.5)
    pm = jax.nn.softmax(sm, axis=-1)
    om = jnp.einsum('bhqk,bkhd->bqhd', pm.astype(qm.dtype), m_v.astype(qm.dtype)).reshape(B, T, MX_WIDTH)
    out_c = om * jax.nn.silu(mg)
    y = x + jnp.concatenate([out_a, out_b, out_c], axis=-1) @ p['w_out']
    return y, k_new, v_new, new_buf, h_last


def setup_inputs(seed: int = 0) -> dict:
    key = jax.random.key(seed)
    ks = jax.random.split(key, 40)
    n = lambda i, shape, s=1.0: jax.random.normal(ks[i], shape, jnp.float32) * s
    a8 = jax.random.uniform(ks[39], (DEPTH, LRU_WIDTH), jnp.float32, 0.9, 0.999)
    a0 = a8 ** (1.0 / LRU_C)
    return {
        'x_prompt': n(0, (BATCH, SEQ, D_MODEL)),
        'x_sample': n(1, (DEC_BATCH, DEC_SEQ, D_MODEL)),
        'mem_prompt': n(2, (BATCH, N_MEM, D_MODEL)),
        'cache_diff_k': n(3, (DEPTH, DEC_BATCH, PAST_LEN, DA_HEADS, 2, DA_HEAD_DIM)),
        'cache_diff_v': n(4, (DEPTH, DEC_BATCH, PAST_LEN, DA_HEADS, DA_VDIM)),
        'cache_mem_k': n(5, (DEPTH, DEC_BATCH, N_MEM, MX_HEADS, MX_HEAD_DIM)),
        'cache_mem_v': n(6, (DEPTH, DEC_BATCH, N_MEM, MX_HEADS, MX_HEAD_DIM)),
        'state_lru_conv': n(7, (DEPTH, DEC_BATCH, CONV_W - 1, LRU_WIDTH)),
        'state_lru_h': n(8, (DEPTH, DEC_BATCH, LRU_WIDTH)),
        'norm_g': 1.0 + n(9, (DEPTH, D_MODEL), 0.02),
        'w_in': n(10, (DEPTH, D_MODEL, IN_WIDTH), D_MODEL ** -0.5),
        'da_q_norm_g': 1.0 + n(11, (DEPTH, DA_HEAD_DIM), 0.02),
        'da_k_norm_g': 1.0 + n(12, (DEPTH, DA_HEAD_DIM), 0.02),
        'lambda_q1': n(13, (DEPTH, DA_HEAD_DIM), 0.1),
        'lambda_k1': n(14, (DEPTH, DA_HEAD_DIM), 0.1),
        'lambda_q2': n(15, (DEPTH, DA_HEAD_DIM), 0.1),
        'lambda_k2': n(16, (DEPTH, DA_HEAD_DIM), 0.1),
        'da_subln_g': 1.0 + n(17, (DEPTH, DA_VDIM), 0.02),
        'lru_conv_w': n(18, (DEPTH, CONV_W, LRU_WIDTH), CONV_W ** -0.5),
        'lru_conv_b': n(19, (DEPTH, LRU_WIDTH), 0.01),
        'lru_w_a': n(20, (DEPTH, LRU_BLOCKS, LRU_BLOCK, LRU_BLOCK), LRU_BLOCK ** -0.5),
        'lru_b_a': n(21, (DEPTH, LRU_WIDTH), 0.01),
        'lru_w_x': n(22, (DEPTH, LRU_BLOCKS, LRU_BLOCK, LRU_BLOCK), LRU_BLOCK ** -0.5),
        'lru_b_x': n(23, (DEPTH, LRU_WIDTH), 0.01),
        'lru_lambda': jnp.log(a0) - jnp.log1p(-a0),
        'mem_norm_g': 1.0 + n(24, (DEPTH, D_MODEL), 0.02),
        'w_mem_kv': n(25, (DEPTH, D_MODEL, 2 * MX_WIDTH), D_MODEL ** -0.5),
        'mx_q_norm_g': 1.0 + n(26, (DEPTH, MX_HEAD_DIM), 0.02),
        'mx_k_norm_g': 1.0 + n(27, (DEPTH, MX_HEAD_DIM), 0.02),
        'w_out': n(28, (DEPTH, MIX_WIDTH, D_MODEL), MIX_WIDTH ** -0.5),
    }


def reference(x_prompt, x_sample, mem_prompt, cache_diff_k, cache_diff_v, cache_mem_k, cache_mem_v,
              state_lru_conv, state_lru_h, norm_g, w_in, da_q_norm_g, da_k_norm_g, lambda_q1, lambda_k1,
              lambda_q2, lambda_k2, da_subln_g, lru_conv_w, lru_conv_b, lru_w_a, lru_b_a, lru_w_x, lru_b_x,
              lru_lambda, mem_norm_g, w_mem_kv, mx_q_norm_g, mx_k_norm_g, w_out):
    Bp, Tp, _ = x_prompt.shape
    Bs, Ts, _ = x_sample.shape
    pos_p = jnp.arange(Tp, dtype=jnp.int32)
    pos_s = cache_diff_k.shape[2] + jnp.arange(Ts, dtype=jnp.int32)
    yp, ys = x_prompt, x_sample
    kp_l, vp_l, mkp_l, mvp_l, cp_l, hp_l = [], [], [], [], [], []
    ks_l, vs_l, cs_l, hs_l = [], [], [], []
    for l in range(DEPTH):
        p = dict(norm_g=norm_g[l], w_in=w_in[l], da_q_norm_g=da_q_norm_g[l], da_k_norm_g=da_k_norm_g[l],
                 lambda_q1=lambda_q1[l], lambda_k1=lambda_k1[l], lambda_q2=lambda_q2[l], lambda_k2=lambda_k2[l],
                 da_subln_g=da_subln_g[l], lru_conv_w=lru_conv_w[l], lru_conv_b=lru_conv_b[l],
                 lru_w_a=lru_w_a[l], lru_b_a=lru_b_a[l], lru_w_x=lru_w_x[l], lru_b_x=lru_b_x[l],
                 lru_lambda=lru_lambda[l], mx_q_norm_g=mx_q_norm_g[l], w_out=w_out[l])
        lambda_init = 0.8 - 0.6 * math.exp(-0.3 * l)
        mk_p, mv_p = mem_kv(mem_prompt, mem_norm_g[l], w_mem_kv[l], mx_k_norm_g[l])
        zbuf = jnp.zeros((Bp, CONV_W - 1, LRU_WIDTH), yp.dtype)
        zh = jnp.zeros((Bp, LRU_WIDTH), yp.dtype)
        yp, k_p, v_p, c_p, h_p = layer(yp, pos_p, None, None, zbuf, zh, mk_p, mv_p, p, lambda_init,
                                       Tp > Q_BLOCK and Tp % Q_BLOCK == 0)
        ys, k_s, v_s, c_s, h_s = layer(ys, pos_s, cache_diff_k[l], cache_diff_v[l], state_lru_conv[l],
                                       state_lru_h[l], cache_mem_k[l], cache_mem_v[l], p, lambda_init, False)
        kp_l.append(k_p); vp_l.append(v_p); mkp_l.append(mk_p); mvp_l.append(mv_p)
        cp_l.append(c_p); hp_l.append(h_p)
        ks_l.append(k_s); vs_l.append(v_s); cs_l.append(c_s); hs_l.append(h_s)
    return (yp, ys,
            jnp.stack(kp_l), jnp.stack(vp_l), jnp.stack(mkp_l), jnp.stack(mvp_l),
            jnp.stack(cp_l), jnp.stack(hp_l),
            jnp.stack(ks_l), jnp.stack(vs_l), jnp.stack(cs_l), jnp.stack(hs_l))
```

```python
import math
from contextlib import ExitStack

import numpy as np
import concourse.bass as bass
import concourse.mybir as mybir
from concourse.bass_utils import run_bass_kernel_spmd

F32 = mybir.dt.float32
BF16 = mybir.dt.bfloat16
ALU = mybir.AluOpType
AF = mybir.ActivationFunctionType
AX = mybir.AxisListType

T = 8192
D = 1024
BT = 512
NBLK = T // BT
PAST = 1024
LAMBDA_INIT = 0.8 - 0.6 * math.exp(0.0)
EPS = 1e-6
ROPE_THETA = 500000.0


class Res:
    __slots__ = ("name", "w", "rs", "sem", "ndma", "excl")

    def __init__(self, name, excl=False):
        self.name = name
        self.w = None
        self.rs = []
        self.sem = None
        self.ndma = 0
        self.excl = excl


class Op:
    __slots__ = ("eng", "fn", "deps", "signal", "kind", "sem", "val")


class Sched:
    ENGS = ("pe", "act", "dve", "pool", "sp")

    def __init__(self, nc, stack):
        self.nc = nc
        self.stack = stack
        self.streams = {e: [] for e in self.ENGS}
        self.esem = {e: stack.enter_context(nc.semaphore("es_" + e)) for e in self.ENGS}
        self.nsem = len(self.ENGS)
        self.last_dma = {}

    def _add(self, eng, fn, reads, writes, kind):
        op = Op()
        op.eng = eng
        op.fn = fn
        op.kind = kind
        op.signal = False
        op.sem = None
        op.val = None
        deps = {}
        for r in reads:
            if r.excl:
                continue
            if r.w is not None:
                deps[id(r.w)] = (r.w, "raw")
        for w in list(writes) + [r for r in reads if r.excl]:
            if w.w is not None:
                deps[id(w.w)] = (w.w, "waw")
            for rd in w.rs:
                if id(rd) not in deps:
                    deps[id(rd)] = (rd, "war")
        op.deps = []
        for p, typ in deps.values():
            if p.kind == "c" and kind == "c" and p.eng == eng:
                if eng == "pe" or typ == "war":
                    continue
            op.deps.append(p)
            if p.kind == "c":
                p.signal = True
        for r in reads:
            if r.excl:
                r.w = op
                r.rs = []
            else:
                r.rs.append(op)
        for w in writes:
            w.w = op
            w.rs = []
        self.streams[eng].append(op)
        return op

    def op(self, eng, fn, reads=(), writes=()):
        return self._add(eng, fn, reads, writes, "c")

    def dma(self, q, out, in_, reads=(), writes=(), semres=None, **kw):
        if semres.sem is None:
            semres.sem = self.stack.enter_context(self.nc.semaphore("ds_%d" % self.nsem))
            self.nsem += 1
        op = self._add(q, lambda e: e.dma_start(out=out, in_=in_, **kw), reads, writes, "d")
        semres.ndma += 1
        op.sem = semres.sem
        op.val = 16 * semres.ndma
        self.last_dma[id(op.sem)] = op
        return op

    def barrier(self):
        lasts = []
        for e in self.ENGS:
            for op in reversed(self.streams[e]):
                if op.kind == "c" and op.fn is not None:
                    lasts.append(op)
                    break
        dmas = list(self.last_dma.values())
        for e in self.ENGS:
            op = Op()
            op.eng = e
            op.fn = None
            op.kind = "c"
            op.signal = False
            op.sem = None
            op.val = None
            op.deps = [p for p in lasts if p.eng != e] + dmas
            for p in op.deps:
                if p.kind == "c":
                    p.signal = True
            self.streams[e].append(op)

    def emit(self):
        nc = self.nc
        for e in self.ENGS:
            cnt = 0
            for op in self.streams[e]:
                if op.kind == "c" and op.signal:
                    cnt += 1
                    op.sem = self.esem[e]
                    op.val = cnt
        last_dma = self.last_dma
        streams = self.streams
        esem = self.esem

        def run(e, eng):
            waited = {}
            for op in streams[e]:
                for p in op.deps:
                    k = id(p.sem)
                    if waited.get(k, 0) >= p.val:
                        continue
                    waited[k] = p.val
                    eng.wait_ge(p.sem, p.val)
                if op.fn is None:
                    continue
                ins = op.fn(eng)
                if op.kind == "d":
                    ins.then_inc(op.sem, 16)
                elif op.signal:
                    ins.then_inc(esem[e], 1)
            if e == "sp":
                for p in last_dma.values():
                    if waited.get(id(p.sem), 0) < p.val:
                        eng.wait_ge(p.sem, p.val)

        with nc.Block() as block:
            @block.tensor
            def _(eng):
                run("pe", eng)

            @block.scalar
            def _(eng):
                run("act", eng)

            @block.vector
            def _(eng):
                run("dve", eng)

            @block.gpsimd
            def _(eng):
                run("pool", eng)

            @block.sync
            def _(eng):
                run("sp", eng)


class Buf:
    __slots__ = ("t", "r")

    def __init__(self, t, r):
        self.t = t
        self.r = r


def build_nc():
    nc = bass.Bass("TRN2", target_bir_lowering=False)

    def din(name, shape, dt=F32):
        return nc.dram_tensor(name, list(shape), dt, kind="ExternalInput").ap()

    def dout(name, shape):
        return nc.dram_tensor(name, list(shape), F32, kind="ExternalOutput").ap()

    def dscr(name, shape, dt=BF16):
        return nc.dram_tensor(name, list(shape), dt).ap()

    xp = din("xp", [T, D]); xs = din("xs", [128, D]); mem = din("mem", [256, D])
    ck = din("ck", [4, PAST, 512]); cv = din("cv", [4, PAST, 512])
    cmk = din("cmk", [4, 256, 256]); cmv = din("cmv", [4, 256, 256])
    sconv = din("sconv", [4, 3, 256]); sh = din("sh", [4, 256])
    w_in = din("w_in", [D, 3072]); w_out = din("w_out", [D, D]); w_mkv = din("w_mkv", [D, 512])
    norm_g = din("norm_g", [D]); mem_norm_g = din("mem_norm_g", [D])
    qg = din("qg", [64]); kg = din("kg", [64]); mqg = din("mqg", [64]); mkg = din("mkg", [64])
    lams = din("lams", [4, 64]); subg = din("subg", [128])
    convw = din("convw", [4, 256]); convb = din("convb", [256]); b_a = din("b_a", [256]); b_x = din("b_x", [256])
    lrul = din("lrul", [256]); w_a = din("w_a", [4, 64, 64]); w_x = din("w_x", [4, 64, 64])
    ident = din("ident", [128, 128]); ropep = din("ropep", [128, (T // 128) * 16]); ropes = din("ropes", [128, 16])

    yp = dout("yp", [T, D]); ys = dout("ys", [128, D])
    kp = dout("kp", [T, 512]); vp = dout("vp", [T, 512])
    mkp = dout("mkp", [256, 256]); mvp = dout("mvp", [256, 256])
    cpo = dout("cpo", [3, 256]); hpo = dout("hpo", [256])
    kso = dout("kso", [128, 512]); vso = dout("vso", [128, 512])
    cso = dout("cso", [4, 3, 256]); hso = dout("hso", [4, 256])

    QT_s = dscr("QT_s", [4, 128, T]); KT_s = dscr("KT_s", [4, 128, T]); V_s = dscr("V_s", [T, 512])
    OT_s = dscr("OT_s", [4, 128, T]); G_s = dscr("G_s", [4, 128, T]); MBC_s = dscr("MBC_s", [4, 128, T])

    gst = ExitStack()
    with gst:
        S = Sched(nc, gst)
        uid = [0]

        def sbuf(st, shape, dt, name=None):
            uid[0] += 1
            nm = "%s_%d" % (name or "b", uid[0])
            return Buf(st.enter_context(nc.sbuf_tensor(nm, list(shape), dt)), Res(nm))

        class Ring:
            def __init__(self, st, n, shape, dt, name):
                self.bufs = [sbuf(st, shape, dt, name) for _ in range(n)]
                self.i = 0

            def next(self):
                b = self.bufs[self.i % len(self.bufs)]
                self.i += 1
                return b

        ps = gst.enter_context(nc.psum_tensor("ps", [128, 8, 512], F32))
        rb = [Res("bank%d" % i, excl=True) for i in range(8)]

        def bank_bf(b):
            return ps[:, b, :].bitcast(BF16).rearrange("p (a c) -> p a c", c=128)

        def mm(out, lhsT, rhs, start, stop, r, w, skip=False):
            if skip:
                S.op("pe", lambda e: e.matmul(out=out, lhsT=lhsT, rhs=rhs, start=start, stop=stop,
                                              skip_group_check=True), r, w)
            else:
                S.op("pe", lambda e: e.matmul(out=out, lhsT=lhsT, rhs=rhs, start=start, stop=stop), r, w)

        def tr(out, in_, r, w):
            S.op("pe", lambda e: e.transpose(out=out, in_=in_, identity=identb.t[:]), list(r) + [identb.r], w)

        def tt(eng, out, in0, in1, op, r, w):
            S.op(eng, lambda e: e.tensor_tensor(out=out, in0=in0, in1=in1, op=op), r, w)

        def ts(eng, out, in0, s1, s2, op0, op1, r, w):
            if s2 is None:
                S.op(eng, lambda e: e.tensor_scalar(out=out, in0=in0, scalar1=s1, scalar2=None, op0=op0), r, w)
            else:
                S.op(eng, lambda e: e.tensor_scalar(out=out, in0=in0, scalar1=s1, scalar2=s2, op0=op0, op1=op1), r, w)

        def stt(eng, out, in0, scalar, in1, op0, op1, r, w, accum_out=None):
            if accum_out is None:
                S.op(eng, lambda e: e.scalar_tensor_tensor(out=out, in0=in0, scalar=scalar, in1=in1, op0=op0, op1=op1), r, w)
            else:
                S.op(eng, lambda e: e.scalar_tensor_tensor(out=out, in0=in0, scalar=scalar, in1=in1, op0=op0, op1=op1,
                                                           accum_out=accum_out), r, w)

        def act(out, in_, func, r, w, bias=None, scale=1.0, accum_out=None):
            if accum_out is not None:
                S.op("act", lambda e: e.activation(out=out, in_=in_, func=func, scale=scale, accum_out=accum_out), r, w)
            elif bias is None:
                S.op("act", lambda e: e.activation(out=out, in_=in_, func=func, scale=scale), r, w)
            else:
                S.op("act", lambda e: e.activation(out=out, in_=in_, func=func, bias=bias, scale=scale), r, w)

        def cp(eng, out, in_, r, w):
            if eng == "act":
                S.op("act", lambda e: e.copy(out=out, in_=in_), r, w)
            else:
                S.op(eng, lambda e: e.tensor_copy(out=out, in_=in_), r, w)

        def memset(eng, ap, val, w):
            S.op(eng, lambda e: e.memset(ap, val), [], w)

        def red(eng, out, in_, op, r, w, absval=False):
            if absval:
                S.op(eng, lambda e: e.tensor_reduce(out=out, in_=in_, axis=AX.X, op=op, apply_absolute_value=True), r, w)
            else:
                S.op(eng, lambda e: e.tensor_reduce(out=out, in_=in_, axis=AX.X, op=op), r, w)

        def recip(out, in_, r, w):
            S.op("dve", lambda e: e.reciprocal(out=out, in_=in_), r, w)

        def scan(out, d0, d1, init, r, w):
            S.op("dve", lambda e: e.tensor_tensor_scan(out=out, data0=d0, data1=d1, initial=init,
                                                       op0=ALU.mult, op1=ALU.add), r, w)

        def dma(q, out, in_, r, w, semres, **kw):
            S.dma(q, out, in_, reads=r, writes=w, semres=semres, **kw)

        nc_ctx = nc.allow_non_contiguous_dma(reason="small strided parameter / state transfers")
        gst.enter_context(nc_ctx)

        identf = sbuf(gst, [128, 128], F32, "identf")
        identb = sbuf(gst, [128, 128], BF16, "identb")
        dma("sp", identf.t[:], ident, [], [identf.r], identf.r)
        cp("dve", identb.t[:], identf.t[:], [identf.r], [identb.r])
        onesb = sbuf(gst, [128, 128], BF16, "onesb")
        memset("pool", onesb.t[:], 1.0, [onesb.r])
        onesm = sbuf(gst, [128, 128], BF16, "onesm")
        memset("pool", onesm.t[:], 1.0 / 128.0, [onesm.r])
        cst = sbuf(gst, [128, 4], F32, "cst")
        memset("pool", cst.t[:, 0:1], EPS, [cst.r])
        memset("pool", cst.t[:, 1:2], 1.0, [cst.r])
        eps_t = cst.t[:, 0:1]
        one_t = cst.t[:, 1:2]

        gv = sbuf(gst, [128, 4, 64], F32, "gv")
        for i, v in enumerate((qg, kg, mqg, mkg)):
            dma("sp", gv.t[:, i, :], v.partition_broadcast(128), [], [gv.r], gv.r)
        gfull = sbuf(gst, [128, 20, 64], F32, "gfull")
        cp("pool", gfull.t[:, 0:8, :], gv.t[:, 0:1, :].to_broadcast([128, 8, 64]), [gv.r], [gfull.r])
        cp("pool", gfull.t[:, 8:16, :], gv.t[:, 1:2, :].to_broadcast([128, 8, 64]), [gv.r], [gfull.r])
        cp("pool", gfull.t[:, 16:20, :], gv.t[:, 2:3, :].to_broadcast([128, 4, 64]), [gv.r], [gfull.r])
        gmk = sbuf(gst, [128, 4, 64], F32, "gmk")
        cp("pool", gmk.t[:, :, :], gv.t[:, 3:4, :].to_broadcast([128, 4, 64]), [gv.r], [gmk.r])
        sv = sbuf(gst, [128, 16], F32, "sv")
        red("dve", sv.t[:, 0:4], gv.t[:, :, :], ALU.max, [gv.r], [sv.r], absval=True)
        tt("dve", sv.t[:, 4:5], sv.t[:, 0:1], sv.t[:, 1:2], ALU.mult, [sv.r], [sv.r])
        tt("dve", sv.t[:, 5:6], sv.t[:, 2:3], sv.t[:, 3:4], ALU.mult, [sv.r], [sv.r])
        ts("dve", sv.t[:, 6:8], sv.t[:, 4:6], -8.0, None, ALU.mult, None, [sv.r], [sv.r])
        nshift_d = sv.t[:, 6:7]
        nshift_m = sv.t[:, 7:8]
        lb = sbuf(gst, [128, 4, 64], F32, "lb")
        dma("sp", lb.t[:, :, :], lams.partition_broadcast(128), [], [lb.r], lb.r)
        lp = sbuf(gst, [128, 2, 64], F32, "lp")
        tt("dve", lp.t[:, 0, :], lb.t[:, 0, :], lb.t[:, 1, :], ALU.mult, [lb.r], [lp.r])
        tt("dve", lp.t[:, 1, :], lb.t[:, 2, :], lb.t[:, 3, :], ALU.mult, [lb.r], [lp.r])
        red("dve", sv.t[:, 8:10], lp.t[:, :, :], ALU.add, [lp.r], [sv.r])
        act(sv.t[:, 10:12], sv.t[:, 8:10], AF.Exp, [sv.r], [sv.r])
        tt("dve", sv.t[:, 12:13], sv.t[:, 11:12], sv.t[:, 10:11], ALU.subtract, [sv.r], [sv.r])
        ts("dve", sv.t[:, 13:14], sv.t[:, 12:13], -LAMBDA_INIT, None, ALU.add, None, [sv.r], [sv.r])
        neglam = sv.t[:, 13:14]
        sg = sbuf(gst, [128, 2], F32, "sg")
        dma("sp", sg.t[:, 0:1], subg.rearrange("(p o) -> p o", o=1), [], [sg.r], sg.r)
        ts("dve", sg.t[:, 1:2], sg.t[:, 0:1], 1.0 - LAMBDA_INIT, None, ALU.mult, None, [sg.r], [sg.r])
        subg_t = sg.t[:, 1:2]
        lv = sbuf(gst, [128, 2, 12], F32, "lv")
        for cc in range(2):
            dma("sp", lv.t[:, cc, 0:4], convw[:, cc * 128:(cc + 1) * 128].rearrange("j p -> p j"), [], [lv.r], lv.r)
            for i, v in enumerate((convb, b_a, b_x, lrul)):
                dma("sp", lv.t[:, cc, 4 + i:5 + i], v[cc * 128:(cc + 1) * 128].rearrange("(p o) -> p o", o=1),
                    [], [lv.r], lv.r)
        act(lv.t[:, :, 8:9], lv.t[:, :, 7:8], AF.Exp, [lv.r], [lv.r], scale=-1.0)
        act(lv.t[:, :, 9:10], lv.t[:, :, 8:9], AF.Ln, [lv.r, cst.r], [lv.r], bias=one_t)
        ts("dve", lv.t[:, :, 10:11], lv.t[:, :, 9:10], -8.0, None, ALU.mult, None, [lv.r], [lv.r])
        wbdf = sbuf(gst, [128, 2, 2, 128], F32, "wbdf")
        memset("pool", wbdf.t[:], 0.0, [wbdf.r])
        for wi, wsrc in enumerate((w_a, w_x)):
            for n in range(4):
                cc, hf = n // 2, n % 2
                dma("sp", wbdf.t[hf * 64:(hf + 1) * 64, wi, cc, hf * 64:(hf + 1) * 64], wsrc[n], [], [wbdf.r], wbdf.r)
        wbd = sbuf(gst, [128, 2, 2, 128], BF16, "wbd")
        cp("dve", wbd.t[:], wbdf.t[:], [wbdf.r], [wbd.r])
        rope_ring = Ring(gst, 4, [128, 16], F32, "rope_t")
        rope_s = sbuf(gst, [128, 16], F32, "rope_s")
        dma("sp", rope_s.t[:], ropes, [], [rope_s.r], rope_s.r)

        qT_smp = sbuf(gst, [128, 4, 128], BF16, "qT_smp")
        kT_smp = sbuf(gst, [128, 4, 128], BF16, "kT_smp")
        qmT_smp = sbuf(gst, [128, 2, 128], BF16, "qmT_smp")
        vb_smp = sbuf(gst, [128, 512], BF16, "vb_smp")
        oT_smp = sbuf(gst, [128, 4, 128], BF16, "oT_smp")
        gate_smp = sbuf(gst, [128, 4, 128], BF16, "gate_smp")
        mbc_smp = sbuf(gst, [128, 4, 128], BF16, "mbc_smp")
        MKT_p = sbuf(gst, [128, 2, 256], BF16, "MKT_p")
        MV_p = sbuf(gst, [128, 2, 256], BF16, "MV_p")
        MKT_s = sbuf(gst, [128, 4, 2, 256], BF16, "MKT_s")
        MV_s = sbuf(gst, [128, 4, 2, 256], BF16, "MV_s")

        r_qk_blk = [Res("qkblk%d" % i) for i in range(NBLK)]
        r_v_blk = [Res("vblk%d" % i) for i in range(NBLK)]
        r_g_blk = [Res("gblk%d" % i) for i in range(NBLK)]
        r_mbc_blk = [Res("mbcblk%d" % i) for i in range(NBLK)]
        r_ot = [[Res("ot%d_%d" % (h, i)) for i in range(NBLK)] for h in range(4)]

        with ExitStack() as sa:
            win = sbuf(sa, [128, 8, 3072], BF16, "win")
            xring = Ring(sa, 4, [128, D], F32, "x")
            xbring = Ring(sa, 2, [128, D], BF16, "xb")
            string = Ring(sa, 4, [128, 4], F32, "st")
            hnT2 = [sbuf(sa, [128, 8, BT], BF16, "hnT") for _ in range(2)]
            r_hn2 = [[Res("hn%d_%d" % (k, i)) for i in range(4)] for k in range(2)]
            qkf_ring = Ring(sa, 4, [128, 1280], F32, "qkf")
            vf_ring = Ring(sa, 4, [128, 512], F32, "vf")
            sq_ring = Ring(sa, 4, [128, 1280], BF16, "sq")
            st20_ring = Ring(sa, 3, [128, 3, 20], F32, "st20")
            qkb_ring = Ring(sa, 4, [128, 1280], BF16, "qkb")
            vb_ring = Ring(sa, 2, [128, 512], BF16, "vb")
            rtmp_ring = Ring(sa, 1, [128, 4, 16, 8], F32, "rtmp")
            qkT_ring = Ring(sa, 2, [128, 8, 128], BF16, "qkT")
            qmT2 = [sbuf(sa, [128, 2, BT], BF16, "qmT") for _ in range(2)]
            ngt = sbuf(sa, [128, 2, 8], F32, "ngt")
            dma("sp", ngt.t[:, 0, :], norm_g.rearrange("(c p) -> p c", p=128), [], [ngt.r], ngt.r)
            dma("sp", ngt.t[:, 1, :], mem_norm_g.rearrange("(c p) -> p c", p=128), [], [ngt.r], ngt.r)

            B_TA, B_TB, B_Q, B_K, B_F0, B_F1, B_MO, B_MR = range(8)

            def norm_a(x_dram):
                xb_ = xring.next()
                dma("sp", xb_.t[:], x_dram, [], [xb_.r], xb_.r)
                return xb_

            def norm_b(xb_):
                st_ = string.next()
                xbf = xbring.next()
                act(xbf.t[:], xb_.t[:], AF.Square, [xb_.r], [st_.r, xbf.r], accum_out=st_.t[:, 0:1])
                act(st_.t[:, 1:2], st_.t[:, 0:1], AF.Ln, [st_.r, cst.r], [st_.r], bias=eps_t, scale=1.0 / D)
                act(st_.t[:, 2:3], st_.t[:, 1:2], AF.Exp, [st_.r], [st_.r], scale=-0.5)
                ts("dve", xbf.t[:], xb_.t[:], st_.t[:, 2:3], None, ALU.mult, None, [xb_.r, st_.r], [xbf.r])
                return xbf

            def norm_c(xbf, hn_dst, rdeps):
                va = bank_bf(B_TA)
                for dc in range(8):
                    tr(va[:, dc, :], xbf.t[:, dc * 128:(dc + 1) * 128], [xbf.r], [rb[B_TA]])
                cp("act", hn_dst, va[:, :, :], [rb[B_TA]], rdeps)

            def gnorm(src, G, gap, rsrc, rg, sq=None):
                s3 = src.rearrange("p (g d) -> p g d", d=64)
                if sq is None:
                    sq = sq_ring.next()
                    tt("pool", sq.t[:, 0:G * 64], src, src, ALU.mult, [rsrc], [sq.r])
                s20 = st20_ring.next()
                red("dve", s20.t[:, 0, 0:G], sq.t[:, 0:G * 64].rearrange("p (g d) -> p g d", d=64), ALU.add,
                    [sq.r], [s20.r])
                act(s20.t[:, 1, 0:G], s20.t[:, 0, 0:G], AF.Ln, [s20.r, cst.r], [s20.r], bias=eps_t, scale=1.0 / 64)
                act(s20.t[:, 2, 0:G], s20.t[:, 1, 0:G], AF.Exp, [s20.r], [s20.r], scale=-0.5)
                tt("dve", s3, s3, s20.t[:, 2, 0:G].unsqueeze(2).to_broadcast([128, G, 64]), ALU.mult,
                   [rsrc, s20.r], [rsrc])
                tt("dve", s3, s3, gap, ALU.mult, [rsrc, rg], [rsrc])

            with ExitStack() as s0:
                wmkv = sbuf(s0, [128, 8, 512], BF16, "wmkv")
                wst = Ring(s0, 2, [128, 1024], F32, "wst")
                for dc in range(8):
                    b = wst.next()
                    dma("sp", b.t[:, 0:512], w_mkv[dc * 128:(dc + 1) * 128, :], [], [b.r], b.r)
                    ts("dve", wmkv.t[:, dc, :], b.t[:, 0:512], ngt.t[:, 1, dc:dc + 1], None, ALU.mult, None,
                       [b.r, ngt.r], [wmkv.r])
                wst2 = Ring(s0, 3, [128, 1024], F32, "wstb")
                win_todo = [(dc, pc) for dc in range(8) for pc in range(3)]

                def stage_win(n):
                    for _ in range(n):
                        if not win_todo:
                            return
                        dc, pc = win_todo.pop(0)
                        b = wst2.next()
                        dma("sp", b.t[:], w_in[dc * 128:(dc + 1) * 128, pc * 1024:(pc + 1) * 1024], [], [b.r], b.r)
                        ts("dve", win.t[:, dc, pc * 1024:(pc + 1) * 1024], b.t[:], ngt.t[:, 0, dc:dc + 1], None,
                           ALU.mult, None, [b.r, ngt.r], [win.r])

                hnT = hnT2[0]
                for mt in range(2):
                    xb_ = norm_a(mem[mt * 128:(mt + 1) * 128, :])
                    xbf = norm_b(xb_)
                    norm_c(xbf, hnT.t[:, :, 0:128], [r_hn2[0][0]])
                    for dc in range(8):
                        mm(ps[:, B_Q, :], hnT.t[:, dc, 0:128], wmkv.t[:, dc, :], dc == 0, dc == 7,
                           [r_hn2[0][0], wmkv.r], [rb[B_Q]])
                    kvf = vf_ring.next()
                    cp("act", kvf.t[:], ps[:, B_Q, :], [rb[B_Q]], [kvf.r])
                    dma("act", mvp[mt * 128:(mt + 1) * 128, :], kvf.t[:, 256:512], [kvf.r], [], kvf.r)
                    cp("dve", MV_p.t[:, mt, :], kvf.t[:, 256:512], [kvf.r], [MV_p.r])
                    gnorm(kvf.t[:, 0:256], 4, gmk.t[:, :, :], kvf.r, gmk.r)
                    dma("act", mkp[mt * 128:(mt + 1) * 128, :], kvf.t[:, 0:256], [kvf.r], [], kvf.r)
                    kb = qkb_ring.next()
                    cp("dve", kb.t[:, 0:256], kvf.t[:, 0:256], [kvf.r], [kb.r])
                    vb_ = bank_bf(B_TB)
                    for hp_ in range(2):
                        tr(vb_[:, hp_, :], kb.t[:, hp_ * 128:(hp_ + 1) * 128], [kb.r], [rb[B_TB]])
                    cp("act", MKT_p.t[:, :, mt * 128:(mt + 1) * 128], vb_[:, 0:2, :], [rb[B_TB]], [MKT_p.r])
                    stage_win(3)
                for b in range(4):
                    for mt in range(2):
                        cf = vf_ring.next()
                        dma("sp", cf.t[:, 0:256], cmk[b, mt * 128:(mt + 1) * 128, :], [], [cf.r], cf.r)
                        dma("sp", cf.t[:, 256:512], cmv[b, mt * 128:(mt + 1) * 128, :], [], [cf.r], cf.r)
                        kb = qkb_ring.next()
                        cp("dve", kb.t[:, 0:256], cf.t[:, 0:256], [cf.r], [kb.r])
                        cp("pool", MV_s.t[:, b, mt, :], cf.t[:, 256:512], [cf.r], [MV_s.r])
                        vb_ = bank_bf(B_TB)
                        for hp_ in range(2):
                            tr(vb_[:, hp_, :], kb.t[:, hp_ * 128:(hp_ + 1) * 128], [kb.r], [rb[B_TB]])
                        cp("act", MKT_s.t[:, b, :, mt * 128:(mt + 1) * 128], vb_[:, 0:2, :], [rb[B_TB]], [MKT_s.r])
                        stage_win(2)
                stage_win(24)
                S.barrier()

            def tile_gen(x_dram, k2, col0, rope_ap, rrope, kout, vout, qmT_dst, r_qm, prompt_tg):
                hn = hnT2[k2]
                rh = r_hn2[k2][col0 // 128]
                xb_ = norm_a(x_dram)
                yield
                xbf = norm_b(xb_)
                yield
                norm_c(xbf, hn.t[:, :, col0:col0 + 128], [rh])
                yield
                qkf = qkf_ring.next()
                vf = vf_ring.next()
                for (bk, c0) in ((B_Q, 0), (B_K, 512)):
                    for dc in range(8):
                        mm(ps[:, bk, :], hn.t[:, dc, col0:col0 + 128], win.t[:, dc, c0:c0 + 512], dc == 0, dc == 7,
                           [rh, win.r], [rb[bk]])
                cp("act", qkf.t[:, 0:1024].rearrange("p (a c) -> p a c", c=512), ps[:, B_Q:B_K + 1, :],
                   [rb[B_Q], rb[B_K]], [qkf.r])
                sq = sq_ring.next()
                act(sq.t[:, 0:1024].rearrange("p (a c) -> p a c", c=512), ps[:, B_Q:B_K + 1, :], AF.Square,
                    [rb[B_Q], rb[B_K]], [sq.r])
                yield
                for (bk, c0, n) in ((B_Q, 1024, 512), (B_K, 2560, 256)):
                    for dc in range(8):
                        mm(ps[:, bk, 0:n], hn.t[:, dc, col0:col0 + 128], win.t[:, dc, c0:c0 + n], dc == 0, dc == 7,
                           [rh, win.r], [rb[bk]])
                cp("act", vf.t[:], ps[:, B_Q, :], [rb[B_Q]], [vf.r])
                cp("act", qkf.t[:, 1024:1280], ps[:, B_K, 0:256], [rb[B_K]], [qkf.r])
                act(sq.t[:, 1024:1280], ps[:, B_K, 0:256], AF.Square, [rb[B_K]], [sq.r])
                dma("act", vout, vf.t[:], [vf.r], [], vf.r)
                if prompt_tg is None:
                    cp("dve", vb_smp.t[:], vf.t[:], [vf.r], [vb_smp.r])
                else:
                    vbb = vb_ring.next()
                    cp("dve", vbb.t[:], vf.t[:], [vf.r], [vbb.r])
                    dma("act", V_s[prompt_tg * 128:(prompt_tg + 1) * 128, :], vbb.t[:], [vbb.r],
                        [r_v_blk[prompt_tg // 4]], vbb.r)
                yield
                gnorm(qkf.t[:, 0:1280], 20, gfull.t[:, :, :], qkf.r, gfull.r, sq)
                yield
                q3 = qkf.t[:, 0:1024].rearrange("p (g d) -> p g d", d=64)
                x1 = q3[:, :, 0:8]
                x2 = q3[:, :, 8:16]
                cosb = rope_ap[:, 0:8].unsqueeze(1).to_broadcast([128, 16, 8])
                sinb = rope_ap[:, 8:16].unsqueeze(1).to_broadcast([128, 16, 8])
                rtmp = rtmp_ring.next()
                tt("pool", rtmp.t[:, 0], x1, cosb, ALU.mult, [qkf.r, rrope], [rtmp.r])
                tt("pool", rtmp.t[:, 1], x2, sinb, ALU.mult, [qkf.r, rrope], [rtmp.r])
                tt("pool", rtmp.t[:, 2], x2, cosb, ALU.mult, [qkf.r, rrope], [rtmp.r])
                tt("pool", rtmp.t[:, 3], x1, sinb, ALU.mult, [qkf.r, rrope], [rtmp.r])
                tt("pool", x1, rtmp.t[:, 0], rtmp.t[:, 1], ALU.subtract, [rtmp.r], [qkf.r])
                tt("pool", x2, rtmp.t[:, 2], rtmp.t[:, 3], ALU.add, [rtmp.r], [qkf.r])
                dma("sp", kout, qkf.t[:, 512:1024], [qkf.r], [], qkf.r)
                yield
                qkb = qkb_ring.next()
                cp("dve", qkb.t[:], qkf.t[:], [qkf.r], [qkb.r])
                yield
                vB = bank_bf(B_TB)
                for g in range(8):
                    tr(vB[:, g, :], qkb.t[:, g * 128:(g + 1) * 128], [qkb.r], [rb[B_TB]])
                if prompt_tg is None:
                    cp("act", qT_smp.t[:], vB[:, 0:4, :], [rb[B_TB]], [qT_smp.r])
                    cp("act", kT_smp.t[:], vB[:, 4:8, :], [rb[B_TB]], [kT_smp.r])
                else:
                    qkT = qkT_ring.next()
                    cp("act", qkT.t[:], vB[:, :, :], [rb[B_TB]], [qkT.r])
                    cols = slice(prompt_tg * 128, (prompt_tg + 1) * 128)
                    dma("act", QT_s[:, :, cols].rearrange("h p t -> p h t"), qkT.t[:, 0:4, :], [qkT.r],
                        [r_qk_blk[prompt_tg // 4]], qkT.r)
                    dma("act", KT_s[:, :, cols].rearrange("h p t -> p h t"), qkT.t[:, 4:8, :], [qkT.r],
                        [r_qk_blk[prompt_tg // 4]], qkT.r)
                vA = bank_bf(B_TA)
                for g in range(2):
                    tr(vA[:, g, :], qkb.t[:, 1024 + g * 128:1024 + (g + 1) * 128], [qkb.r], [rb[B_TA]])
                cp("act", qmT_dst, vA[:, 0:2, :], [rb[B_TA]], [r_qm])

            lx_p = sbuf(sa, [128, 2, 1, 3 + BT], F32, "lx_p")
            lx_s = sbuf(sa, [128, 2, 4, 3 + 32], F32, "lx_s")
            r_lx_p = [Res("lxp0"), Res("lxp1")]
            r_lx_s = [Res("lxs0"), Res("lxs1")]
            hprev = sbuf(sa, [128, 2], F32, "hprev")
            r_hprev = [Res("hp0"), Res("hp1")]
            h0s = sbuf(sa, [128, 2, 4], F32, "h0s")
            memset("pool", lx_p.t[:, :, :, 0:3], 0.0, r_lx_p)
            memset("pool", hprev.t[:], 0.0, r_hprev)
            for cc in range(2):
                for sb_i in range(4):
                    dma("sp", lx_s.t[:, cc, sb_i, 0:3], sconv[sb_i, :, cc * 128:(cc + 1) * 128].rearrange("j p -> p j"),
                        [], [r_lx_s[cc]], r_lx_s[cc])
                dma("sp", h0s.t[:, cc, :], sh[:, cc * 128:(cc + 1) * 128].rearrange("b p -> p b"), [], [h0s.r], h0s.r)
            fring = Ring(sa, 1, [128, 5, BT], F32, "fw")
            hl_s = sbuf(sa, [128, 2, 4], F32, "hl_s")
            xcb_ring = Ring(sa, 1, [128, BT], BF16, "xcb")
            slg_ring = Ring(sa, 1, [128, 2, BT], BF16, "slg")
            smg_ring = Ring(sa, 1, [128, 2, BT], BF16, "smg")
            gate_ring = Ring(sa, 1, [128, 4, BT], BF16, "gate")
            mbc_ring = Ring(sa, 1, [128, 4, BT], BF16, "mbc")
            e_ring = Ring(sa, 2, [128, BT], BF16, "eT")
            mo_ring = Ring(sa, 1, [128, 1, BT], F32, "mo")
            fbank = [0]

            def next_fbank():
                fbank[0] += 1
                return B_F0 if fbank[0] % 2 else B_F1

            def feat_gen(part, NT, k2, nseg, L, lxb, r_lx, hinit_fn, gate_b, mbc_b, r_mbcB, qmT_ap, rqm, memgroups,
                         last, outs):
                hn = hnT2[k2]
                rhs_ = r_hn2[k2][0:max(1, NT // 128)]

                def proj(col, bk=None):
                    if bk is None:
                        bk = next_fbank()
                    for dc in range(8):
                        mm(ps[:, bk, 0:NT], win.t[:, dc, col:col + 128], hn.t[:, dc, 0:NT], dc == 0, dc == 7,
                           [win.r] + rhs_, [rb[bk]])
                    return bk

                if part == "A":
                    smg = smg_ring.next()
                    for i in range(4):
                        bk = proj(1536 + i * 128)
                        act(gate_b.t[:, i, 0:NT], ps[:, bk, 0:NT], AF.Silu, [rb[bk]], [gate_b.r])
                    for hp_ in range(2):
                        bk = proj(2816 + hp_ * 128)
                        act(smg.t[:, hp_, 0:NT], ps[:, bk, 0:NT], AF.Silu, [rb[bk]], [smg.r])
                    yield
                    for _ in mem_part(NT, mbc_b, smg, qmT_ap, rqm, memgroups):
                        yield
                    return
                slg = slg_ring.next()
                for cc in range(2):
                    proj(2048 + cc * 128, B_TA)
                    cp("act", lxb.t[:, cc, :, 3:3 + L], ps[:, B_TA, 0:NT].rearrange("p (s l) -> p s l", l=L),
                       [rb[B_TA]], [r_lx[cc]])
                    yield
                for cc in range(2):
                    proj(2304 + cc * 128, B_TB)
                    act(slg.t[:, cc, 0:NT], ps[:, B_TB, 0:NT], AF.Silu, [rb[B_TB]], [slg.r])
                    yield
                for cc in range(2):
                    fw = fring.next()
                    xc = fw.t[:, 0, 0:NT]
                    xc3 = xc.rearrange("p (s l) -> p s l", l=L)
                    ts("pool", xc3, lxb.t[:, cc, :, 0:L], lv.t[:, cc, 0:1], lv.t[:, cc, 4:5], ALU.mult, ALU.add,
                       [r_lx[cc], lv.r], [fw.r])
                    for j in range(1, 4):
                        stt("dve", xc3, lxb.t[:, cc, :, j:j + L], lv.t[:, cc, j:j + 1], xc3, ALU.mult, ALU.add,
                            [r_lx[cc], lv.r, fw.r], [fw.r])
                    xcb = xcb_ring.next()
                    cp("dve", xcb.t[:, 0:NT], xc, [fw.r], [xcb.r])
                    yield
                    bka, bkx = B_TA, B_TB
                    mm(ps[:, bka, 0:NT], wbd.t[:, 0, cc, :], xcb.t[:, 0:NT], True, True, [wbd.r, xcb.r], [rb[bka]])
                    mm(ps[:, bkx, 0:NT], wbd.t[:, 1, cc, :], xcb.t[:, 0:NT], True, True, [wbd.r, xcb.r], [rb[bkx]])
                    act(fw.t[:, 1, 0:NT], ps[:, bka, 0:NT], AF.Sigmoid, [rb[bka], lv.r], [fw.r], bias=lv.t[:, cc, 5:6])
                    act(fw.t[:, 2, 0:NT], ps[:, bkx, 0:NT], AF.Sigmoid, [rb[bkx], lv.r], [fw.r], bias=lv.t[:, cc, 6:7])
                    yield
                    a_ = fw.t[:, 3, 0:NT]
                    tmp = fw.t[:, 4, 0:NT]
                    h_ = fw.t[:, 1, 0:NT]
                    act(a_, fw.t[:, 1, 0:NT], AF.Exp, [fw.r, lv.r], [fw.r], scale=lv.t[:, cc, 10:11])
                    tt("pool", tmp, a_, a_, ALU.mult, [fw.r], [fw.r])
                    act(tmp, tmp, AF.Ln, [fw.r, cst.r], [fw.r], bias=one_t, scale=-1.0)
                    act(tmp, tmp, AF.Exp, [fw.r], [fw.r], scale=0.5)
                    tt("dve", fw.t[:, 2, 0:NT], fw.t[:, 2, 0:NT], fw.t[:, 0, 0:NT], ALU.mult, [fw.r], [fw.r])
                    tt("dve", tmp, tmp, fw.t[:, 2, 0:NT], ALU.mult, [fw.r], [fw.r])
                    yield
                    for s_ in range(nseg):
                        scan(h_[:, s_ * L:(s_ + 1) * L], a_[:, s_ * L:(s_ + 1) * L], tmp[:, s_ * L:(s_ + 1) * L],
                             hinit_fn(cc, s_), [fw.r, r_hprev[cc], h0s.r], [fw.r])
                    tt("dve", mbc_b.t[:, cc, 0:NT], h_, slg.t[:, cc, 0:NT], ALU.mult, [fw.r, slg.r], [r_mbcB])
                    if nseg == 1:
                        cp("pool", hprev.t[:, cc:cc + 1], h_[:, L - 1:L], [fw.r], [r_hprev[cc]])
                        if last:
                            dma("sp", outs["conv"][:, cc * 128:(cc + 1) * 128].rearrange("j p -> p j"),
                                lxb.t[:, cc, 0, L:L + 3], [r_lx[cc]], [], r_lx[cc])
                            dma("sp", outs["h"][cc * 128:(cc + 1) * 128].rearrange("(p o) -> p o", o=1),
                                hprev.t[:, cc:cc + 1], [r_hprev[cc]], [], r_hprev[cc])
                        else:
                            cp("pool", lxb.t[:, cc, 0, 0:3], lxb.t[:, cc, 0, L:L + 3], [r_lx[cc]], [r_lx[cc]])
                    else:
                        for sb_i in range(nseg):
                            dma("sp", outs["conv"][sb_i, :, cc * 128:(cc + 1) * 128].rearrange("j p -> p j"),
                                lxb.t[:, cc, sb_i, L:L + 3], [r_lx[cc]], [], r_lx[cc])
                        hlast = hl_s.t[:, cc, 0:nseg]
                        cp("pool", hlast, h_.rearrange("p (s l) -> p s l", l=L)[:, :, L - 1], [fw.r], [hl_s.r])
                        dma("sp", outs["h"][:, cc * 128:(cc + 1) * 128].rearrange("b p -> p b"), hlast, [hl_s.r], [],
                            hl_s.r)
                    yield
            def mem_part(NT, mbc_b, smg, qmT_ap, rqm, memgroups):
                for hp_ in range(2):
                    mo = mo_ring.next()
                    its = []
                    for (c0, n, mkt_fn, mv_fn, rmk, rmv) in memgroups:
                        subs = [(c0 + k_ * 256, 256) for k_ in range(n // 256)] if n > 256 else [(c0, n)]
                        for (cc0, nn) in subs:
                            for hh in range(2):
                                its.append((cc0, nn, hh, mkt_fn, mv_fn, rmk, rmv))
                    ebuf = {}

                    def mS(k_, hp_=hp_, its=its, ebuf=ebuf):
                        cc0, nn, hh, mkt_fn, mv_fn, rmk, rmv = its[k_]
                        bk = next_fbank()
                        for mt in range(2):
                            mm(ps[:, bk, mt * nn:(mt + 1) * nn], mkt_fn(hp_, mt)[hh * 64:(hh + 1) * 64, :],
                               qmT_ap[hh * 64:(hh + 1) * 64, hp_, cc0:cc0 + nn], True, True, [rmk, rqm], [rb[bk]])
                        eb = e_ring.next()
                        act(eb.t[:, 0:2 * nn], ps[:, bk, 0:2 * nn], AF.Exp, [rb[bk], sv.r], [eb.r], bias=nshift_m,
                            scale=0.125)
                        ebuf[k_] = eb

                    def mPV(k_, hp_=hp_, its=its, ebuf=ebuf):
                        cc0, nn, hh, mkt_fn, mv_fn, rmk, rmv = its[k_]
                        h = 2 * hp_ + hh
                        eb = ebuf.pop(k_)
                        for mt in range(2):
                            mm(ps[hh * 64:(hh + 1) * 64, B_MO, cc0:cc0 + nn], mv_fn(mt)[:, h * 64:(h + 1) * 64],
                               eb.t[:, mt * nn:(mt + 1) * nn], mt == 0, mt == 1, [rmv, eb.r], [rb[B_MO]])
                        for mt in range(2):
                            mm(ps[hh * 64:(hh + 1) * 64, B_MR, cc0:cc0 + nn], onesb.t[:, 0:64],
                               eb.t[:, mt * nn:(mt + 1) * nn], mt == 0, mt == 1, [onesb.r, eb.r], [rb[B_MR]])

                    mS(0)
                    for k_ in range(len(its)):
                        if k_ + 1 < len(its):
                            mS(k_ + 1)
                        mPV(k_)
                        yield
                    recip(mo.t[:, 0, 0:NT], ps[:, B_MR, 0:NT], [rb[B_MR]], [mo.r])
                    tt("dve", mo.t[:, 0, 0:NT], ps[:, B_MO, 0:NT], mo.t[:, 0, 0:NT], ALU.mult, [rb[B_MO], mo.r], [mo.r])
                    tt("dve", mbc_b.t[:, 2 + hp_, 0:NT], mo.t[:, 0, 0:NT], smg.t[:, hp_, 0:NT], ALU.mult,
                       [mo.r, smg.r], [mbc_b.r])
                    yield

            tiles = []
            for bi in range(NBLK):
                for ti in range(4):
                    tiles.append((bi, ti))
            tiles.append((NBLK, 0))
            tile_i = [0]
            active = []
            tiles_left = {bi: 4 for bi in range(NBLK)}
            tiles_left[NBLK] = 1
            feat_q = []
            feat_done = set()
            cur_feat = [None]
            FEAT_STEPS = 1

            def make_tile(bi, ti):
                k2 = bi % 2
                if bi < NBLK:
                    tg = bi * 4 + ti
                    rp_ = rope_ring.next()
                    dma("sp", rp_.t[:], ropep[:, tg * 16:(tg + 1) * 16], [], [rp_.r], rp_.r)
                    return tile_gen(xp[tg * 128:(tg + 1) * 128, :], k2, ti * 128, rp_.t[:, :], rp_.r,
                                    kp[tg * 128:(tg + 1) * 128, :], vp[tg * 128:(tg + 1) * 128, :],
                                    qmT2[k2].t[:, :, ti * 128:(ti + 1) * 128], qmT2[k2].r, tg)
                return tile_gen(xs, k2, 0, rope_s.t[:, :], rope_s.r, kso, vso, qmT2[k2].t[:, :, 0:128], qmT2[k2].r, None)

            def make_feat(bi):
                k2 = bi % 2
                r_mbcB = Res("mbcB%d" % bi)
                if bi < NBLK:
                    gate_b = gate_ring.next(); mbc_b = mbc_ring.next()
                    cols = slice(bi * BT, (bi + 1) * BT)

                    def post(gate_b=gate_b, mbc_b=mbc_b, cols=cols, bi=bi, r_mbcB=r_mbcB):
                        dma("act", G_s[:, :, cols].rearrange("h p t -> p h t"), gate_b.t[:], [gate_b.r], [r_g_blk[bi]],
                            gate_b.r)
                        dma("act", MBC_s[:, :, cols].rearrange("h p t -> p h t"), mbc_b.t[:], [mbc_b.r, r_mbcB],
                            [r_mbc_blk[bi]], mbc_b.r)

                    args = (BT, k2, 1, BT, lx_p, r_lx_p, lambda cc, s_: hprev.t[:, cc:cc + 1], gate_b, mbc_b, r_mbcB,
                            qmT2[k2].t, qmT2[k2].r,
                            [(0, BT, lambda hp_, mt: MKT_p.t[:, hp_, mt * 128:(mt + 1) * 128],
                              lambda mt: MV_p.t[:, mt, :], MKT_p.r, MV_p.r)],
                            bi == NBLK - 1, {"conv": cpo, "h": hpo})
                    return [bi, feat_gen("A", *args), feat_gen("B", *args), post, mbc_b, r_mbcB]

                def mk_grp(b):
                    return (32 * b, 32, lambda hp_, mt: MKT_s.t[:, b, hp_, mt * 128:(mt + 1) * 128],
                            lambda mt: MV_s.t[:, b, mt, :], MKT_s.r, MV_s.r)

                args = (128, k2, 4, 32, lx_s, r_lx_s, lambda cc, s_: h0s.t[:, cc, s_:s_ + 1], gate_smp, mbc_smp, r_mbcB,
                        qmT2[k2].t, qmT2[k2].r, [mk_grp(b) for b in range(4)], True, {"conv": cso, "h": hso})
                return [bi, feat_gen("A", *args), feat_gen("B", *args), None, mbc_smp, r_mbcB]

            while tile_i[0] < len(tiles) or active or feat_q or cur_feat[0] is not None:
                while len(active) < 4 and tile_i[0] < len(tiles):
                    bi, ti = tiles[tile_i[0]]
                    if bi >= 2 and (bi - 2) not in feat_done:
                        break
                    if active and min(a_[2][0] for a_ in active) < 2:
                        break
                    active.append((bi, make_tile(bi, ti), [0]))
                    tile_i[0] += 1
                for item in list(active):
                    bi, g, cnt_ = item
                    cnt_[0] += 1
                    try:
                        next(g)
                    except StopIteration:
                        active.remove(item)
                        tiles_left[bi] -= 1
                        if tiles_left[bi] == 0:
                            feat_q.append(bi)
                if cur_feat[0] is None and feat_q:
                    cur_feat[0] = make_feat(feat_q.pop(0))
                if cur_feat[0] is not None:
                    cf = cur_feat[0]
                    for gi in (1, 2):
                        if cf[gi] is not None:
                            try:
                                next(cf[gi])
                            except StopIteration:
                                cf[gi] = None
                    if cf[1] is None and cf[2] is None:
                        if cf[3] is not None:
                            cf[3]()
                        feat_done.add(cf[0])
                        cur_feat[0] = None
            S.barrier()

        with ExitStack() as sb_:
            KT_ring = Ring(sb_, 2, [128, T], BF16, "KTh")
            V_ring = Ring(sb_, 2, [128, T // 128, 128], BF16, "Vh")
            QT_ring = Ring(sb_, 4, [128, BT], BF16, "QTb")
            E_ring = Ring(sb_, 3, [128, 2, 32], BF16, "E")
            ep_ring = Ring(sb_, 2, [128, 4, 32], F32, "ep")
            ot_ring = Ring(sb_, 2, [128, 32], BF16, "ot")
            SB = [(0, 1), (2, 3)]
            B_O = (4, 5)
            B_R = (6, 7)

            acc = sbuf(sb_, [128, 2, 32], F32, "acc")
            r_acc = [Res("acc0"), Res("acc1")]
            hl_ring = Ring(sb_, 2, [128, 2, 2, 32], BF16, "hl")
            r_hl = {}

            def attn_unit(steps, kt_fn, v_fn, qt_fn, rk, rv, rq, NQ, out_ap, rout):
                n = len(steps)
                ebs = {}
                ceng = ("dve", "pool")

                def issue_S(j):
                    ksz, q0, _, _ = steps[j]
                    pair = SB[j % 2]
                    for c in range(2):
                        mm(ps[0:ksz, pair[c], q0:NQ], kt_fn(j, c), qt_fn(c)[:, q0:NQ], True, True,
                           [rk, rq], [rb[pair[c]]])
                    eb = E_ring.next()
                    act(eb.t[0:ksz, :, q0:NQ], ps[0:ksz, pair[0]:pair[1] + 1, q0:NQ], AF.Exp,
                        [rb[pair[0]], rb[pair[1]], sv.r], [eb.r], bias=nshift_d[0:ksz, :], scale=0.125)
                    ebs[j] = eb

                def issue_PV(j):
                    ksz, q0, mq0, pm = steps[j]
                    eb = ebs.pop(j)
                    if mq0 is not None:
                        memset("pool", eb.t[64:128, :, mq0:mq0 + 64], 0.0, [eb.r])
                    for (p0, p1) in pm:
                        memset("pool", eb.t[p0:p1, :, q0:NQ], 0.0, [eb.r])
                    for c in range(2):
                        mm(ps[:, B_O[c], q0:NQ], v_fn(j), eb.t[0:ksz, c, q0:NQ], j == 0, j == n - 1,
                           [rv, eb.r], [rb[B_O[c]]])
                    if j == 0:
                        cp("dve", acc.t[:, 0, 0:NQ], eb.t[:, 0, 0:NQ], [eb.r], [r_acc[0]])
                    else:
                        tt("dve", acc.t[:, 0, q0:NQ], acc.t[:, 0, q0:NQ], eb.t[:, 0, q0:NQ], ALU.add,
                           [eb.r, r_acc[0]], [r_acc[0]])
                    mm(ps[:, B_R[1], q0:NQ], onesb.t[0:ksz, :], eb.t[0:ksz, 1, q0:NQ], j == 0, j == n - 1,
                       [onesb.r, eb.r], [rb[B_R[1]]])

                issue_S(0)
                for j in range(n):
                    if j + 1 < n:
                        issue_S(j + 1)
                    issue_PV(j)
                hl = hl_ring.next()
                if id(hl) not in r_hl:
                    r_hl[id(hl)] = [Res("hl0"), Res("hl1")]
                rh = r_hl[id(hl)]
                for c in range(1):
                    cp(ceng[c], hl.t[:, c, 0, 0:NQ], acc.t[:, c, 0:NQ], [r_acc[c]], [rh[c]])
                    tt(ceng[c], hl.t[:, c, 1, 0:NQ], acc.t[:, c, 0:NQ], hl.t[:, c, 0, 0:NQ], ALU.subtract,
                       [r_acc[c], rh[c]], [rh[c]])
                    mm(ps[:, B_R[c], 0:NQ], onesb.t[:], hl.t[:, c, 0, 0:NQ], True, False, [onesb.r, rh[c]], [rb[B_R[c]]])
                    mm(ps[:, B_R[c], 0:NQ], onesb.t[:], hl.t[:, c, 1, 0:NQ], False, True, [onesb.r, rh[c]], [rb[B_R[c]]])
                ep = ep_ring.next()
                cp("dve", ep.t[:, 0, 0:NQ], ps[:, B_O[0], 0:NQ], [rb[B_O[0]]], [ep.r])
                cp("dve", ep.t[:, 1, 0:NQ], ps[:, B_O[1], 0:NQ], [rb[B_O[1]]], [ep.r])
                cp("dve", ep.t[:, 2, 0:NQ], ps[:, B_R[1], 0:NQ], [rb[B_R[1]]], [ep.r])
                recip(ep.t[:, 3, 0:NQ], ps[:, B_R[0], 0:NQ], [rb[B_R[0]]], [ep.r])
                recip(ep.t[:, 2, 0:NQ], ep.t[:, 2, 0:NQ], [ep.r], [ep.r])
                tt("dve", ep.t[:, 0, 0:NQ], ep.t[:, 0, 0:NQ], ep.t[:, 3, 0:NQ], ALU.mult, [ep.r], [ep.r])
                tt("dve", ep.t[:, 1, 0:NQ], ep.t[:, 1, 0:NQ], ep.t[:, 2, 0:NQ], ALU.mult, [ep.r], [ep.r])
                stt("dve", out_ap, ep.t[:, 1, 0:NQ], neglam, ep.t[:, 0, 0:NQ], ALU.mult, ALU.add, [ep.r, sv.r], [rout])

            ckst = Ring(sb_, 3, [128, 512], F32, "ckst")
            ckb_ring = Ring(sb_, 2, [128, 512], BF16, "ckb")
            KTs_ring = Ring(sb_, 2, [128, 4, PAST], BF16, "KTs")
            Vs_ring = Ring(sb_, 2, [128, PAST // 128, 512], BF16, "Vs")
            for b in range(4):
                KTs = KTs_ring.next()
                Vs = Vs_ring.next()
                for kt in range(PAST // 128):
                    cf = ckst.next()
                    dma("sp", cf.t[:], ck[b, kt * 128:(kt + 1) * 128, :], [], [cf.r], cf.r)
                    cb = ckb_ring.next()
                    cp("dve", cb.t[:], cf.t[:], [cf.r], [cb.r])
                    bk = SB[kt % 2][0]
                    vw = bank_bf(bk)
                    for h in range(4):
                        tr(vw[:, h, :], cb.t[:, h * 128:(h + 1) * 128], [cb.r], [rb[bk]])
                    cp("dve", KTs.t[:, :, kt * 128:(kt + 1) * 128], vw[:, 0:4, :], [rb[bk]], [KTs.r])
                    cf2 = ckst.next()
                    dma("sp", cf2.t[:], cv[b, kt * 128:(kt + 1) * 128, :], [], [cf2.r], cf2.r)
                    cp("pool", Vs.t[:, kt, :], cf2.t[:], [cf2.r], [Vs.r])
                for h in range(4):
                    steps = [(128, 0, None, []) for _ in range(PAST // 128)]
                    pm = [(32 * qd, 32 * qd + 32) for qd in range(4) if qd != b]
                    steps.append((128, 0, None, pm))

                    def kt_fn(j, c, KTs=KTs, h=h):
                        if j < PAST // 128:
                            return KTs.t[c * 64:(c + 1) * 64, h, j * 128:(j + 1) * 128]
                        return kT_smp.t[c * 64:(c + 1) * 64, h, :]

                    def v_fn(j, Vs=Vs, h=h):
                        if j < PAST // 128:
                            return Vs.t[:, j, h * 128:(h + 1) * 128]
                        return vb_smp.t[:, h * 128:(h + 1) * 128]

                    def qt_fn(c, h=h, b=b):
                        return qT_smp.t[c * 64:(c + 1) * 64, h, 32 * b:32 * b + 32]

                    attn_unit(steps, kt_fn, v_fn, qt_fn, KTs.r, Vs.r, qT_smp.r, 32,
                              oT_smp.t[:, h, 32 * b:32 * b + 32], oT_smp.r)

            QB = 256
            NQB = T // QB
            SP3 = [(0, 1), (2, 3), (4, 5)]
            OB = 6
            RB = 7
            E4_ring = Ring(sb_, 12, [128, 2, 2, QB], BF16, "E4")
            acc2 = sbuf(sb_, [128, 2, QB], F32, "acc2")
            hl2_ring = Ring(sb_, 2, [128, 2, 2, QB], BF16, "hl2")
            ep2_ring = Ring(sb_, 2, [128, 2, 2 * QB], F32, "ep2")
            ot2_ring = Ring(sb_, 3, [128, QB], BF16, "ot2")
            heads = {}

            def load_head(h):
                KTh = KT_ring.next(); Vh = V_ring.next()
                dma("sp", KTh.t[:], KT_s[h], r_qk_blk, [KTh.r], KTh.r)
                for part in range(4):
                    dma("sp", Vh.t[:, part * 16:(part + 1) * 16, :],
                        V_s[part * 2048:(part + 1) * 2048, h * 128:(h + 1) * 128].rearrange("(t p) e -> p t e", p=128),
                        r_v_blk, [Vh.r], Vh.r)
                heads[h] = (KTh, Vh)

            gsteps = []
            for h in range(4):
                for i in range(NQB):
                    for p in range(i + 1):
                        gsteps.append((h, i, p))
            qblk = {}

            def ensure_q(h, i):
                key = (h, i // 2)
                if key not in qblk:
                    qb = QT_ring.next()
                    dma("sp", qb.t[:], QT_s[h, :, (i // 2) * BT:(i // 2 + 1) * BT], [r_qk_blk[i // 2]], [qb.r], qb.r)
                    qblk[key] = qb
                return qblk[key]

            ebs = {}

            def issue_S2(g):
                h, i, p = gsteps[g]
                if p == 0:
                    ensure_q(h, i)
                    if i + 2 < NQB:
                        ensure_q(h, i + 2)
                KTh, Vh = heads[h]
                qb = ensure_q(h, i)
                pair = SP3[g % 3]
                qo = (i % 2) * QB
                for t in range(2):
                    kt = 2 * p + t
                    for c in range(2):
                        mm(ps[:, pair[c], t * QB:(t + 1) * QB], KTh.t[c * 64:(c + 1) * 64, kt * 128:(kt + 1) * 128],
                           qb.t[c * 64:(c + 1) * 64, qo:qo + QB], True, True, [KTh.r, qb.r], [rb[pair[c]]])
                eb = E4_ring.next()
                act(eb.t[:].rearrange("p c t q -> p c (t q)"), ps[:, pair[0]:pair[1] + 1, :], AF.Exp,
                    [rb[pair[0]], rb[pair[1]], sv.r], [eb.r], bias=nshift_d, scale=0.125)
                ebs[g] = eb

            def issue_PV2(g):
                h, i, p = gsteps[g]
                KTh, Vh = heads[h]
                eb = ebs.pop(g)
                if p == i:
                    memset("pool", eb.t[64:128, :, 0, 0:64], 0.0, [eb.r])
                    memset("pool", eb.t[:, :, 1, 0:128], 0.0, [eb.r])
                    memset("pool", eb.t[64:128, :, 1, 128:192], 0.0, [eb.r])
                for t in range(2):
                    kt = 2 * p + t
                    for c in range(2):
                        mm(ps[:, OB, c * QB:(c + 1) * QB], Vh.t[:, kt, :], eb.t[:, c, t, :],
                           p == 0 and t == 0 and c == 0, p == i and t == 1, [Vh.r, eb.r], [rb[OB]], skip=True)
                for t in range(2):
                    mm(ps[:, RB, 0:QB], onesb.t[:], eb.t[:, 1, t, :], p == 0 and t == 0, False, [onesb.r, eb.r], [rb[RB]],
                       skip=True)
                if p == 0:
                    cp("dve", acc2.t[:, :, :], eb.t[:, 0, :, :], [eb.r], [acc2.r])
                else:
                    tt("dve", acc2.t[:, :, :], acc2.t[:, :, :], eb.t[:, 0, :, :], ALU.add, [eb.r, acc2.r], [acc2.r])
                if p == i:
                    flush_pending(NG)
                    ep = ep2_ring.next()
                    cp("dve", ep.t[:, 0, :], ps[:, OB, :], [rb[OB]], [ep.r])
                    cp("dve", ep.t[:, 1, 0:QB], ps[:, RB, 0:QB], [rb[RB]], [ep.r])
                    recip(ep.t[:, 1, 0:QB], ep.t[:, 1, 0:QB], [ep.r], [ep.r])
                    hl = hl2_ring.next()
                    cp("dve", hl.t[:, 0, :, :], acc2.t[:, :, :], [acc2.r], [hl.r])
                    tt("dve", hl.t[:, 1, :, :], acc2.t[:, :, :], hl.t[:, 0, :, :], ALU.subtract, [acc2.r, hl.r], [hl.r])

                    def rest(ep=ep, hl=hl, h=h, i=i):
                        for k_ in range(2):
                            for t in range(2):
                                mm(ps[:, RB, QB:2 * QB], onesb.t[:], hl.t[:, k_, t, :], False, k_ == 1 and t == 1,
                                   [onesb.r, hl.r], [rb[RB]], skip=True)
                        cp("dve", ep.t[:, 1, QB:2 * QB], ps[:, RB, QB:2 * QB], [rb[RB]], [ep.r])
                        memset("dve", ps[:, RB, QB:2 * QB], 0.0, [rb[RB]])
                        recip(ep.t[:, 1, QB:2 * QB], ep.t[:, 1, QB:2 * QB], [ep.r], [ep.r])
                        tt("pool", ep.t[:, 0, 0:QB], ep.t[:, 0, 0:QB], ep.t[:, 1, QB:2 * QB], ALU.mult, [ep.r], [ep.r])
                        tt("pool", ep.t[:, 0, QB:2 * QB], ep.t[:, 0, QB:2 * QB], ep.t[:, 1, 0:QB], ALU.mult, [ep.r], [ep.r])
                        ot = ot2_ring.next()
                        stt("dve", ot.t[:], ep.t[:, 0, QB:2 * QB], neglam, ep.t[:, 0, 0:QB], ALU.mult, ALU.add,
                            [ep.r, sv.r], [ot.r])
                        dma("sp", OT_s[h, :, i * QB:(i + 1) * QB], ot.t[:], [ot.r], [r_ot[h][i // 2]], ot.r)

                    pending.append((g + 2, rest))
                    if i == NQB - 1 and h + 2 < 4:
                        load_head(h + 2)

            pending = []

            def flush_pending(gnow):
                while pending and pending[0][0] <= gnow:
                    pending.pop(0)[1]()

            memset("dve", ps[:, RB, QB:2 * QB], 0.0, [rb[RB]])
            load_head(0)
            load_head(1)
            NG = len(gsteps)
            issue_S2(0)
            issue_S2(1)
            for g in range(NG):
                if g + 2 < NG:
                    issue_S2(g + 2)
                issue_PV2(g)
                flush_pending(g)
            flush_pending(NG + 10)
            S.barrier()

        with ExitStack() as sc_:
            wout = sbuf(sc_, [128, 8, D], BF16, "wout")
            with ExitStack() as s0:
                wst = Ring(s0, 2, [128, D], F32, "wst2")
                for dc in range(8):
                    b = wst.next()
                    dma("sp", b.t[:], w_out[dc * 128:(dc + 1) * 128, :], [], [b.r], b.r)
                    cp("dve" if dc % 2 == 0 else "pool", wout.t[:, dc, :], b.t[:], [b.r], [wout.r])
                S.barrier()
            o_ring = Ring(sc_, 2, [128, 4, BT], BF16, "o3")
            g_ring = Ring(sc_, 2, [128, 4, BT], BF16, "g3")
            m_ring = Ring(sc_, 3, [128, 4, BT], BF16, "m3")
            oa_ring = Ring(sc_, 3, [128, 4, BT], BF16, "oa3")
            osq_ring = Ring(sc_, 2, [128, BT], BF16, "osq")
            rs_ring = Ring(sc_, 2, [128, 2, BT], F32, "rs3")
            x_ring = Ring(sc_, 3, [128, D], F32, "x3")
            y_ring = Ring(sc_, 3, [128, D], F32, "y3")
            B_ST = (0, 1)
            B_Y = ((2, 3), (4, 5), (6, 7))
            cnt = [0, 0]

            def out_stats(NT, o_b, g_b):
                oa = oa_ring.next()
                for h in range(4):
                    osq = osq_ring.next()
                    tt("pool", osq.t[:, 0:NT], o_b.t[:, h, 0:NT], o_b.t[:, h, 0:NT], ALU.mult, [o_b.r], [osq.r])
                    bk = B_ST[cnt[0] % 2]
                    cnt[0] += 1
                    mm(ps[:, bk, 0:NT], onesm.t[:], osq.t[:, 0:NT], True, True, [onesm.r, osq.r], [rb[bk]])
                    rs = rs_ring.next()
                    act(rs.t[:, 0, 0:NT], ps[:, bk, 0:NT], AF.Ln, [rb[bk], cst.r], [rs.r], bias=eps_t)
                    act(rs.t[:, 1, 0:NT], rs.t[:, 0, 0:NT], AF.Exp, [rs.r], [rs.r], scale=-0.5)
                    tt("dve", rs.t[:, 0, 0:NT], o_b.t[:, h, 0:NT], rs.t[:, 1, 0:NT], ALU.mult, [o_b.r, rs.r], [rs.r])
                    stt("dve", oa.t[:, h, 0:NT], rs.t[:, 0, 0:NT], subg_t, g_b.t[:, h, 0:NT], ALU.mult, ALU.mult,
                        [rs.r, sg.r, g_b.r], [oa.r])
                return oa

            def out_tiles(NT, oa, m_b, x_dram, y_dram):
                for ti in range(NT // 128):
                    xb_ = x_ring.next()
                    dma("sp", xb_.t[:], x_dram[ti * 128:(ti + 1) * 128, :], [], [xb_.r], xb_.r)
                    pair = B_Y[cnt[1] % 3]
                    cnt[1] += 1
                    for half in range(2):
                        bk = pair[half]
                        for dc in range(8):
                            lhs = oa.t[:, dc, ti * 128:(ti + 1) * 128] if dc < 4 else m_b.t[:, dc - 4, ti * 128:(ti + 1) * 128]
                            mm(ps[:, bk, :], lhs, wout.t[:, dc, half * 512:(half + 1) * 512], dc == 0, dc == 7,
                               [oa.r, m_b.r, wout.r], [rb[bk]])
                    yb = y_ring.next()
                    for half in range(2):
                        tt("dve", yb.t[:, half * 512:(half + 1) * 512], ps[:, pair[half], :],
                           xb_.t[:, half * 512:(half + 1) * 512], ALU.add, [rb[pair[half]], xb_.r], [yb.r])
                    dma("act", y_dram[ti * 128:(ti + 1) * 128, :], yb.t[:], [yb.r], [], yb.r)

            def load_blk(bi):
                cols = slice(bi * BT, (bi + 1) * BT)
                o_b = o_ring.next(); g_b = g_ring.next(); m_b = m_ring.next()
                dma("sp", o_b.t[:], OT_s[:, :, cols].rearrange("h p t -> p h t"), [r_ot[h][bi] for h in range(4)],
                    [o_b.r], o_b.r)
                dma("sp", g_b.t[:], G_s[:, :, cols].rearrange("h p t -> p h t"), [r_g_blk[bi]], [g_b.r], g_b.r)
                dma("sp", m_b.t[:], MBC_s[:, :, cols].rearrange("h p t -> p h t"), [r_mbc_blk[bi]], [m_b.r], m_b.r)
                return o_b, g_b, m_b

            oa_s = out_stats(128, oT_smp, gate_smp)
            blk = load_blk(0)
            oa_n = out_stats(BT, blk[0], blk[1])
            out_tiles(128, oa_s, mbc_smp, xs, ys)
            for bi in range(NBLK):
                cur_oa, cur_m = oa_n, blk[2]
                if bi + 1 < NBLK:
                    blk = load_blk(bi + 1)
                    oa_n = out_stats(BT, blk[0], blk[1])
                out_tiles(BT, cur_oa, cur_m, xp[bi * BT:(bi + 1) * BT, :], yp[bi * BT:(bi + 1) * BT, :])

        S.emit()
    return nc


_NC_CACHE = {}


def _rope_table(pos):
    half = 8
    inv = ROPE_THETA ** (-np.arange(0, 16, 2, dtype=np.float32) / np.float32(16))
    ang = pos.astype(np.float32)[:, None] * inv[None, :].astype(np.float32)
    return np.concatenate([np.cos(ang), np.sin(ang)], axis=1).astype(np.float32)


def kernel(x_prompt, x_sample, mem_prompt, cache_diff_k, cache_diff_v, cache_mem_k, cache_mem_v,
           state_lru_conv, state_lru_h, norm_g, w_in, da_q_norm_g, da_k_norm_g, lambda_q1, lambda_k1,
           lambda_q2, lambda_k2, da_subln_g, lru_conv_w, lru_conv_b, lru_w_a, lru_b_a, lru_w_x, lru_b_x,
           lru_lambda, mem_norm_g, w_mem_kv, mx_q_norm_g, mx_k_norm_g, w_out):
    f = lambda a: np.ascontiguousarray(np.asarray(a, dtype=np.float32))
    if "nc" not in _NC_CACHE:
        _NC_CACHE["nc"] = build_nc()
    nc = _NC_CACHE["nc"]
    shared = {
        "w_in": f(w_in[0]), "w_out": f(w_out[0]), "w_mkv": f(w_mem_kv[0]),
        "norm_g": f(norm_g[0]), "mem_norm_g": f(mem_norm_g[0]),
        "qg": f(da_q_norm_g[0]), "kg": f(da_k_norm_g[0]), "mqg": f(mx_q_norm_g[0]), "mkg": f(mx_k_norm_g[0]),
        "lams": f(np.stack([np.asarray(lambda_q1[0]), np.asarray(lambda_k1[0]), np.asarray(lambda_q2[0]),
                            np.asarray(lambda_k2[0])])),
        "subg": f(da_subln_g[0]), "convw": f(lru_conv_w[0]), "convb": f(lru_conv_b[0]),
        "b_a": f(lru_b_a[0]), "b_x": f(lru_b_x[0]), "lrul": f(lru_lambda[0]),
        "w_a": f(lru_w_a[0]), "w_x": f(lru_w_x[0]),
        "ident": np.eye(128, dtype=np.float32),
        "ropep": np.ascontiguousarray(_rope_table(np.arange(T)).reshape(T // 128, 128, 16).transpose(1, 0, 2).reshape(128, -1)),
        "ropes": _rope_table(PAST + (np.arange(128) % 32)),
    }
    xp_ = np.asarray(x_prompt); xs_ = np.asarray(x_sample); mem_ = np.asarray(mem_prompt)
    ck_ = np.asarray(cache_diff_k)[0]; cv_ = np.asarray(cache_diff_v)[0]
    cmk_ = np.asarray(cache_mem_k)[0]; cmv_ = np.asarray(cache_mem_v)[0]
    sc_ = np.asarray(state_lru_conv)[0]; sh_ = np.asarray(state_lru_h)[0]
    in_maps = []
    for c in range(8):
        sl = slice(4 * c, 4 * c + 4)
        m = dict(shared)
        m["xp"] = f(xp_[c]); m["xs"] = f(xs_[sl].reshape(128, D)); m["mem"] = f(mem_[c])
        m["ck"] = f(ck_[sl].reshape(4, PAST, 512)); m["cv"] = f(cv_[sl].reshape(4, PAST, 512))
        m["cmk"] = f(cmk_[sl].reshape(4, 256, 256)); m["cmv"] = f(cmv_[sl].reshape(4, 256, 256))
        m["sconv"] = f(sc_[sl]); m["sh"] = f(sh_[sl])
        in_maps.append(m)
    res = run_bass_kernel_spmd(nc, in_maps, core_ids=list(range(8)))
    R = res.results
    cat = lambda k: np.stack([np.asarray(r[k]) for r in R])
    y_p = cat("yp")
    y_s = cat("ys").reshape(32, 32, D)
    k_p = cat("kp").reshape(1, 8, T, 4, 2, 64)
    v_p = cat("vp").reshape(1, 8, T, 4, 128)
    mk_p = cat("mkp").reshape(1, 8, 256, 4, 64)
    mv_p = cat("mvp").reshape(1, 8, 256, 4, 64)
    c_p = cat("cpo").reshape(1, 8, 3, 256)
    h_p = cat("hpo").reshape(1, 8, 256)
    k_s = cat("kso").reshape(1, 32, 32, 4, 2, 64)
    v_s = cat("vso").reshape(1, 32, 32, 4, 128)
    c_s = cat("cso").reshape(1, 32, 3, 256)
    h_s = cat("hso").reshape(1, 32, 256)
    return (y_p, y_s, k_p, v_p, mk_p, mv_p, c_p, h_p, k_s, v_s, c_s, h_s)
```

```python
import math
from contextlib import ExitStack

import numpy as np
import concourse.bass as bass
import concourse.mybir as mybir
from concourse.bass_utils import run_bass_kernel_spmd

F32 = mybir.dt.float32
BF16 = mybir.dt.bfloat16
ALU = mybir.AluOpType
AF = mybir.ActivationFunctionType
AX = mybir.AxisListType

T = 8192
D = 1024
BT = 512
NBLK = T // BT
PAST = 1024
LAMBDA_INIT = 0.8 - 0.6 * math.exp(0.0)
EPS = 1e-6
ROPE_THETA = 500000.0


class Res:
    __slots__ = ("name", "w", "rs", "sem", "ndma", "excl")

    def __init__(self, name, excl=False):
        self.name = name
        self.w = None
        self.rs = []
        self.sem = None
        self.ndma = 0
        self.excl = excl


class Op:
    __slots__ = ("eng", "fn", "deps", "signal", "kind", "sem", "val")


class Sched:
    ENGS = ("pe", "act", "dve", "pool", "sp")

    def __init__(self, nc, stack):
        self.nc = nc
        self.stack = stack
        self.streams = {e: [] for e in self.ENGS}
        self.esem = {e: stack.enter_context(nc.semaphore("es_" + e)) for e in self.ENGS}
        self.nsem = len(self.ENGS)
        self.last_dma = {}

    def _add(self, eng, fn, reads, writes, kind):
        op = Op()
        op.eng = eng
        op.fn = fn
        op.kind = kind
        op.signal = False
        op.sem = None
        op.val = None
        deps = {}
        for r in reads:
            if r.excl:
                continue
            if r.w is not None:
                deps[id(r.w)] = (r.w, "raw")
        for w in list(writes) + [r for r in reads if r.excl]:
            if w.w is not None:
                deps[id(w.w)] = (w.w, "waw")
            for rd in w.rs:
                if id(rd) not in deps:
                    deps[id(rd)] = (rd, "war")
        op.deps = []
        for p, typ in deps.values():
            if p.kind == "c" and kind == "c" and p.eng == eng:
                if eng == "pe" or typ == "war":
                    continue
            op.deps.append(p)
            if p.kind == "c":
                p.signal = True
        for r in reads:
            if r.excl:
                r.w = op
                r.rs = []
            else:
                r.rs.append(op)
        for w in writes:
            w.w = op
            w.rs = []
        self.streams[eng].append(op)
        return op

    def op(self, eng, fn, reads=(), writes=()):
        return self._add(eng, fn, reads, writes, "c")

    def dma(self, q, out, in_, reads=(), writes=(), semres=None, **kw):
        if semres.sem is None:
            semres.sem = self.stack.enter_context(self.nc.semaphore("ds_%d" % self.nsem))
            self.nsem += 1
        op = self._add(q, lambda e: e.dma_start(out=out, in_=in_, **kw), reads, writes, "d")
        semres.ndma += 1
        op.sem = semres.sem
        op.val = 16 * semres.ndma
        self.last_dma[id(op.sem)] = op
        return op

    def barrier(self):
        lasts = []
        for e in self.ENGS:
            for op in reversed(self.streams[e]):
                if op.kind == "c" and op.fn is not None:
                    lasts.append(op)
                    break
        dmas = list(self.last_dma.values())
        for e in self.ENGS:
            op = Op()
            op.eng = e
            op.fn = None
            op.kind = "c"
            op.signal = False
            op.sem = None
            op.val = None
            op.deps = [p for p in lasts if p.eng != e] + dmas
            for p in op.deps:
                if p.kind == "c":
                    p.signal = True
            self.streams[e].append(op)

    def emit(self):
        nc = self.nc
        for e in self.ENGS:
            cnt = 0
            for op in self.streams[e]:
                if op.kind == "c" and op.signal:
                    cnt += 1
                    op.sem = self.esem[e]
                    op.val = cnt
        last_dma = self.last_dma
        streams = self.streams
        esem = self.esem

        def run(e, eng):
            waited = {}
            for op in streams[e]:
                for p in op.deps:
                    k = id(p.sem)
                    if waited.get(k, 0) >= p.val:
                        continue
                    waited[k] = p.val
                    eng.wait_ge(p.sem, p.val)
                if op.fn is None:
                    continue
                ins = op.fn(eng)
                if op.kind == "d":
                    ins.then_inc(op.sem, 16)
                elif op.signal:
                    ins.then_inc(esem[e], 1)
            if e == "sp":
                for p in last_dma.values():
                    if waited.get(id(p.sem), 0) < p.val:
                        eng.wait_ge(p.sem, p.val)

        with nc.Block() as block:
            @block.tensor
            def _(eng):
                run("pe", eng)

            @block.scalar
            def _(eng):
                run("act", eng)

            @block.vector
            def _(eng):
                run("dve", eng)

            @block.gpsimd
            def _(eng):
                run("pool", eng)

            @block.sync
            def _(eng):
                run("sp", eng)


class Buf:
    __slots__ = ("t", "r")

    def __init__(self, t, r):
        self.t = t
        self.r = r


def build_nc():
    nc = bass.Bass("TRN2", target_bir_lowering=False)

    def din(name, shape, dt=F32):
        return nc.dram_tensor(name, list(shape), dt, kind="ExternalInput").ap()

    def dout(name, shape):
        return nc.dram_tensor(name, list(shape), F32, kind="ExternalOutput").ap()

    def dscr(name, shape, dt=BF16):
        return nc.dram_tensor(name, list(shape), dt).ap()

    xp = din("xp", [T, D]); xs = din("xs", [128, D]); mem = din("mem", [256, D])
    ck = din("ck", [4, PAST, 512]); cv = din("cv", [4, PAST, 512])
    cmk = din("cmk", [4, 256, 256]); cmv = din("cmv", [4, 256, 256])
    sconv = din("sconv", [4, 3, 256]); sh = din("sh", [4, 256])
    w_in = din("w_in", [D, 3072]); w_out = din("w_out", [D, D]); w_mkv = din("w_mkv", [D, 512])
    norm_g = din("norm_g", [D]); mem_norm_g = din("mem_norm_g", [D])
    qg = din("qg", [64]); kg = din("kg", [64]); mqg = din("mqg", [64]); mkg = din("mkg", [64])
    lams = din("lams", [4, 64]); subg = din("subg", [128])
    convw = din("convw", [4, 256]); convb = din("convb", [256]); b_a = din("b_a", [256]); b_x = din("b_x", [256])
    lrul = din("lrul", [256]); w_a = din("w_a", [4, 64, 64]); w_x = din("w_x", [4, 64, 64])
    ident = din("ident", [128, 128]); ropep = din("ropep", [128, (T // 128) * 16]); ropes = din("ropes", [128, 16])

    yp = dout("yp", [T, D]); ys = dout("ys", [128, D])
    kp = dout("kp", [T, 512]); vp = dout("vp", [T, 512])
    mkp = dout("mkp", [256, 256]); mvp = dout("mvp", [256, 256])
    cpo = dout("cpo", [3, 256]); hpo = dout("hpo", [256])
    kso = dout("kso", [128, 512]); vso = dout("vso", [128, 512])
    cso = dout("cso", [4, 3, 256]); hso = dout("hso", [4, 256])

    QT_s = dscr("QT_s", [4, 128, T]); KT_s = dscr("KT_s", [4, 128, T]); V_s = dscr("V_s", [T, 512])
    OT_s = dscr("OT_s", [4, 128, T]); G_s = dscr("G_s", [4, 128, T]); MBC_s = dscr("MBC_s", [4, 128, T])

    gst = ExitStack()
    with gst:
        S = Sched(nc, gst)
        uid = [0]

        def sbuf(st, shape, dt, name=None):
            uid[0] += 1
            nm = "%s_%d" % (name or "b", uid[0])
            return Buf(st.enter_context(nc.sbuf_tensor(nm, list(shape), dt)), Res(nm))

        class Ring:
            def __init__(self, st, n, shape, dt, name):
                self.bufs = [sbuf(st, shape, dt, name) for _ in range(n)]
                self.i = 0

            def next(self):
                b = self.bufs[self.i % len(self.bufs)]
                self.i += 1
                return b

        ps = gst.enter_context(nc.psum_tensor("ps", [128, 8, 512], F32))
        rb = [Res("bank%d" % i, excl=True) for i in range(8)]

        def bank_bf(b):
            return ps[:, b, :].bitcast(BF16).rearrange("p (a c) -> p a c", c=128)

        def mm(out, lhsT, rhs, start, stop, r, w, skip=False):
            if skip:
                S.op("pe", lambda e: e.matmul(out=out, lhsT=lhsT, rhs=rhs, start=start, stop=stop,
                                              skip_group_check=True), r, w)
            else:
                S.op("pe", lambda e: e.matmul(out=out, lhsT=lhsT, rhs=rhs, start=start, stop=stop), r, w)

        def tr(out, in_, r, w):
            S.op("pe", lambda e: e.transpose(out=out, in_=in_, identity=identb.t[:]), list(r) + [identb.r], w)

        def tt(eng, out, in0, in1, op, r, w):
            S.op(eng, lambda e: e.tensor_tensor(out=out, in0=in0, in1=in1, op=op), r, w)

        def ts(eng, out, in0, s1, s2, op0, op1, r, w):
            if s2 is None:
                S.op(eng, lambda e: e.tensor_scalar(out=out, in0=in0, scalar1=s1, scalar2=None, op0=op0), r, w)
            else:
                S.op(eng, lambda e: e.tensor_scalar(out=out, in0=in0, scalar1=s1, scalar2=s2, op0=op0, op1=op1), r, w)

        def stt(eng, out, in0, scalar, in1, op0, op1, r, w, accum_out=None):
            if accum_out is None:
                S.op(eng, lambda e: e.scalar_tensor_tensor(out=out, in0=in0, scalar=scalar, in1=in1, op0=op0, op1=op1), r, w)
            else:
                S.op(eng, lambda e: e.scalar_tensor_tensor(out=out, in0=in0, scalar=scalar, in1=in1, op0=op0, op1=op1,
                                                           accum_out=accum_out), r, w)

        def act(out, in_, func, r, w, bias=None, scale=1.0, accum_out=None):
            if accum_out is not None:
                S.op("act", lambda e: e.activation(out=out, in_=in_, func=func, scale=scale, accum_out=accum_out), r, w)
            elif bias is None:
                S.op("act", lambda e: e.activation(out=out, in_=in_, func=func, scale=scale), r, w)
            else:
                S.op("act", lambda e: e.activation(out=out, in_=in_, func=func, bias=bias, scale=scale), r, w)

        def cp(eng, out, in_, r, w):
            if eng == "act":
                S.op("act", lambda e: e.copy(out=out, in_=in_), r, w)
            else:
                S.op(eng, lambda e: e.tensor_copy(out=out, in_=in_), r, w)

        def memset(eng, ap, val, w):
            S.op(eng, lambda e: e.memset(ap, val), [], w)

        def red(eng, out, in_, op, r, w, absval=False):
            if absval:
                S.op(eng, lambda e: e.tensor_reduce(out=out, in_=in_, axis=AX.X, op=op, apply_absolute_value=True), r, w)
            else:
                S.op(eng, lambda e: e.tensor_reduce(out=out, in_=in_, axis=AX.X, op=op), r, w)

        def recip(out, in_, r, w):
            S.op("dve", lambda e: e.reciprocal(out=out, in_=in_), r, w)

        def scan(out, d0, d1, init, r, w):
            S.op("dve", lambda e: e.tensor_tensor_scan(out=out, data0=d0, data1=d1, initial=init,
                                                       op0=ALU.mult, op1=ALU.add), r, w)

        def dma(q, out, in_, r, w, semres, **kw):
            S.dma(q, out, in_, reads=r, writes=w, semres=semres, **kw)

        nc_ctx = nc.allow_non_contiguous_dma(reason="small strided parameter / state transfers")
        gst.enter_context(nc_ctx)

        identf = sbuf(gst, [128, 128], F32, "identf")
        identb = sbuf(gst, [128, 128], BF16, "identb")
        dma("sp", identf.t[:], ident, [], [identf.r], identf.r)
        cp("dve", identb.t[:], identf.t[:], [identf.r], [identb.r])
        onesb = sbuf(gst, [128, 128], BF16, "onesb")
        memset("pool", onesb.t[:], 1.0, [onesb.r])
        onesm = sbuf(gst, [128, 128], BF16, "onesm")
        memset("pool", onesm.t[:], 1.0 / 128.0, [onesm.r])
        cst = sbuf(gst, [128, 4], F32, "cst")
        memset("pool", cst.t[:, 0:1], EPS, [cst.r])
        memset("pool", cst.t[:, 1:2], 1.0, [cst.r])
        eps_t = cst.t[:, 0:1]
        one_t = cst.t[:, 1:2]

        gv = sbuf(gst, [128, 4, 64], F32, "gv")
        for i, v in enumerate((qg, kg, mqg, mkg)):
            dma("sp", gv.t[:, i, :], v.partition_broadcast(128), [], [gv.r], gv.r)
        gfull = sbuf(gst, [128, 20, 64], F32, "gfull")
        cp("pool", gfull.t[:, 0:8, :], gv.t[:, 0:1, :].to_broadcast([128, 8, 64]), [gv.r], [gfull.r])
        cp("pool", gfull.t[:, 8:16, :], gv.t[:, 1:2, :].to_broadcast([128, 8, 64]), [gv.r], [gfull.r])
        cp("pool", gfull.t[:, 16:20, :], gv.t[:, 2:3, :].to_broadcast([128, 4, 64]), [gv.r], [gfull.r])
        gmk = sbuf(gst, [128, 4, 64], F32, "gmk")
        cp("pool", gmk.t[:, :, :], gv.t[:, 3:4, :].to_broadcast([128, 4, 64]), [gv.r], [gmk.r])
        sv = sbuf(gst, [128, 16], F32, "sv")
        red("dve", sv.t[:, 0:4], gv.t[:, :, :], ALU.max, [gv.r], [sv.r], absval=True)
        tt("dve", sv.t[:, 4:5], sv.t[:, 0:1], sv.t[:, 1:2], ALU.mult, [sv.r], [sv.r])
        tt("dve", sv.t[:, 5:6], sv.t[:, 2:3], sv.t[:, 3:4], ALU.mult, [sv.r], [sv.r])
        ts("dve", sv.t[:, 6:8], sv.t[:, 4:6], -8.0, None, ALU.mult, None, [sv.r], [sv.r])
        nshift_d = sv.t[:, 6:7]
        nshift_m = sv.t[:, 7:8]
        lb = sbuf(gst, [128, 4, 64], F32, "lb")
        dma("sp", lb.t[:, :, :], lams.partition_broadcast(128), [], [lb.r], lb.r)
        lp = sbuf(gst, [128, 2, 64], F32, "lp")
        tt("dve", lp.t[:, 0, :], lb.t[:, 0, :], lb.t[:, 1, :], ALU.mult, [lb.r], [lp.r])
        tt("dve", lp.t[:, 1, :], lb.t[:, 2, :], lb.t[:, 3, :], ALU.mult, [lb.r], [lp.r])
        red("dve", sv.t[:, 8:10], lp.t[:, :, :], ALU.add, [lp.r], [sv.r])
        act(sv.t[:, 10:12], sv.t[:, 8:10], AF.Exp, [sv.r], [sv.r])
        tt("dve", sv.t[:, 12:13], sv.t[:, 11:12], sv.t[:, 10:11], ALU.subtract, [sv.r], [sv.r])
        ts("dve", sv.t[:, 13:14], sv.t[:, 12:13], -LAMBDA_INIT, None, ALU.add, None, [sv.r], [sv.r])
        neglam = sv.t[:, 13:14]
        sg = sbuf(gst, [128, 2], F32, "sg")
        dma("sp", sg.t[:, 0:1], subg.rearrange("(p o) -> p o", o=1), [], [sg.r], sg.r)
        ts("dve", sg.t[:, 1:2], sg.t[:, 0:1], 1.0 - LAMBDA_INIT, None, ALU.mult, None, [sg.r], [sg.r])
        subg_t = sg.t[:, 1:2]
        lv = sbuf(gst, [128, 2, 12], F32, "lv")
        for cc in range(2):
            dma("sp", lv.t[:, cc, 0:4], convw[:, cc * 128:(cc + 1) * 128].rearrange("j p -> p j"), [], [lv.r], lv.r)
            for i, v in enumerate((convb, b_a, b_x, lrul)):
                dma("sp", lv.t[:, cc, 4 + i:5 + i], v[cc * 128:(cc + 1) * 128].rearrange("(p o) -> p o", o=1),
                    [], [lv.r], lv.r)
        act(lv.t[:, :, 8:9], lv.t[:, :, 7:8], AF.Exp, [lv.r], [lv.r], scale=-1.0)
        act(lv.t[:, :, 9:10], lv.t[:, :, 8:9], AF.Ln, [lv.r, cst.r], [lv.r], bias=one_t)
        ts("dve", lv.t[:, :, 10:11], lv.t[:, :, 9:10], -8.0, None, ALU.mult, None, [lv.r], [lv.r])
        wbdf = sbuf(gst, [128, 2, 2, 128], F32, "wbdf")
        memset("pool", wbdf.t[:], 0.0, [wbdf.r])
        for wi, wsrc in enumerate((w_a, w_x)):
            for n in range(4):
                cc, hf = n // 2, n % 2
                dma("sp", wbdf.t[hf * 64:(hf + 1) * 64, wi, cc, hf * 64:(hf + 1) * 64], wsrc[n], [], [wbdf.r], wbdf.r)
        wbd = sbuf(gst, [128, 2, 2, 128], BF16, "wbd")
        cp("dve", wbd.t[:], wbdf.t[:], [wbdf.r], [wbd.r])
        rope_ring = Ring(gst, 4, [128, 16], F32, "rope_t")
        rope_s = sbuf(gst, [128, 16], F32, "rope_s")
        dma("sp", rope_s.t[:], ropes, [], [rope_s.r], rope_s.r)

        qT_smp = sbuf(gst, [128, 4, 128], BF16, "qT_smp")
        kT_smp = sbuf(gst, [128, 4, 128], BF16, "kT_smp")
        qmT_smp = sbuf(gst, [128, 2, 128], BF16, "qmT_smp")
        vb_smp = sbuf(gst, [128, 512], BF16, "vb_smp")
        oT_smp = sbuf(gst, [128, 4, 128], BF16, "oT_smp")
        gate_smp = sbuf(gst, [128, 4, 128], BF16, "gate_smp")
        mbc_smp = sbuf(gst, [128, 4, 128], BF16, "mbc_smp")
        MKT_p = sbuf(gst, [128, 2, 256], BF16, "MKT_p")
        MV_p = sbuf(gst, [128, 2, 256], BF16, "MV_p")
        MKT_s = sbuf(gst, [128, 4, 2, 256], BF16, "MKT_s")
        MV_s = sbuf(gst, [128, 4, 2, 256], BF16, "MV_s")

        r_qk_blk = [Res("qkblk%d" % i) for i in range(NBLK)]
        r_v_blk = [Res("vblk%d" % i) for i in range(NBLK)]
        r_g_blk = [Res("gblk%d" % i) for i in range(NBLK)]
        r_mbc_blk = [Res("mbcblk%d" % i) for i in range(NBLK)]
        r_ot = [[Res("ot%d_%d" % (h, i)) for i in range(NBLK)] for h in range(4)]

        with ExitStack() as sa:
            win = sbuf(sa, [128, 8, 3072], BF16, "win")
            xring = Ring(sa, 4, [128, D], F32, "x")
            xbring = Ring(sa, 2, [128, D], BF16, "xb")
            string = Ring(sa, 4, [128, 4], F32, "st")
            hnT2 = [sbuf(sa, [128, 8, BT], BF16, "hnT") for _ in range(2)]
            r_hn2 = [[Res("hn%d_%d" % (k, i)) for i in range(4)] for k in range(2)]
            qkf_ring = Ring(sa, 4, [128, 1280], F32, "qkf")
            vf_ring = Ring(sa, 4, [128, 512], F32, "vf")
            sq_ring = Ring(sa, 4, [128, 1280], BF16, "sq")
            st20_ring = Ring(sa, 3, [128, 3, 20], F32, "st20")
            qkb_ring = Ring(sa, 4, [128, 1280], BF16, "qkb")
            vb_ring = Ring(sa, 2, [128, 512], BF16, "vb")
            rtmp_ring = Ring(sa, 1, [128, 4, 16, 8], F32, "rtmp")
            qkT_ring = Ring(sa, 2, [128, 8, 128], BF16, "qkT")
            qmT2 = [sbuf(sa, [128, 2, BT], BF16, "qmT") for _ in range(2)]
            ngt = sbuf(sa, [128, 2, 8], F32, "ngt")
            dma("sp", ngt.t[:, 0, :], norm_g.rearrange("(c p) -> p c", p=128), [], [ngt.r], ngt.r)
            dma("sp", ngt.t[:, 1, :], mem_norm_g.rearrange("(c p) -> p c", p=128), [], [ngt.r], ngt.r)

            B_TA, B_TB, B_Q, B_K, B_F0, B_F1, B_MO, B_MR = range(8)

            def norm_a(x_dram):
                xb_ = xring.next()
                dma("sp", xb_.t[:], x_dram, [], [xb_.r], xb_.r)
                return xb_

            def norm_b(xb_):
                st_ = string.next()
                xbf = xbring.next()
                act(xbf.t[:], xb_.t[:], AF.Square, [xb_.r], [st_.r, xbf.r], accum_out=st_.t[:, 0:1])
                act(st_.t[:, 1:2], st_.t[:, 0:1], AF.Ln, [st_.r, cst.r], [st_.r], bias=eps_t, scale=1.0 / D)
                act(st_.t[:, 2:3], st_.t[:, 1:2], AF.Exp, [st_.r], [st_.r], scale=-0.5)
                ts("dve", xbf.t[:], xb_.t[:], st_.t[:, 2:3], None, ALU.mult, None, [xb_.r, st_.r], [xbf.r])
                return xbf

            def norm_c(xbf, hn_dst, rdeps):
                va = bank_bf(B_TA)
                for dc in range(8):
                    tr(va[:, dc, :], xbf.t[:, dc * 128:(dc + 1) * 128], [xbf.r], [rb[B_TA]])
                cp("act", hn_dst, va[:, :, :], [rb[B_TA]], rdeps)

            def gnorm(src, G, gap, rsrc, rg, sq=None):
                s3 = src.rearrange("p (g d) -> p g d", d=64)
                if sq is None:
                    sq = sq_ring.next()
                    tt("pool", sq.t[:, 0:G * 64], src, src, ALU.mult, [rsrc], [sq.r])
                s20 = st20_ring.next()
                red("dve", s20.t[:, 0, 0:G], sq.t[:, 0:G * 64].rearrange("p (g d) -> p g d", d=64), ALU.add,
                    [sq.r], [s20.r])
                act(s20.t[:, 1, 0:G], s20.t[:, 0, 0:G], AF.Ln, [s20.r, cst.r], [s20.r], bias=eps_t, scale=1.0 / 64)
                act(s20.t[:, 2, 0:G], s20.t[:, 1, 0:G], AF.Exp, [s20.r], [s20.r], scale=-0.5)
                tt("dve", s3, s3, s20.t[:, 2, 0:G].unsqueeze(2).to_broadcast([128, G, 64]), ALU.mult,
                   [rsrc, s20.r], [rsrc])
                tt("dve", s3, s3, gap, ALU.mult, [rsrc, rg], [rsrc])

            with ExitStack() as s0:
                wmkv = sbuf(s0, [128, 8, 512], BF16, "wmkv")
                wst = Ring(s0, 2, [128, 1024], F32, "wst")
                for dc in range(8):
                    b = wst.next()
                    dma("sp", b.t[:, 0:512], w_mkv[dc * 128:(dc + 1) * 128, :], [], [b.r], b.r)
                    ts("dve", wmkv.t[:, dc, :], b.t[:, 0:512], ngt.t[:, 1, dc:dc + 1], None, ALU.mult, None,
                       [b.r, ngt.r], [wmkv.r])
                wst2 = Ring(s0, 3, [128, 1024], F32, "wstb")
                win_todo = [(dc, pc) for dc in range(8) for pc in range(3)]

                def stage_win(n):
                    for _ in range(n):
                        if not win_todo:
                            return
                        dc, pc = win_todo.pop(0)
                        b = wst2.next()
                        dma("sp", b.t[:], w_in[dc * 128:(dc + 1) * 128, pc * 1024:(pc + 1) * 1024], [], [b.r], b.r)
                        ts("dve", win.t[:, dc, pc * 1024:(pc + 1) * 1024], b.t[:], ngt.t[:, 0, dc:dc + 1], None,
                           ALU.mult, None, [b.r, ngt.r], [win.r])

                hnT = hnT2[0]
                for mt in range(2):
                    xb_ = norm_a(mem[mt * 128:(mt + 1) * 128, :])
                    xbf = norm_b(xb_)
                    norm_c(xbf, hnT.t[:, :, 0:128], [r_hn2[0][0]])
                    for dc in range(8):
                        mm(ps[:, B_Q, :], hnT.t[:, dc, 0:128], wmkv.t[:, dc, :], dc == 0, dc == 7,
                           [r_hn2[0][0], wmkv.r], [rb[B_Q]])
                    kvf = vf_ring.next()
                    cp("act", kvf.t[:], ps[:, B_Q, :], [rb[B_Q]], [kvf.r])
                    dma("act", mvp[mt * 128:(mt + 1) * 128, :], kvf.t[:, 256:512], [kvf.r], [], kvf.r)
                    cp("dve", MV_p.t[:, mt, :], kvf.t[:, 256:512], [kvf.r], [MV_p.r])
                    gnorm(kvf.t[:, 0:256], 4, gmk.t[:, :, :], kvf.r, gmk.r)
                    dma("act", mkp[mt * 128:(mt + 1) * 128, :], kvf.t[:, 0:256], [kvf.r], [], kvf.r)
                    kb = qkb_ring.next()
                    cp("dve", kb.t[:, 0:256], kvf.t[:, 0:256], [kvf.r], [kb.r])
                    vb_ = bank_bf(B_TB)
                    for hp_ in range(2):
                        tr(vb_[:, hp_, :], kb.t[:, hp_ * 128:(hp_ + 1) * 128], [kb.r], [rb[B_TB]])
                    cp("act", MKT_p.t[:, :, mt * 128:(mt + 1) * 128], vb_[:, 0:2, :], [rb[B_TB]], [MKT_p.r])
                    stage_win(3)
                for b in range(4):
                    for mt in range(2):
                        cf = vf_ring.next()
                        dma("sp", cf.t[:, 0:256], cmk[b, mt * 128:(mt + 1) * 128, :], [], [cf.r], cf.r)
                        dma("sp", cf.t[:, 256:512], cmv[b, mt * 128:(mt + 1) * 128, :], [], [cf.r], cf.r)
                        kb = qkb_ring.next()
                        cp("dve", kb.t[:, 0:256], cf.t[:, 0:256], [cf.r], [kb.r])
                        cp("pool", MV_s.t[:, b, mt, :], cf.t[:, 256:512], [cf.r], [MV_s.r])
                        vb_ = bank_bf(B_TB)
                        for hp_ in range(2):
                            tr(vb_[:, hp_, :], kb.t[:, hp_ * 128:(hp_ + 1) * 128], [kb.r], [rb[B_TB]])
                        cp("act", MKT_s.t[:, b, :, mt * 128:(mt + 1) * 128], vb_[:, 0:2, :], [rb[B_TB]], [MKT_s.r])
                        stage_win(2)
                stage_win(24)
                S.barrier()

            def tile_gen(x_dram, k2, col0, rope_ap, rrope, kout, vout, qmT_dst, r_qm, prompt_tg):
                hn = hnT2[k2]
                rh = r_hn2[k2][col0 // 128]
                xb_ = norm_a(x_dram)
                yield
                xbf = norm_b(xb_)
                yield
                norm_c(xbf, hn.t[:, :, col0:col0 + 128], [rh])
                yield
                qkf = qkf_ring.next()
                vf = vf_ring.next()
                for (bk, c0) in ((B_Q, 0), (B_K, 512)):
                    for dc in range(8):
                        mm(ps[:, bk, :], hn.t[:, dc, col0:col0 + 128], win.t[:, dc, c0:c0 + 512], dc == 0, dc == 7,
                           [rh, win.r], [rb[bk]])
                cp("act", qkf.t[:, 0:1024].rearrange("p (a c) -> p a c", c=512), ps[:, B_Q:B_K + 1, :],
                   [rb[B_Q], rb[B_K]], [qkf.r])
                sq = sq_ring.next()
                act(sq.t[:, 0:1024].rearrange("p (a c) -> p a c", c=512), ps[:, B_Q:B_K + 1, :], AF.Square,
                    [rb[B_Q], rb[B_K]], [sq.r])
                yield
                for (bk, c0, n) in ((B_Q, 1024, 512), (B_K, 2560, 256)):
                    for dc in range(8):
                        mm(ps[:, bk, 0:n], hn.t[:, dc, col0:col0 + 128], win.t[:, dc, c0:c0 + n], dc == 0, dc == 7,
                           [rh, win.r], [rb[bk]])
                cp("act", vf.t[:], ps[:, B_Q, :], [rb[B_Q]], [vf.r])
                cp("act", qkf.t[:, 1024:1280], ps[:, B_K, 0:256], [rb[B_K]], [qkf.r])
                act(sq.t[:, 1024:1280], ps[:, B_K, 0:256], AF.Square, [rb[B_K]], [sq.r])
                dma("act", vout, vf.t[:], [vf.r], [], vf.r)
                if prompt_tg is None:
                    cp("dve", vb_smp.t[:], vf.t[:], [vf.r], [vb_smp.r])
                else:
                    vbb = vb_ring.next()
                    cp("dve", vbb.t[:], vf.t[:], [vf.r], [vbb.r])
                    dma("act", V_s[prompt_tg * 128:(prompt_tg + 1) * 128, :], vbb.t[:], [vbb.r],
                        [r_v_blk[prompt_tg // 4]], vbb.r)
                yield
                gnorm(qkf.t[:, 0:1280], 20, gfull.t[:, :, :], qkf.r, gfull.r, sq)
                yield
                q3 = qkf.t[:, 0:1024].rearrange("p (g d) -> p g d", d=64)
                x1 = q3[:, :, 0:8]
                x2 = q3[:, :, 8:16]
                cosb = rope_ap[:, 0:8].unsqueeze(1).to_broadcast([128, 16, 8])
                sinb = rope_ap[:, 8:16].unsqueeze(1).to_broadcast([128, 16, 8])
                rtmp = rtmp_ring.next()
                tt("pool", rtmp.t[:, 0], x1, cosb, ALU.mult, [qkf.r, rrope], [rtmp.r])
                tt("pool", rtmp.t[:, 1], x2, sinb, ALU.mult, [qkf.r, rrope], [rtmp.r])
                tt("pool", rtmp.t[:, 2], x2, cosb, ALU.mult, [qkf.r, rrope], [rtmp.r])
                tt("pool", rtmp.t[:, 3], x1, sinb, ALU.mult, [qkf.r, rrope], [rtmp.r])
                tt("pool", x1, rtmp.t[:, 0], rtmp.t[:, 1], ALU.subtract, [rtmp.r], [qkf.r])
                tt("pool", x2, rtmp.t[:, 2], rtmp.t[:, 3], ALU.add, [rtmp.r], [qkf.r])
                dma("sp", kout, qkf.t[:, 512:1024], [qkf.r], [], qkf.r)
                yield
                qkb = qkb_ring.next()
                cp("dve", qkb.t[:], qkf.t[:], [qkf.r], [qkb.r])
                yield
                vB = bank_bf(B_TB)
                for g in range(8):
                    tr(vB[:, g, :], qkb.t[:, g * 128:(g + 1) * 128], [qkb.r], [rb[B_TB]])
                if prompt_tg is None:
                    cp("act", qT_smp.t[:], vB[:, 0:4, :], [rb[B_TB]], [qT_smp.r])
                    cp("act", kT_smp.t[:], vB[:, 4:8, :], [rb[B_TB]], [kT_smp.r])
                else:
                    qkT = qkT_ring.next()
                    cp("act", qkT.t[:], vB[:, :, :], [rb[B_TB]], [qkT.r])
                    cols = slice(prompt_tg * 128, (prompt_tg + 1) * 128)
                    dma("act", QT_s[:, :, cols].rearrange("h p t -> p h t"), qkT.t[:, 0:4, :], [qkT.r],
                        [r_qk_blk[prompt_tg // 4]], qkT.r)
                    dma("act", KT_s[:, :, cols].rearrange("h p t -> p h t"), qkT.t[:, 4:8, :], [qkT.r],
                        [r_qk_blk[prompt_tg // 4]], qkT.r)
                vA = bank_bf(B_TA)
                for g in range(2):
                    tr(vA[:, g, :], qkb.t[:, 1024 + g * 128:1024 + (g + 1) * 128], [qkb.r], [rb[B_TA]])
                cp("act", qmT_dst, vA[:, 0:2, :], [rb[B_TA]], [r_qm])

            lx_p = sbuf(sa, [128, 2, 1, 3 + BT], F32, "lx_p")
            lx_s = sbuf(sa, [128, 2, 4, 3 + 32], F32, "lx_s")
            r_lx_p = [Res("lxp0"), Res("lxp1")]
            r_lx_s = [Res("lxs0"), Res("lxs1")]
            hprev = sbuf(sa, [128, 2], F32, "hprev")
            r_hprev = [Res("hp0"), Res("hp1")]
            h0s = sbuf(sa, [128, 2, 4], F32, "h0s")
            memset("pool", lx_p.t[:, :, :, 0:3], 0.0, r_lx_p)
            memset("pool", hprev.t[:], 0.0, r_hprev)
            for cc in range(2):
                for sb_i in range(4):
                    dma("sp", lx_s.t[:, cc, sb_i, 0:3], sconv[sb_i, :, cc * 128:(cc + 1) * 128].rearrange("j p -> p j"),
                        [], [r_lx_s[cc]], r_lx_s[cc])
                dma("sp", h0s.t[:, cc, :], sh[:, cc * 128:(cc + 1) * 128].rearrange("b p -> p b"), [], [h0s.r], h0s.r)
            fring = Ring(sa, 1, [128, 5, BT], F32, "fw")
            hl_s = sbuf(sa, [128, 2, 4], F32, "hl_s")
            xcb_ring = Ring(sa, 1, [128, BT], BF16, "xcb")
            slg_ring = Ring(sa, 1, [128, 2, BT], BF16, "slg")
            smg_ring = Ring(sa, 1, [128, 2, BT], BF16, "smg")
            gate_ring = Ring(sa, 1, [128, 4, BT], BF16, "gate")
            mbc_ring = Ring(sa, 1, [128, 4, BT], BF16, "mbc")
            e_ring = Ring(sa, 2, [128, BT], BF16, "eT")
            mo_ring = Ring(sa, 1, [128, 1, BT], F32, "mo")
            fbank = [0]

            def next_fbank():
                fbank[0] += 1
                return B_F0 if fbank[0] % 2 else B_F1

            def feat_gen(part, NT, k2, nseg, L, lxb, r_lx, hinit_fn, gate_b, mbc_b, r_mbcB, qmT_ap, rqm, memgroups,
                         last, outs):
                hn = hnT2[k2]
                rhs_ = r_hn2[k2][0:max(1, NT // 128)]

                def proj(col, bk=None):
                    if bk is None:
                        bk = next_fbank()
                    for dc in range(8):
                        mm(ps[:, bk, 0:NT], win.t[:, dc, col:col + 128], hn.t[:, dc, 0:NT], dc == 0, dc == 7,
                           [win.r] + rhs_, [rb[bk]])
                    return bk

                if part == "A":
                    smg = smg_ring.next()
                    for i in range(4):
                        bk = proj(1536 + i * 128)
                        act(gate_b.t[:, i, 0:NT], ps[:, bk, 0:NT], AF.Silu, [rb[bk]], [gate_b.r])
                    for hp_ in range(2):
                        bk = proj(2816 + hp_ * 128)
                        act(smg.t[:, hp_, 0:NT], ps[:, bk, 0:NT], AF.Silu, [rb[bk]], [smg.r])
                    yield
                    for _ in mem_part(NT, mbc_b, smg, qmT_ap, rqm, memgroups):
                        yield
                    return
                slg = slg_ring.next()
                for cc in range(2):
                    proj(2048 + cc * 128, B_TA)
                    cp("act", lxb.t[:, cc, :, 3:3 + L], ps[:, B_TA, 0:NT].rearrange("p (s l) -> p s l", l=L),
                       [rb[B_TA]], [r_lx[cc]])
                    yield
                for cc in range(2):
                    proj(2304 + cc * 128, B_TB)
                    act(slg.t[:, cc, 0:NT], ps[:, B_TB, 0:NT], AF.Silu, [rb[B_TB]], [slg.r])
                    yield
                for cc in range(2):
                    fw = fring.next()
                    xc = fw.t[:, 0, 0:NT]
                    xc3 = xc.rearrange("p (s l) -> p s l", l=L)
                    ts("pool", xc3, lxb.t[:, cc, :, 0:L], lv.t[:, cc, 0:1], lv.t[:, cc, 4:5], ALU.mult, ALU.add,
                       [r_lx[cc], lv.r], [fw.r])
                    for j in range(1, 4):
                        stt("dve", xc3, lxb.t[:, cc, :, j:j + L], lv.t[:, cc, j:j + 1], xc3, ALU.mult, ALU.add,
                            [r_lx[cc], lv.r, fw.r], [fw.r])
                    xcb = xcb_ring.next()
                    cp("dve", xcb.t[:, 0:NT], xc, [fw.r], [xcb.r])
                    yield
                    bka, bkx = B_TA, B_TB
                    mm(ps[:, bka, 0:NT], wbd.t[:, 0, cc, :], xcb.t[:, 0:NT], True, True, [wbd.r, xcb.r], [rb[bka]])
                    mm(ps[:, bkx, 0:NT], wbd.t[:, 1, cc, :], xcb.t[:, 0:NT], True, True, [wbd.r, xcb.r], [rb[bkx]])
                    act(fw.t[:, 1, 0:NT], ps[:, bka, 0:NT], AF.Sigmoid, [rb[bka], lv.r], [fw.r], bias=lv.t[:, cc, 5:6])
                    act(fw.t[:, 2, 0:NT], ps[:, bkx, 0:NT], AF.Sigmoid, [rb[bkx], lv.r], [fw.r], bias=lv.t[:, cc, 6:7])
                    yield
                    a_ = fw.t[:, 3, 0:NT]
                    tmp = fw.t[:, 4, 0:NT]
                    h_ = fw.t[:, 1, 0:NT]
                    act(a_, fw.t[:, 1, 0:NT], AF.Exp, [fw.r, lv.r], [fw.r], scale=lv.t[:, cc, 10:11])
                    tt("pool", tmp, a_, a_, ALU.mult, [fw.r], [fw.r])
                    act(tmp, tmp, AF.Ln, [fw.r, cst.r], [fw.r], bias=one_t, scale=-1.0)
                    act(tmp, tmp, AF.Exp, [fw.r], [fw.r], scale=0.5)
                    tt("dve", fw.t[:, 2, 0:NT], fw.t[:, 2, 0:NT], fw.t[:, 0, 0:NT], ALU.mult, [fw.r], [fw.r])
                    tt("dve", tmp, tmp, fw.t[:, 2, 0:NT], ALU.mult, [fw.r], [fw.r])
                    yield
                    for s_ in range(nseg):
                        scan(h_[:, s_ * L:(s_ + 1) * L], a_[:, s_ * L:(s_ + 1) * L], tmp[:, s_ * L:(s_ + 1) * L],
                             hinit_fn(cc, s_), [fw.r, r_hprev[cc], h0s.r], [fw.r])
                    tt("dve", mbc_b.t[:, cc, 0:NT], h_, slg.t[:, cc, 0:NT], ALU.mult, [fw.r, slg.r], [r_mbcB])
                    if nseg == 1:
                        cp("pool", hprev.t[:, cc:cc + 1], h_[:, L - 1:L], [fw.r], [r_hprev[cc]])
                        if last:
                            dma("sp", outs["conv"][:, cc * 128:(cc + 1) * 128].rearrange("j p -> p j"),
                                lxb.t[:, cc, 0, L:L + 3], [r_lx[cc]], [], r_lx[cc])
                            dma("sp", outs["h"][cc * 128:(cc + 1) * 128].rearrange("(p o) -> p o", o=1),
                                hprev.t[:, cc:cc + 1], [r_hprev[cc]], [], r_hprev[cc])
                        else:
                            cp("pool", lxb.t[:, cc, 0, 0:3], lxb.t[:, cc, 0, L:L + 3], [r_lx[cc]], [r_lx[cc]])
                    else:
                        for sb_i in range(nseg):
                            dma("sp", outs["conv"][sb_i, :, cc * 128:(cc + 1) * 128].rearrange("j p -> p j"),
                                lxb.t[:, cc, sb_i, L:L + 3], [r_lx[cc]], [], r_lx[cc])
                        hlast = hl_s.t[:, cc, 0:nseg]
                        cp("pool", hlast, h_.rearrange("p (s l) -> p s l", l=L)[:, :, L - 1], [fw.r], [hl_s.r])
                        dma("sp", outs["h"][:, cc * 128:(cc + 1) * 128].rearrange("b p -> p b"), hlast, [hl_s.r], [],
                            hl_s.r)
                    yield
            def mem_part(NT, mbc_b, smg, qmT_ap, rqm, memgroups):
                for hp_ in range(2):
                    mo = mo_ring.next()
                    its = []
                    for (c0, n, mkt_fn, mv_fn, rmk, rmv) in memgroups:
                        subs = [(c0 + k_ * 256, 256) for k_ in range(n // 256)] if n > 256 else [(c0, n)]
                        for (cc0, nn) in subs:
                            for hh in range(2):
                                its.append((cc0, nn, hh, mkt_fn, mv_fn, rmk, rmv))
                    ebuf = {}

                    def mS(k_, hp_=hp_, its=its, ebuf=ebuf):
                        cc0, nn, hh, mkt_fn, mv_fn, rmk, rmv = its[k_]
                        bk = next_fbank()
                        for mt in range(2):
                            mm(ps[:, bk, mt * nn:(mt + 1) * nn], mkt_fn(hp_, mt)[hh * 64:(hh + 1) * 64, :],
                               qmT_ap[hh * 64:(hh + 1) * 64, hp_, cc0:cc0 + nn], True, True, [rmk, rqm], [rb[bk]])
                        eb = e_ring.next()
                        act(eb.t[:, 0:2 * nn], ps[:, bk, 0:2 * nn], AF.Exp, [rb[bk], sv.r], [eb.r], bias=nshift_m,
                            scale=0.125)
                        ebuf[k_] = eb

                    def mPV(k_, hp_=hp_, its=its, ebuf=ebuf):
                        cc0, nn, hh, mkt_fn, mv_fn, rmk, rmv = its[k_]
                        h = 2 * hp_ + hh
                        eb = ebuf.pop(k_)
                        for mt in range(2):
                            mm(ps[hh * 64:(hh + 1) * 64, B_MO, cc0:cc0 + nn], mv_fn(mt)[:, h * 64:(h + 1) * 64],
                               eb.t[:, mt * nn:(mt + 1) * nn], mt == 0, mt == 1, [rmv, eb.r], [rb[B_MO]])
                        for mt in range(2):
                            mm(ps[hh * 64:(hh + 1) * 64, B_MR, cc0:cc0 + nn], onesb.t[:, 0:64],
                               eb.t[:, mt * nn:(mt + 1) * nn], mt == 0, mt == 1, [onesb.r, eb.r], [rb[B_MR]])

                    mS(0)
                    for k_ in range(len(its)):
                        if k_ + 1 < len(its):
                            mS(k_ + 1)
                        mPV(k_)
                        yield
                    recip(mo.t[:, 0, 0:NT], ps[:, B_MR, 0:NT], [rb[B_MR]], [mo.r])
                    tt("dve", mo.t[:, 0, 0:NT], ps[:, B_MO, 0:NT], mo.t[:, 0, 0:NT], ALU.mult, [rb[B_MO], mo.r], [mo.r])
                    tt("dve", mbc_b.t[:, 2 + hp_, 0:NT], mo.t[:, 0, 0:NT], smg.t[:, hp_, 0:NT], ALU.mult,
                       [mo.r, smg.r], [mbc_b.r])
                    yield

            tiles = []
            for bi in range(NBLK):
                for ti in range(4):
                    tiles.append((bi, ti))
            tiles.append((NBLK, 0))
            tile_i = [0]
            active = []
            tiles_left = {bi: 4 for bi in range(NBLK)}
            tiles_left[NBLK] = 1
            feat_q = []
            feat_done = set()
            cur_feat = [None]
            FEAT_STEPS = 1

            def make_tile(bi, ti):
                k2 = bi % 2
                if bi < NBLK:
                    tg = bi * 4 + ti
                    rp_ = rope_ring.next()
                    dma("sp", rp_.t[:], ropep[:, tg * 16:(tg + 1) * 16], [], [rp_.r], rp_.r)
                    return tile_gen(xp[tg * 128:(tg + 1) * 128, :], k2, ti * 128, rp_.t[:, :], rp_.r,
                                    kp[tg * 128:(tg + 1) * 128, :], vp[tg * 128:(tg + 1) * 128, :],
                                    qmT2[k2].t[:, :, ti * 128:(ti + 1) * 128], qmT2[k2].r, tg)
                return tile_gen(xs, k2, 0, rope_s.t[:, :], rope_s.r, kso, vso, qmT2[k2].t[:, :, 0:128], qmT2[k2].r, None)

            def make_feat(bi):
                k2 = bi % 2
                r_mbcB = Res("mbcB%d" % bi)
                if bi < NBLK:
                    gate_b = gate_ring.next(); mbc_b = mbc_ring.next()
                    cols = slice(bi * BT, (bi + 1) * BT)

                    def post(gate_b=gate_b, mbc_b=mbc_b, cols=cols, bi=bi, r_mbcB=r_mbcB):
                        dma("act", G_s[:, :, cols].rearrange("h p t -> p h t"), gate_b.t[:], [gate_b.r], [r_g_blk[bi]],
                            gate_b.r)
                        dma("act", MBC_s[:, :, cols].rearrange("h p t -> p h t"), mbc_b.t[:], [mbc_b.r, r_mbcB],
                            [r_mbc_blk[bi]], mbc_b.r)

                    args = (BT, k2, 1, BT, lx_p, r_lx_p, lambda cc, s_: hprev.t[:, cc:cc + 1], gate_b, mbc_b, r_mbcB,
                            qmT2[k2].t, qmT2[k2].r,
                            [(0, BT, lambda hp_, mt: MKT_p.t[:, hp_, mt * 128:(mt + 1) * 128],
                              lambda mt: MV_p.t[:, mt, :], MKT_p.r, MV_p.r)],
                            bi == NBLK - 1, {"conv": cpo, "h": hpo})
                    return [bi, feat_gen("A", *args), feat_gen("B", *args), post, mbc_b, r_mbcB]

                def mk_grp(b):
                    return (32 * b, 32, lambda hp_, mt: MKT_s.t[:, b, hp_, mt * 128:(mt + 1) * 128],
                            lambda mt: MV_s.t[:, b, mt, :], MKT_s.r, MV_s.r)

                args = (128, k2, 4, 32, lx_s, r_lx_s, lambda cc, s_: h0s.t[:, cc, s_:s_ + 1], gate_smp, mbc_smp, r_mbcB,
                        qmT2[k2].t, qmT2[k2].r, [mk_grp(b) for b in range(4)], True, {"conv": cso, "h": hso})
                return [bi, feat_gen("A", *args), feat_gen("B", *args), None, mbc_smp, r_mbcB]

            while tile_i[0] < len(tiles) or active or feat_q or cur_feat[0] is not None:
                while len(active) < 4 and tile_i[0] < len(tiles):
                    bi, ti = tiles[tile_i[0]]
                    if bi >= 2 and (bi - 2) not in feat_done:
                        break
                    if active and min(a_[2][0] for a_ in active) < 2:
                        break
                    active.append((bi, make_tile(bi, ti), [0]))
                    tile_i[0] += 1
                for item in list(active):
                    bi, g, cnt_ = item
                    cnt_[0] += 1
                    try:
                        next(g)
                    except StopIteration:
                        active.remove(item)
                        tiles_left[bi] -= 1
                        if tiles_left[bi] == 0:
                            feat_q.append(bi)
                if cur_feat[0] is None and feat_q:
                    cur_feat[0] = make_feat(feat_q.pop(0))
                if cur_feat[0] is not None:
                    cf = cur_feat[0]
                    for gi in (1, 2):
                        if cf[gi] is not None:
                            try:
                                next(cf[gi])
                            except StopIteration:
                                cf[gi] = None
                    if cf[1] is None and cf[2] is None:
                        if cf[3] is not None:
                            cf[3]()
                        feat_done.add(cf[0])
                        cur_feat[0] = None
            S.barrier()

        with ExitStack() as sb_:
            KT_ring = Ring(sb_, 2, [128, T], BF16, "KTh")
            V_ring = Ring(sb_, 2, [128, T // 128, 128], BF16, "Vh")
            QT_ring = Ring(sb_, 4, [128, BT], BF16, "QTb")
            E_ring = Ring(sb_, 4, [128, 2, 32], BF16, "E")
            ep_ring = Ring(sb_, 2, [128, 4, 32], F32, "ep")
            ot_ring = Ring(sb_, 2, [128, 32], BF16, "ot")
            SB = [(0, 1), (2, 3)]
            B_O = (4, 5)
            B_R = (6, 7)

            acc = sbuf(sb_, [128, 2, 32], F32, "acc")
            r_acc = [Res("acc0"), Res("acc1")]
            hl_ring = Ring(sb_, 2, [128, 2, 2, 32], BF16, "hl")
            r_hl = {}

            SB3s = [(0, 1), (2, 3), (4, 5)]
            OBs = 6
            RBs = 7

            def attn_unit(steps, kt_fn, v_fn, qt_fn, rk, rv, rq, NQ, out_ap, rout):
                n = len(steps)
                ebs = {}

                def issue_S(j):
                    ksz, q0, _, _ = steps[j]
                    pair = SB3s[j % 3]
                    for c in range(2):
                        mm(ps[0:ksz, pair[c], q0:NQ], kt_fn(j, c), qt_fn(c)[:, q0:NQ], True, True,
                           [rk, rq], [rb[pair[c]]])
                    eb = E_ring.next()
                    act(eb.t[0:ksz, :, q0:NQ], ps[0:ksz, pair[0]:pair[1] + 1, q0:NQ], AF.Exp,
                        [rb[pair[0]], rb[pair[1]], sv.r], [eb.r], bias=nshift_d[0:ksz, :], scale=0.125)
                    ebs[j] = eb

                def issue_PV(j):
                    ksz, q0, mq0, pm = steps[j]
                    eb = ebs.pop(j)
                    if mq0 is not None:
                        memset("pool", eb.t[64:128, :, mq0:mq0 + 64], 0.0, [eb.r])
                    for (p0, p1) in pm:
                        memset("pool", eb.t[p0:p1, :, q0:NQ], 0.0, [eb.r])
                    for c in range(2):
                        mm(ps[:, OBs, c * NQ + q0:(c + 1) * NQ], v_fn(j), eb.t[0:ksz, c, q0:NQ],
                           j == 0 and c == 0, j == n - 1, [rv, eb.r], [rb[OBs]], skip=True)
                    if j == 0:
                        cp("dve", acc.t[:, 0, 0:NQ], eb.t[:, 0, 0:NQ], [eb.r], [r_acc[0]])
                    else:
                        tt("dve", acc.t[:, 0, q0:NQ], acc.t[:, 0, q0:NQ], eb.t[:, 0, q0:NQ], ALU.add,
                           [eb.r, r_acc[0]], [r_acc[0]])
                    mm(ps[:, RBs, q0:NQ], onesb.t[0:ksz, :], eb.t[0:ksz, 1, q0:NQ], j == 0, j == n - 1,
                       [onesb.r, eb.r], [rb[RBs]], skip=True)

                issue_S(0)
                if n > 1:
                    issue_S(1)
                for j in range(n):
                    if j + 2 < n:
                        issue_S(j + 2)
                    issue_PV(j)
                hl = hl_ring.next()
                if id(hl) not in r_hl:
                    r_hl[id(hl)] = [Res("hl0"), Res("hl1")]
                rh = r_hl[id(hl)]
                cp("dve", hl.t[:, 0, 0, 0:NQ], acc.t[:, 0, 0:NQ], [r_acc[0]], [rh[0]])
                tt("dve", hl.t[:, 0, 1, 0:NQ], acc.t[:, 0, 0:NQ], hl.t[:, 0, 0, 0:NQ], ALU.subtract,
                   [r_acc[0], rh[0]], [rh[0]])
                mm(ps[:, RBs, NQ:2 * NQ], onesb.t[:], hl.t[:, 0, 0, 0:NQ], False, False, [onesb.r, rh[0]], [rb[RBs]], skip=True)
                mm(ps[:, RBs, NQ:2 * NQ], onesb.t[:], hl.t[:, 0, 1, 0:NQ], False, True, [onesb.r, rh[0]], [rb[RBs]], skip=True)
                ep = ep_ring.next()
                cp("dve", ep.t[:, 0, 0:NQ], ps[:, OBs, 0:NQ], [rb[OBs]], [ep.r])
                cp("dve", ep.t[:, 1, 0:NQ], ps[:, OBs, NQ:2 * NQ], [rb[OBs]], [ep.r])
                cp("dve", ep.t[:, 2, 0:NQ], ps[:, RBs, 0:NQ], [rb[RBs]], [ep.r])
                recip(ep.t[:, 3, 0:NQ], ps[:, RBs, NQ:2 * NQ], [rb[RBs]], [ep.r])
                recip(ep.t[:, 2, 0:NQ], ep.t[:, 2, 0:NQ], [ep.r], [ep.r])
                tt("dve", ep.t[:, 0, 0:NQ], ep.t[:, 0, 0:NQ], ep.t[:, 3, 0:NQ], ALU.mult, [ep.r], [ep.r])
                tt("dve", ep.t[:, 1, 0:NQ], ep.t[:, 1, 0:NQ], ep.t[:, 2, 0:NQ], ALU.mult, [ep.r], [ep.r])
                stt("dve", out_ap, ep.t[:, 1, 0:NQ], neglam, ep.t[:, 0, 0:NQ], ALU.mult, ALU.add, [ep.r, sv.r], [rout])

            ckst = Ring(sb_, 3, [128, 512], F32, "ckst")
            ckb_ring = Ring(sb_, 2, [128, 512], BF16, "ckb")
            KTs_ring = Ring(sb_, 2, [128, 4, PAST], BF16, "KTs")
            Vs_ring = Ring(sb_, 2, [128, PAST // 128, 512], BF16, "Vs")
            for b in range(4):
                KTs = KTs_ring.next()
                Vs = Vs_ring.next()
                for kt in range(PAST // 128):
                    cf = ckst.next()
                    dma("sp", cf.t[:], ck[b, kt * 128:(kt + 1) * 128, :], [], [cf.r], cf.r)
                    cb = ckb_ring.next()
                    cp("dve", cb.t[:], cf.t[:], [cf.r], [cb.r])
                    bk = SB[kt % 2][0]
                    vw = bank_bf(bk)
                    for h in range(4):
                        tr(vw[:, h, :], cb.t[:, h * 128:(h + 1) * 128], [cb.r], [rb[bk]])
                    cp("dve", KTs.t[:, :, kt * 128:(kt + 1) * 128], vw[:, 0:4, :], [rb[bk]], [KTs.r])
                    cf2 = ckst.next()
                    dma("sp", cf2.t[:], cv[b, kt * 128:(kt + 1) * 128, :], [], [cf2.r], cf2.r)
                    cp("pool", Vs.t[:, kt, :], cf2.t[:], [cf2.r], [Vs.r])
                for h in range(4):
                    steps = [(128, 0, None, []) for _ in range(PAST // 128)]
                    pm = [(32 * qd, 32 * qd + 32) for qd in range(4) if qd != b]
                    steps.append((128, 0, None, pm))

                    def kt_fn(j, c, KTs=KTs, h=h):
                        if j < PAST // 128:
                            return KTs.t[c * 64:(c + 1) * 64, h, j * 128:(j + 1) * 128]
                        return kT_smp.t[c * 64:(c + 1) * 64, h, :]

                    def v_fn(j, Vs=Vs, h=h):
                        if j < PAST // 128:
                            return Vs.t[:, j, h * 128:(h + 1) * 128]
                        return vb_smp.t[:, h * 128:(h + 1) * 128]

                    def qt_fn(c, h=h, b=b):
                        return qT_smp.t[c * 64:(c + 1) * 64, h, 32 * b:32 * b + 32]

                    attn_unit(steps, kt_fn, v_fn, qt_fn, KTs.r, Vs.r, qT_smp.r, 32,
                              oT_smp.t[:, h, 32 * b:32 * b + 32], oT_smp.r)

            QB = 256
            NQB = T // QB
            SP3 = [(0, 1), (2, 3), (4, 5)]
            OB = 6
            RB = 7
            E4_ring = Ring(sb_, 8, [128, 2, 2, QB], BF16, "E4")
            acc2 = sbuf(sb_, [128, 2, QB], F32, "acc2")
            hl2_ring = Ring(sb_, 2, [128, 2, 2, QB], BF16, "hl2")
            ep2_ring = Ring(sb_, 2, [128, 2, 2 * QB], F32, "ep2")
            ot2_ring = Ring(sb_, 3, [128, QB], BF16, "ot2")
            heads = {}

            def load_head(h):
                KTh = KT_ring.next(); Vh = V_ring.next()
                dma("sp", KTh.t[:], KT_s[h], r_qk_blk, [KTh.r], KTh.r)
                for part in range(4):
                    dma("sp", Vh.t[:, part * 16:(part + 1) * 16, :],
                        V_s[part * 2048:(part + 1) * 2048, h * 128:(h + 1) * 128].rearrange("(t p) e -> p t e", p=128),
                        r_v_blk, [Vh.r], Vh.r)
                heads[h] = (KTh, Vh)

            gsteps = []
            for h in range(4):
                for i in range(NQB):
                    for p in range(i + 1):
                        gsteps.append((h, i, p))
            qblk = {}

            def ensure_q(h, i):
                key = (h, i // 2)
                if key not in qblk:
                    qb = QT_ring.next()
                    dma("sp", qb.t[:], QT_s[h, :, (i // 2) * BT:(i // 2 + 1) * BT], [r_qk_blk[i // 2]], [qb.r], qb.r)
                    qblk[key] = qb
                return qblk[key]

            ebs = {}

            def issue_S2(g):
                h, i, p = gsteps[g]
                if p == 0:
                    ensure_q(h, i)
                    if i + 2 < NQB:
                        ensure_q(h, i + 2)
                KTh, Vh = heads[h]
                qb = ensure_q(h, i)
                pair = SP3[g % 3]
                qo = (i % 2) * QB
                for t in range(2):
                    kt = 2 * p + t
                    for c in range(2):
                        mm(ps[:, pair[c], t * QB:(t + 1) * QB], KTh.t[c * 64:(c + 1) * 64, kt * 128:(kt + 1) * 128],
                           qb.t[c * 64:(c + 1) * 64, qo:qo + QB], True, True, [KTh.r, qb.r], [rb[pair[c]]])
                eb = E4_ring.next()
                act(eb.t[:].rearrange("p c t q -> p c (t q)"), ps[:, pair[0]:pair[1] + 1, :], AF.Exp,
                    [rb[pair[0]], rb[pair[1]], sv.r], [eb.r], bias=nshift_d, scale=0.125)
                ebs[g] = eb

            def issue_PV2(g):
                h, i, p = gsteps[g]
                KTh, Vh = heads[h]
                eb = ebs.pop(g)
                if p == i:
                    memset("pool", eb.t[64:128, :, 0, 0:64], 0.0, [eb.r])
                    memset("pool", eb.t[:, :, 1, 0:128], 0.0, [eb.r])
                    memset("pool", eb.t[64:128, :, 1, 128:192], 0.0, [eb.r])
                for t in range(2):
                    kt = 2 * p + t
                    for c in range(2):
                        mm(ps[:, OB, c * QB:(c + 1) * QB], Vh.t[:, kt, :], eb.t[:, c, t, :],
                           p == 0 and t == 0 and c == 0, p == i and t == 1, [Vh.r, eb.r], [rb[OB]], skip=True)
                for t in range(2):
                    mm(ps[:, RB, 0:QB], onesb.t[:], eb.t[:, 1, t, :], p == 0 and t == 0, False, [onesb.r, eb.r], [rb[RB]],
                       skip=True)
                if p == 0:
                    cp("dve", acc2.t[:, :, :], eb.t[:, 0, :, :], [eb.r], [acc2.r])
                else:
                    tt("dve", acc2.t[:, :, :], acc2.t[:, :, :], eb.t[:, 0, :, :], ALU.add, [eb.r, acc2.r], [acc2.r])
                if p == i:
                    flush_pending(NG)
                    ep = ep2_ring.next()
                    cp("dve", ep.t[:, 0, :], ps[:, OB, :], [rb[OB]], [ep.r])
                    cp("dve", ep.t[:, 1, 0:QB], ps[:, RB, 0:QB], [rb[RB]], [ep.r])
                    recip(ep.t[:, 1, 0:QB], ep.t[:, 1, 0:QB], [ep.r], [ep.r])
                    hl = hl2_ring.next()
                    cp("dve", hl.t[:, 0, :, :], acc2.t[:, :, :], [acc2.r], [hl.r])
                    tt("dve", hl.t[:, 1, :, :], acc2.t[:, :, :], hl.t[:, 0, :, :], ALU.subtract, [acc2.r, hl.r], [hl.r])

                    def rest(ep=ep, hl=hl, h=h, i=i):
                        for k_ in range(2):
                            for t in range(2):
                                mm(ps[:, RB, QB:2 * QB], onesb.t[:], hl.t[:, k_, t, :], False, k_ == 1 and t == 1,
                                   [onesb.r, hl.r], [rb[RB]], skip=True)
                        cp("dve", ep.t[:, 1, QB:2 * QB], ps[:, RB, QB:2 * QB], [rb[RB]], [ep.r])
                        memset("dve", ps[:, RB, QB:2 * QB], 0.0, [rb[RB]])
                        recip(ep.t[:, 1, QB:2 * QB], ep.t[:, 1, QB:2 * QB], [ep.r], [ep.r])
                        tt("pool", ep.t[:, 0, 0:QB], ep.t[:, 0, 0:QB], ep.t[:, 1, QB:2 * QB], ALU.mult, [ep.r], [ep.r])
                        tt("pool", ep.t[:, 0, QB:2 * QB], ep.t[:, 0, QB:2 * QB], ep.t[:, 1, 0:QB], ALU.mult, [ep.r], [ep.r])
                        ot = ot2_ring.next()
                        stt("dve", ot.t[:], ep.t[:, 0, QB:2 * QB], neglam, ep.t[:, 0, 0:QB], ALU.mult, ALU.add,
                            [ep.r, sv.r], [ot.r])
                        dma("sp", OT_s[h, :, i * QB:(i + 1) * QB], ot.t[:], [ot.r], [r_ot[h][i // 2]], ot.r)

                    pending.append((g + 2, rest))
                    if i == NQB - 1 and h + 2 < 4:
                        load_head(h + 2)

            pending = []

            def flush_pending(gnow):
                while pending and pending[0][0] <= gnow:
                    pending.pop(0)[1]()

            memset("dve", ps[:, RB, QB:2 * QB], 0.0, [rb[RB]])
            load_head(0)
            load_head(1)
            NG = len(gsteps)
            issue_S2(0)
            issue_S2(1)
            for g in range(NG):
                if g + 2 < NG:
                    issue_S2(g + 2)
                issue_PV2(g)
                flush_pending(g)
            flush_pending(NG + 10)
            S.barrier()

        with ExitStack() as sc_:
            wout = sbuf(sc_, [128, 8, D], BF16, "wout")
            with ExitStack() as s0:
                wst = Ring(s0, 2, [128, D], F32, "wst2")
                for dc in range(8):
                    b = wst.next()
                    dma("sp", b.t[:], w_out[dc * 128:(dc + 1) * 128, :], [], [b.r], b.r)
                    cp("dve" if dc % 2 == 0 else "pool", wout.t[:, dc, :], b.t[:], [b.r], [wout.r])
                S.barrier()
            o_ring = Ring(sc_, 2, [128, 4, BT], BF16, "o3")
            g_ring = Ring(sc_, 2, [128, 4, BT], BF16, "g3")
            m_ring = Ring(sc_, 3, [128, 4, BT], BF16, "m3")
            oa_ring = Ring(sc_, 3, [128, 4, BT], BF16, "oa3")
            osq_ring = Ring(sc_, 2, [128, BT], BF16, "osq")
            rs_ring = Ring(sc_, 2, [128, 2, BT], F32, "rs3")
            x_ring = Ring(sc_, 3, [128, D], F32, "x3")
            y_ring = Ring(sc_, 3, [128, D], F32, "y3")
            B_ST = (0, 1)
            B_Y = ((2, 3), (4, 5), (6, 7))
            cnt = [0, 0]

            def out_stats(NT, o_b, g_b):
                oa = oa_ring.next()
                for h in range(4):
                    osq = osq_ring.next()
                    tt("pool", osq.t[:, 0:NT], o_b.t[:, h, 0:NT], o_b.t[:, h, 0:NT], ALU.mult, [o_b.r], [osq.r])
                    bk = B_ST[cnt[0] % 2]
                    cnt[0] += 1
                    mm(ps[:, bk, 0:NT], onesm.t[:], osq.t[:, 0:NT], True, True, [onesm.r, osq.r], [rb[bk]])
                    rs = rs_ring.next()
                    act(rs.t[:, 0, 0:NT], ps[:, bk, 0:NT], AF.Ln, [rb[bk], cst.r], [rs.r], bias=eps_t)
                    act(rs.t[:, 1, 0:NT], rs.t[:, 0, 0:NT], AF.Exp, [rs.r], [rs.r], scale=-0.5)
                    tt("dve", rs.t[:, 0, 0:NT], o_b.t[:, h, 0:NT], rs.t[:, 1, 0:NT], ALU.mult, [o_b.r, rs.r], [rs.r])
                    stt("dve", oa.t[:, h, 0:NT], rs.t[:, 0, 0:NT], subg_t, g_b.t[:, h, 0:NT], ALU.mult, ALU.mult,
                        [rs.r, sg.r, g_b.r], [oa.r])
                return oa

            def out_tiles(NT, oa, m_b, x_dram, y_dram):
                for ti in range(NT // 128):
                    xb_ = x_ring.next()
                    dma("sp", xb_.t[:], x_dram[ti * 128:(ti + 1) * 128, :], [], [xb_.r], xb_.r)
                    pair = B_Y[cnt[1] % 3]
                    cnt[1] += 1
                    for half in range(2):
                        bk = pair[half]
                        for dc in range(8):
                            lhs = oa.t[:, dc, ti * 128:(ti + 1) * 128] if dc < 4 else m_b.t[:, dc - 4, ti * 128:(ti + 1) * 128]
                            mm(ps[:, bk, :], lhs, wout.t[:, dc, half * 512:(half + 1) * 512], dc == 0, dc == 7,
                               [oa.r, m_b.r, wout.r], [rb[bk]])
                    yb = y_ring.next()
                    for half in range(2):
                        tt("dve", yb.t[:, half * 512:(half + 1) * 512], ps[:, pair[half], :],
                           xb_.t[:, half * 512:(half + 1) * 512], ALU.add, [rb[pair[half]], xb_.r], [yb.r])
                    dma("act", y_dram[ti * 128:(ti + 1) * 128, :], yb.t[:], [yb.r], [], yb.r)

            def load_blk(bi):
                cols = slice(bi * BT, (bi + 1) * BT)
                o_b = o_ring.next(); g_b = g_ring.next(); m_b = m_ring.next()
                dma("sp", o_b.t[:], OT_s[:, :, cols].rearrange("h p t -> p h t"), [r_ot[h][bi] for h in range(4)],
                    [o_b.r], o_b.r)
                dma("sp", g_b.t[:], G_s[:, :, cols].rearrange("h p t -> p h t"), [r_g_blk[bi]], [g_b.r], g_b.r)
                dma("sp", m_b.t[:], MBC_s[:, :, cols].rearrange("h p t -> p h t"), [r_mbc_blk[bi]], [m_b.r], m_b.r)
                return o_b, g_b, m_b

            oa_s = out_stats(128, oT_smp, gate_smp)
            blk = load_blk(0)
            oa_n = out_stats(BT, blk[0], blk[1])
            out_tiles(128, oa_s, mbc_smp, xs, ys)
            for bi in range(NBLK):
                cur_oa, cur_m = oa_n, blk[2]
                if bi + 1 < NBLK:
                    blk = load_blk(bi + 1)
                    oa_n = out_stats(BT, blk[0], blk[1])
                out_tiles(BT, cur_oa, cur_m, xp[bi * BT:(bi + 1) * BT, :], yp[bi * BT:(bi + 1) * BT, :])

        S.emit()
    return nc


_NC_CACHE = {}


def _rope_table(pos):
    half = 8
    inv = ROPE_THETA ** (-np.arange(0, 16, 2, dtype=np.float32) / np.float32(16))
    ang = pos.astype(np.float32)[:, None] * inv[None, :].astype(np.float32)
    return np.concatenate([np.cos(ang), np.sin(ang)], axis=1).astype(np.float32)


def kernel(x_prompt, x_sample, mem_prompt, cache_diff_k, cache_diff_v, cache_mem_k, cache_mem_v,
           state_lru_conv, state_lru_h, norm_g, w_in, da_q_norm_g, da_k_norm_g, lambda_q1, lambda_k1,
           lambda_q2, lambda_k2, da_subln_g, lru_conv_w, lru_conv_b, lru_w_a, lru_b_a, lru_w_x, lru_b_x,
           lru_lambda, mem_norm_g, w_mem_kv, mx_q_norm_g, mx_k_norm_g, w_out):
    f = lambda a: np.ascontiguousarray(np.asarray(a, dtype=np.float32))
    if "nc" not in _NC_CACHE:
        _NC_CACHE["nc"] = build_nc()
    nc = _NC_CACHE["nc"]
    shared = {
        "w_in": f(w_in[0]), "w_out": f(w_out[0]), "w_mkv": f(w_mem_kv[0]),
        "norm_g": f(norm_g[0]), "mem_norm_g": f(mem_norm_g[0]),
        "qg": f(da_q_norm_g[0]), "kg": f(da_k_norm_g[0]), "mqg": f(mx_q_norm_g[0]), "mkg": f(mx_k_norm_g[0]),
        "lams": f(np.stack([np.asarray(lambda_q1[0]), np.asarray(lambda_k1[0]), np.asarray(lambda_q2[0]),
                            np.asarray(lambda_k2[0])])),
        "subg": f(da_subln_g[0]), "convw": f(lru_conv_w[0]), "convb": f(lru_conv_b[0]),
        "b_a": f(lru_b_a[0]), "b_x": f(lru_b_x[0]), "lrul": f(lru_lambda[0]),
        "w_a": f(lru_w_a[0]), "w_x": f(lru_w_x[0]),
        "ident": np.eye(128, dtype=np.float32),
        "ropep": np.ascontiguousarray(_rope_table(np.arange(T)).reshape(T // 128, 128, 16).transpose(1, 0, 2).reshape(128, -1)),
        "ropes": _rope_table(PAST + (np.arange(128) % 32)),
    }
    xp_ = np.asarray(x_prompt); xs_ = np.asarray(x_sample); mem_ = np.asarray(mem_prompt)
    ck_ = np.asarray(cache_diff_k)[0]; cv_ = np.asarray(cache_diff_v)[0]
    cmk_ = np.asarray(cache_mem_k)[0]; cmv_ = np.asarray(cache_mem_v)[0]
    sc_ = np.asarray(state_lru_conv)[0]; sh_ = np.asarray(state_lru_h)[0]
    in_maps = []
    for c in range(8):
        sl = slice(4 * c, 4 * c + 4)
        m = dict(shared)
        m["xp"] = f(xp_[c]); m["xs"] = f(xs_[sl].reshape(128, D)); m["mem"] = f(mem_[c])
        m["ck"] = f(ck_[sl].reshape(4, PAST, 512)); m["cv"] = f(cv_[sl].reshape(4, PAST, 512))
        m["cmk"] = f(cmk_[sl].reshape(4, 256, 256)); m["cmv"] = f(cmv_[sl].reshape(4, 256, 256))
        m["sconv"] = f(sc_[sl]); m["sh"] = f(sh_[sl])
        in_maps.append(m)
    res = run_bass_kernel_spmd(nc, in_maps, core_ids=list(range(8)))
    R = res.results
    cat = lambda k: np.stack([np.asarray(r[k]) for r in R])
    y_p = cat("yp")
    y_s = cat("ys").reshape(32, 32, D)
    k_p = cat("kp").reshape(1, 8, T, 4, 2, 64)
    v_p = cat("vp").reshape(1, 8, T, 4, 128)
    mk_p = cat("mkp").reshape(1, 8, 256, 4, 64)
    mv_p = cat("mvp").reshape(1, 8, 256, 4, 64)
    c_p = cat("cpo").reshape(1, 8, 3, 256)
    h_p = cat("hpo").reshape(1, 8, 256)
    k_s = cat("kso").reshape(1, 32, 32, 4, 2, 64)
    v_s = cat("vso").reshape(1, 32, 32, 4, 128)
    c_s = cat("cso").reshape(1, 32, 3, 256)
    h_s = cat("hso").reshape(1, 32, 256)
    return (y_p, y_s, k_p, v_p, mk_p, mv_p, c_p, h_p, k_s, v_s, c_s, h_s)
```

```python
import math
from contextlib import ExitStack

import numpy as np
import concourse.bass as bass
import concourse.mybir as mybir
from concourse.bass_utils import run_bass_kernel_spmd

F32 = mybir.dt.float32
BF16 = mybir.dt.bfloat16
ALU = mybir.AluOpType
AF = mybir.ActivationFunctionType
AX = mybir.AxisListType

T = 8192
D = 1024
BT = 512
NBLK = T // BT
PAST = 1024
LAMBDA_INIT = 0.8 - 0.6 * math.exp(0.0)
EPS = 1e-6
ROPE_THETA = 500000.0


class Res:
    __slots__ = ("name", "w", "rs", "sem", "ndma", "excl")

    def __init__(self, name, excl=False):
        self.name = name
        self.w = None
        self.rs = []
        self.sem = None
        self.ndma = 0
        self.excl = excl


class Op:
    __slots__ = ("eng", "fn", "deps", "signal", "kind", "sem", "val")


class Sched:
    ENGS = ("pe", "act", "dve", "pool", "sp")

    def __init__(self, nc, stack):
        self.nc = nc
        self.stack = stack
        self.streams = {e: [] for e in self.ENGS}
        self.esem = {e: stack.enter_context(nc.semaphore("es_" + e)) for e in self.ENGS}
        self.nsem = len(self.ENGS)
        self.last_dma = {}

    def _add(self, eng, fn, reads, writes, kind):
        op = Op()
        op.eng = eng
        op.fn = fn
        op.kind = kind
        op.signal = False
        op.sem = None
        op.val = None
        deps = {}
        for r in reads:
            if r.excl:
                continue
            if r.w is not None:
                deps[id(r.w)] = (r.w, "raw")
        for w in list(writes) + [r for r in reads if r.excl]:
            if w.w is not None:
                deps[id(w.w)] = (w.w, "waw")
            for rd in w.rs:
                if id(rd) not in deps:
                    deps[id(rd)] = (rd, "war")
        op.deps = []
        for p, typ in deps.values():
            if p.kind == "c" and kind == "c" and p.eng == eng:
                if eng == "pe" or typ == "war":
                    continue
            op.deps.append(p)
            if p.kind == "c":
                p.signal = True
        for r in reads:
            if r.excl:
                r.w = op
                r.rs = []
            else:
                r.rs.append(op)
        for w in writes:
            w.w = op
            w.rs = []
        self.streams[eng].append(op)
        return op

    def op(self, eng, fn, reads=(), writes=()):
        return self._add(eng, fn, reads, writes, "c")

    def dma(self, q, out, in_, reads=(), writes=(), semres=None, **kw):
        if semres.sem is None:
            semres.sem = self.stack.enter_context(self.nc.semaphore("ds_%d" % self.nsem))
            self.nsem += 1
        op = self._add(q, lambda e: e.dma_start(out=out, in_=in_, **kw), reads, writes, "d")
        semres.ndma += 1
        op.sem = semres.sem
        op.val = 16 * semres.ndma
        self.last_dma[id(op.sem)] = op
        return op

    def barrier(self):
        lasts = []
        for e in self.ENGS:
            for op in reversed(self.streams[e]):
                if op.kind == "c" and op.fn is not None:
                    lasts.append(op)
                    break
        dmas = list(self.last_dma.values())
        for e in self.ENGS:
            op = Op()
            op.eng = e
            op.fn = None
            op.kind = "c"
            op.signal = False
            op.sem = None
            op.val = None
            op.deps = [p for p in lasts if p.eng != e] + dmas
            for p in op.deps:
                if p.kind == "c":
                    p.signal = True
            self.streams[e].append(op)

    def emit(self):
        nc = self.nc
        for e in self.ENGS:
            cnt = 0
            for op in self.streams[e]:
                if op.kind == "c" and op.signal:
                    cnt += 1
                    op.sem = self.esem[e]
                    op.val = cnt
        last_dma = self.last_dma
        streams = self.streams
        esem = self.esem

        def run(e, eng):
            waited = {}
            for op in streams[e]:
                for p in op.deps:
                    k = id(p.sem)
                    if waited.get(k, 0) >= p.val:
                        continue
                    waited[k] = p.val
                    eng.wait_ge(p.sem, p.val)
                if op.fn is None:
                    continue
                ins = op.fn(eng)
                if op.kind == "d":
                    ins.then_inc(op.sem, 16)
                elif op.signal:
                    ins.then_inc(esem[e], 1)
            if e == "sp":
                for p in last_dma.values():
                    if waited.get(id(p.sem), 0) < p.val:
                        eng.wait_ge(p.sem, p.val)

        with nc.Block() as block:
            @block.tensor
            def _(eng):
                run("pe", eng)

            @block.scalar
            def _(eng):
                run("act", eng)

            @block.vector
            def _(eng):
                run("dve", eng)

            @block.gpsimd
            def _(eng):
                run("pool", eng)

            @block.sync
            def _(eng):
                run("sp", eng)


class Buf:
    __slots__ = ("t", "r")

    def __init__(self, t, r):
        self.t = t
        self.r = r


def build_nc():
    nc = bass.Bass("TRN2", target_bir_lowering=False)

    def din(name, shape, dt=F32):
        return nc.dram_tensor(name, list(shape), dt, kind="ExternalInput").ap()

    def dout(name, shape):
        return nc.dram_tensor(name, list(shape), F32, kind="ExternalOutput").ap()

    def dscr(name, shape, dt=BF16):
        return nc.dram_tensor(name, list(shape), dt).ap()

    xp = din("xp", [T, D]); xs = din("xs", [128, D]); mem = din("mem", [256, D])
    ck = din("ck", [4, PAST, 512]); cv = din("cv", [4, PAST, 512])
    cmk = din("cmk", [4, 256, 256]); cmv = din("cmv", [4, 256, 256])
    sconv = din("sconv", [4, 3, 256]); sh = din("sh", [4, 256])
    w_in = din("w_in", [D, 3072]); w_out = din("w_out", [D, D]); w_mkv = din("w_mkv", [D, 512])
    norm_g = din("norm_g", [D]); mem_norm_g = din("mem_norm_g", [D])
    qg = din("qg", [64]); kg = din("kg", [64]); mqg = din("mqg", [64]); mkg = din("mkg", [64])
    lams = din("lams", [4, 64]); subg = din("subg", [128])
    convw = din("convw", [4, 256]); convb = din("convb", [256]); b_a = din("b_a", [256]); b_x = din("b_x", [256])
    lrul = din("lrul", [256]); w_a = din("w_a", [4, 64, 64]); w_x = din("w_x", [4, 64, 64])
    ident = din("ident", [128, 128]); ropep = din("ropep", [128, (T // 128) * 16]); ropes = din("ropes", [128, 16])

    yp = dout("yp", [T, D]); ys = dout("ys", [128, D])
    kp = dout("kp", [T, 512]); vp = dout("vp", [T, 512])
    mkp = dout("mkp", [256, 256]); mvp = dout("mvp", [256, 256])
    cpo = dout("cpo", [3, 256]); hpo = dout("hpo", [256])
    kso = dout("kso", [128, 512]); vso = dout("vso", [128, 512])
    cso = dout("cso", [4, 3, 256]); hso = dout("hso", [4, 256])

    QT_s = dscr("QT_s", [4, 128, T]); KT_s = dscr("KT_s", [4, 128, T]); V_s = dscr("V_s", [T, 512])
    OT_s = dscr("OT_s", [4, 128, T]); G_s = dscr("G_s", [4, 128, T]); MBC_s = dscr("MBC_s", [4, 128, T])

    gst = ExitStack()
    with gst:
        S = Sched(nc, gst)
        uid = [0]

        def sbuf(st, shape, dt, name=None):
            uid[0] += 1
            nm = "%s_%d" % (name or "b", uid[0])
            return Buf(st.enter_context(nc.sbuf_tensor(nm, list(shape), dt)), Res(nm))

        class Ring:
            def __init__(self, st, n, shape, dt, name):
                self.bufs = [sbuf(st, shape, dt, name) for _ in range(n)]
                self.i = 0

            def next(self):
                b = self.bufs[self.i % len(self.bufs)]
                self.i += 1
                return b

        ps = gst.enter_context(nc.psum_tensor("ps", [128, 8, 512], F32))
        rb = [Res("bank%d" % i, excl=True) for i in range(8)]

        def bank_bf(b):
            return ps[:, b, :].bitcast(BF16).rearrange("p (a c) -> p a c", c=128)

        def mm(out, lhsT, rhs, start, stop, r, w, skip=False):
            if skip:
                S.op("pe", lambda e: e.matmul(out=out, lhsT=lhsT, rhs=rhs, start=start, stop=stop,
                                              skip_group_check=True), r, w)
            else:
                S.op("pe", lambda e: e.matmul(out=out, lhsT=lhsT, rhs=rhs, start=start, stop=stop), r, w)

        def tr(out, in_, r, w):
            S.op("pe", lambda e: e.transpose(out=out, in_=in_, identity=identb.t[:]), list(r) + [identb.r], w)

        def tt(eng, out, in0, in1, op, r, w):
            S.op(eng, lambda e: e.tensor_tensor(out=out, in0=in0, in1=in1, op=op), r, w)

        def ts(eng, out, in0, s1, s2, op0, op1, r, w):
            if s2 is None:
                S.op(eng, lambda e: e.tensor_scalar(out=out, in0=in0, scalar1=s1, scalar2=None, op0=op0), r, w)
            else:
                S.op(eng, lambda e: e.tensor_scalar(out=out, in0=in0, scalar1=s1, scalar2=s2, op0=op0, op1=op1), r, w)

        def stt(eng, out, in0, scalar, in1, op0, op1, r, w, accum_out=None):
            if accum_out is None:
                S.op(eng, lambda e: e.scalar_tensor_tensor(out=out, in0=in0, scalar=scalar, in1=in1, op0=op0, op1=op1), r, w)
            else:
                S.op(eng, lambda e: e.scalar_tensor_tensor(out=out, in0=in0, scalar=scalar, in1=in1, op0=op0, op1=op1,
                                                           accum_out=accum_out), r, w)

        def act(out, in_, func, r, w, bias=None, scale=1.0, accum_out=None):
            if accum_out is not None:
                S.op("act", lambda e: e.activation(out=out, in_=in_, func=func, scale=scale, accum_out=accum_out), r, w)
            elif bias is None:
                S.op("act", lambda e: e.activation(out=out, in_=in_, func=func, scale=scale), r, w)
            else:
                S.op("act", lambda e: e.activation(out=out, in_=in_, func=func, bias=bias, scale=scale), r, w)

        def cp(eng, out, in_, r, w):
            if eng == "act":
                S.op("act", lambda e: e.copy(out=out, in_=in_), r, w)
            else:
                S.op(eng, lambda e: e.tensor_copy(out=out, in_=in_), r, w)

        def memset(eng, ap, val, w):
            S.op(eng, lambda e: e.memset(ap, val), [], w)

        def red(eng, out, in_, op, r, w, absval=False):
            if absval:
                S.op(eng, lambda e: e.tensor_reduce(out=out, in_=in_, axis=AX.X, op=op, apply_absolute_value=True), r, w)
            else:
                S.op(eng, lambda e: e.tensor_reduce(out=out, in_=in_, axis=AX.X, op=op), r, w)

        def recip(out, in_, r, w):
            S.op("dve", lambda e: e.reciprocal(out=out, in_=in_), r, w)

        def scan(out, d0, d1, init, r, w):
            S.op("dve", lambda e: e.tensor_tensor_scan(out=out, data0=d0, data1=d1, initial=init,
                                                       op0=ALU.mult, op1=ALU.add), r, w)

        def dma(q, out, in_, r, w, semres, **kw):
            S.dma(q, out, in_, reads=r, writes=w, semres=semres, **kw)

        nc_ctx = nc.allow_non_contiguous_dma(reason="small strided parameter / state transfers")
        gst.enter_context(nc_ctx)

        identf = sbuf(gst, [128, 128], F32, "identf")
        identb = sbuf(gst, [128, 128], BF16, "identb")
        dma("sp", identf.t[:], ident, [], [identf.r], identf.r)
        cp("dve", identb.t[:], identf.t[:], [identf.r], [identb.r])
        onesb = sbuf(gst, [128, 128], BF16, "onesb")
        memset("pool", onesb.t[:], 1.0, [onesb.r])
        onesm = sbuf(gst, [128, 128], BF16, "onesm")
        memset("pool", onesm.t[:], 1.0 / 128.0, [onesm.r])
        cst = sbuf(gst, [128, 4], F32, "cst")
        memset("pool", cst.t[:, 0:1], EPS, [cst.r])
        memset("pool", cst.t[:, 1:2], 1.0, [cst.r])
        eps_t = cst.t[:, 0:1]
        one_t = cst.t[:, 1:2]

        gv = sbuf(gst, [128, 4, 64], F32, "gv")
        for i, v in enumerate((qg, kg, mqg, mkg)):
            dma("sp", gv.t[:, i, :], v.partition_broadcast(128), [], [gv.r], gv.r)
        gfull = sbuf(gst, [128, 20, 64], F32, "gfull")
        cp("pool", gfull.t[:, 0:8, :], gv.t[:, 0:1, :].to_broadcast([128, 8, 64]), [gv.r], [gfull.r])
        cp("pool", gfull.t[:, 8:16, :], gv.t[:, 1:2, :].to_broadcast([128, 8, 64]), [gv.r], [gfull.r])
        cp("pool", gfull.t[:, 16:20, :], gv.t[:, 2:3, :].to_broadcast([128, 4, 64]), [gv.r], [gfull.r])
        gmk = sbuf(gst, [128, 4, 64], F32, "gmk")
        cp("pool", gmk.t[:, :, :], gv.t[:, 3:4, :].to_broadcast([128, 4, 64]), [gv.r], [gmk.r])
        sv = sbuf(gst, [128, 16], F32, "sv")
        red("dve", sv.t[:, 0:4], gv.t[:, :, :], ALU.max, [gv.r], [sv.r], absval=True)
        tt("dve", sv.t[:, 4:5], sv.t[:, 0:1], sv.t[:, 1:2], ALU.mult, [sv.r], [sv.r])
        tt("dve", sv.t[:, 5:6], sv.t[:, 2:3], sv.t[:, 3:4], ALU.mult, [sv.r], [sv.r])
        ts("dve", sv.t[:, 6:8], sv.t[:, 4:6], -8.0, None, ALU.mult, None, [sv.r], [sv.r])
        nshift_d = sv.t[:, 6:7]
        nshift_m = sv.t[:, 7:8]
        lb = sbuf(gst, [128, 4, 64], F32, "lb")
        dma("sp", lb.t[:, :, :], lams.partition_broadcast(128), [], [lb.r], lb.r)
        lp = sbuf(gst, [128, 2, 64], F32, "lp")
        tt("dve", lp.t[:, 0, :], lb.t[:, 0, :], lb.t[:, 1, :], ALU.mult, [lb.r], [lp.r])
        tt("dve", lp.t[:, 1, :], lb.t[:, 2, :], lb.t[:, 3, :], ALU.mult, [lb.r], [lp.r])
        red("dve", sv.t[:, 8:10], lp.t[:, :, :], ALU.add, [lp.r], [sv.r])
        act(sv.t[:, 10:12], sv.t[:, 8:10], AF.Exp, [sv.r], [sv.r])
        tt("dve", sv.t[:, 12:13], sv.t[:, 11:12], sv.t[:, 10:11], ALU.subtract, [sv.r], [sv.r])
        ts("dve", sv.t[:, 13:14], sv.t[:, 12:13], -LAMBDA_INIT, None, ALU.add, None, [sv.r], [sv.r])
        neglam = sv.t[:, 13:14]
        sg = sbuf(gst, [128, 2], F32, "sg")
        dma("sp", sg.t[:, 0:1], subg.rearrange("(p o) -> p o", o=1), [], [sg.r], sg.r)
        ts("dve", sg.t[:, 1:2], sg.t[:, 0:1], 1.0 - LAMBDA_INIT, None, ALU.mult, None, [sg.r], [sg.r])
        subg_t = sg.t[:, 1:2]
        lv = sbuf(gst, [128, 2, 12], F32, "lv")
        for cc in range(2):
            dma("sp", lv.t[:, cc, 0:4], convw[:, cc * 128:(cc + 1) * 128].rearrange("j p -> p j"), [], [lv.r], lv.r)
            for i, v in enumerate((convb, b_a, b_x, lrul)):
                dma("sp", lv.t[:, cc, 4 + i:5 + i], v[cc * 128:(cc + 1) * 128].rearrange("(p o) -> p o", o=1),
                    [], [lv.r], lv.r)
        act(lv.t[:, :, 8:9], lv.t[:, :, 7:8], AF.Exp, [lv.r], [lv.r], scale=-1.0)
        act(lv.t[:, :, 9:10], lv.t[:, :, 8:9], AF.Ln, [lv.r, cst.r], [lv.r], bias=one_t)
        ts("dve", lv.t[:, :, 10:11], lv.t[:, :, 9:10], -8.0, None, ALU.mult, None, [lv.r], [lv.r])
        wbdf = sbuf(gst, [128, 2, 2, 128], F32, "wbdf")
        memset("pool", wbdf.t[:], 0.0, [wbdf.r])
        for wi, wsrc in enumerate((w_a, w_x)):
            for n in range(4):
                cc, hf = n // 2, n % 2
                dma("sp", wbdf.t[hf * 64:(hf + 1) * 64, wi, cc, hf * 64:(hf + 1) * 64], wsrc[n], [], [wbdf.r], wbdf.r)
        wbd = sbuf(gst, [128, 2, 2, 128], BF16, "wbd")
        cp("dve", wbd.t[:], wbdf.t[:], [wbdf.r], [wbd.r])
        rope_ring = Ring(gst, 4, [128, 16], F32, "rope_t")
        rope_s = sbuf(gst, [128, 16], F32, "rope_s")
        dma("sp", rope_s.t[:], ropes, [], [rope_s.r], rope_s.r)

        qT_smp = sbuf(gst, [128, 4, 128], BF16, "qT_smp")
        kT_smp = sbuf(gst, [128, 4, 128], BF16, "kT_smp")
        qmT_smp = sbuf(gst, [128, 2, 128], BF16, "qmT_smp")
        vb_smp = sbuf(gst, [128, 512], BF16, "vb_smp")
        oT_smp = sbuf(gst, [128, 4, 128], BF16, "oT_smp")
        gate_smp = sbuf(gst, [128, 4, 128], BF16, "gate_smp")
        mbc_smp = sbuf(gst, [128, 4, 128], BF16, "mbc_smp")
        MKT_p = sbuf(gst, [128, 2, 256], BF16, "MKT_p")
        MV_p = sbuf(gst, [128, 2, 256], BF16, "MV_p")
        MKT_s = sbuf(gst, [128, 4, 2, 256], BF16, "MKT_s")
        MV_s = sbuf(gst, [128, 4, 2, 256], BF16, "MV_s")

        r_qk_blk = [Res("qkblk%d" % i) for i in range(NBLK)]
        r_v_blk = [Res("vblk%d" % i) for i in range(NBLK)]
        r_g_blk = [Res("gblk%d" % i) for i in range(NBLK)]
        r_mbc_blk = [Res("mbcblk%d" % i) for i in range(NBLK)]
        r_ot = [[Res("ot%d_%d" % (h, i)) for i in range(NBLK)] for h in range(4)]

        with ExitStack() as sa:
            win = sbuf(sa, [128, 8, 3072], BF16, "win")
            xring = Ring(sa, 4, [128, D], F32, "x")
            xbring = Ring(sa, 2, [128, D], BF16, "xb")
            string = Ring(sa, 4, [128, 4], F32, "st")
            hnT2 = [sbuf(sa, [128, 8, BT], BF16, "hnT") for _ in range(2)]
            r_hn2 = [[Res("hn%d_%d" % (k, i)) for i in range(4)] for k in range(2)]
            qkf_ring = Ring(sa, 4, [128, 1280], F32, "qkf")
            vf_ring = Ring(sa, 4, [128, 512], F32, "vf")
            sq_ring = Ring(sa, 4, [128, 1280], BF16, "sq")
            st20_ring = Ring(sa, 3, [128, 3, 20], F32, "st20")
            qkb_ring = Ring(sa, 4, [128, 1280], BF16, "qkb")
            vb_ring = Ring(sa, 2, [128, 512], BF16, "vb")
            rtmp_ring = Ring(sa, 1, [128, 4, 16, 8], F32, "rtmp")
            qkT_ring = Ring(sa, 2, [128, 8, 128], BF16, "qkT")
            qmT2 = [sbuf(sa, [128, 2, BT], BF16, "qmT") for _ in range(2)]
            ngt = sbuf(sa, [128, 2, 8], F32, "ngt")
            dma("sp", ngt.t[:, 0, :], norm_g.rearrange("(c p) -> p c", p=128), [], [ngt.r], ngt.r)
            dma("sp", ngt.t[:, 1, :], mem_norm_g.rearrange("(c p) -> p c", p=128), [], [ngt.r], ngt.r)

            B_TA, B_TB, B_Q, B_K, B_F0, B_F1, B_MO, B_MR = range(8)

            def norm_a(x_dram):
                xb_ = xring.next()
                dma("sp", xb_.t[:], x_dram, [], [xb_.r], xb_.r)
                return xb_

            def norm_b(xb_):
                st_ = string.next()
                xbf = xbring.next()
                act(xbf.t[:], xb_.t[:], AF.Square, [xb_.r], [st_.r, xbf.r], accum_out=st_.t[:, 0:1])
                act(st_.t[:, 1:2], st_.t[:, 0:1], AF.Ln, [st_.r, cst.r], [st_.r], bias=eps_t, scale=1.0 / D)
                act(st_.t[:, 2:3], st_.t[:, 1:2], AF.Exp, [st_.r], [st_.r], scale=-0.5)
                ts("dve", xbf.t[:], xb_.t[:], st_.t[:, 2:3], None, ALU.mult, None, [xb_.r, st_.r], [xbf.r])
                return xbf

            def norm_c(xbf, hn_dst, rdeps):
                va = bank_bf(B_TA)
                for dc in range(8):
                    tr(va[:, dc, :], xbf.t[:, dc * 128:(dc + 1) * 128], [xbf.r], [rb[B_TA]])
                cp("act", hn_dst, va[:, :, :], [rb[B_TA]], rdeps)

            def gnorm(src, G, gap, rsrc, rg, sq=None):
                s3 = src.rearrange("p (g d) -> p g d", d=64)
                if sq is None:
                    sq = sq_ring.next()
                    tt("pool", sq.t[:, 0:G * 64], src, src, ALU.mult, [rsrc], [sq.r])
                s20 = st20_ring.next()
                red("dve", s20.t[:, 0, 0:G], sq.t[:, 0:G * 64].rearrange("p (g d) -> p g d", d=64), ALU.add,
                    [sq.r], [s20.r])
                act(s20.t[:, 1, 0:G], s20.t[:, 0, 0:G], AF.Ln, [s20.r, cst.r], [s20.r], bias=eps_t, scale=1.0 / 64)
                act(s20.t[:, 2, 0:G], s20.t[:, 1, 0:G], AF.Exp, [s20.r], [s20.r], scale=-0.5)
                tt("dve", s3, s3, s20.t[:, 2, 0:G].unsqueeze(2).to_broadcast([128, G, 64]), ALU.mult,
                   [rsrc, s20.r], [rsrc])
                tt("dve", s3, s3, gap, ALU.mult, [rsrc, rg], [rsrc])

            with ExitStack() as s0:
                wmkv = sbuf(s0, [128, 8, 512], BF16, "wmkv")
                wst = Ring(s0, 2, [128, 1024], F32, "wst")
                for dc in range(8):
                    b = wst.next()
                    dma("sp", b.t[:, 0:512], w_mkv[dc * 128:(dc + 1) * 128, :], [], [b.r], b.r)
                    ts("dve", wmkv.t[:, dc, :], b.t[:, 0:512], ngt.t[:, 1, dc:dc + 1], None, ALU.mult, None,
                       [b.r, ngt.r], [wmkv.r])
                wst2 = Ring(s0, 3, [128, 1024], F32, "wstb")
                win_todo = [(dc, pc) for dc in range(8) for pc in range(3)]

                def stage_win(n):
                    for _ in range(n):
                        if not win_todo:
                            return
                        dc, pc = win_todo.pop(0)
                        b = wst2.next()
                        dma("sp", b.t[:], w_in[dc * 128:(dc + 1) * 128, pc * 1024:(pc + 1) * 1024], [], [b.r], b.r)
                        ts("dve", win.t[:, dc, pc * 1024:(pc + 1) * 1024], b.t[:], ngt.t[:, 0, dc:dc + 1], None,
                           ALU.mult, None, [b.r, ngt.r], [win.r])

                hnT = hnT2[0]
                for mt in range(2):
                    xb_ = norm_a(mem[mt * 128:(mt + 1) * 128, :])
                    xbf = norm_b(xb_)
                    norm_c(xbf, hnT.t[:, :, 0:128], [r_hn2[0][0]])
                    for dc in range(8):
                        mm(ps[:, B_Q, :], hnT.t[:, dc, 0:128], wmkv.t[:, dc, :], dc == 0, dc == 7,
                           [r_hn2[0][0], wmkv.r], [rb[B_Q]])
                    kvf = vf_ring.next()
                    cp("act", kvf.t[:], ps[:, B_Q, :], [rb[B_Q]], [kvf.r])
                    dma("act", mvp[mt * 128:(mt + 1) * 128, :], kvf.t[:, 256:512], [kvf.r], [], kvf.r)
                    cp("dve", MV_p.t[:, mt, :], kvf.t[:, 256:512], [kvf.r], [MV_p.r])
                    gnorm(kvf.t[:, 0:256], 4, gmk.t[:, :, :], kvf.r, gmk.r)
                    dma("act", mkp[mt * 128:(mt + 1) * 128, :], kvf.t[:, 0:256], [kvf.r], [], kvf.r)
                    kb = qkb_ring.next()
                    cp("dve", kb.t[:, 0:256], kvf.t[:, 0:256], [kvf.r], [kb.r])
                    vb_ = bank_bf(B_TB)
                    for hp_ in range(2):
                        tr(vb_[:, hp_, :], kb.t[:, hp_ * 128:(hp_ + 1) * 128], [kb.r], [rb[B_TB]])
                    cp("act", MKT_p.t[:, :, mt * 128:(mt + 1) * 128], vb_[:, 0:2, :], [rb[B_TB]], [MKT_p.r])
                    stage_win(3)
                for b in range(4):
                    for mt in range(2):
                        cf = vf_ring.next()
                        dma("sp", cf.t[:, 0:256], cmk[b, mt * 128:(mt + 1) * 128, :], [], [cf.r], cf.r)
                        dma("sp", cf.t[:, 256:512], cmv[b, mt * 128:(mt + 1) * 128, :], [], [cf.r], cf.r)
                        kb = qkb_ring.next()
                        cp("dve", kb.t[:, 0:256], cf.t[:, 0:256], [cf.r], [kb.r])
                        cp("pool", MV_s.t[:, b, mt, :], cf.t[:, 256:512], [cf.r], [MV_s.r])
                        vb_ = bank_bf(B_TB)
                        for hp_ in range(2):
                            tr(vb_[:, hp_, :], kb.t[:, hp_ * 128:(hp_ + 1) * 128], [kb.r], [rb[B_TB]])
                        cp("act", MKT_s.t[:, b, :, mt * 128:(mt + 1) * 128], vb_[:, 0:2, :], [rb[B_TB]], [MKT_s.r])
                        stage_win(2)
                stage_win(24)
                S.barrier()

            def tile_gen(x_dram, k2, col0, rope_ap, rrope, kout, vout, qmT_dst, r_qm, prompt_tg):
                hn = hnT2[k2]
                rh = r_hn2[k2][col0 // 128]
                xb_ = norm_a(x_dram)
                yield
                xbf = norm_b(xb_)
                yield
                norm_c(xbf, hn.t[:, :, col0:col0 + 128], [rh])
                yield
                qkf = qkf_ring.next()
                vf = vf_ring.next()
                for (bk, c0) in ((B_Q, 0), (B_K, 512)):
                    for dc in range(8):
                        mm(ps[:, bk, :], hn.t[:, dc, col0:col0 + 128], win.t[:, dc, c0:c0 + 512], dc == 0, dc == 7,
                           [rh, win.r], [rb[bk]])
                cp("act", qkf.t[:, 0:1024].rearrange("p (a c) -> p a c", c=512), ps[:, B_Q:B_K + 1, :],
                   [rb[B_Q], rb[B_K]], [qkf.r])
                sq = sq_ring.next()
                act(sq.t[:, 0:1024].rearrange("p (a c) -> p a c", c=512), ps[:, B_Q:B_K + 1, :], AF.Square,
                    [rb[B_Q], rb[B_K]], [sq.r])
                yield
                for (bk, c0, n) in ((B_Q, 1024, 512), (B_K, 2560, 256)):
                    for dc in range(8):
                        mm(ps[:, bk, 0:n], hn.t[:, dc, col0:col0 + 128], win.t[:, dc, c0:c0 + n], dc == 0, dc == 7,
                           [rh, win.r], [rb[bk]])
                cp("act", vf.t[:], ps[:, B_Q, :], [rb[B_Q]], [vf.r])
                cp("act", qkf.t[:, 1024:1280], ps[:, B_K, 0:256], [rb[B_K]], [qkf.r])
                act(sq.t[:, 1024:1280], ps[:, B_K, 0:256], AF.Square, [rb[B_K]], [sq.r])
                dma("sp", vout, vf.t[:], [vf.r], [], vf.r)
                if prompt_tg is None:
                    cp("dve", vb_smp.t[:], vf.t[:], [vf.r], [vb_smp.r])
                else:
                    vbb = vb_ring.next()
                    cp("dve", vbb.t[:], vf.t[:], [vf.r], [vbb.r])
                    dma("sp", V_s[prompt_tg * 128:(prompt_tg + 1) * 128, :], vbb.t[:], [vbb.r],
                        [r_v_blk[prompt_tg // 4]], vbb.r)
                yield
                gnorm(qkf.t[:, 0:1280], 20, gfull.t[:, :, :], qkf.r, gfull.r, sq)
                yield
                q3 = qkf.t[:, 0:1024].rearrange("p (g d) -> p g d", d=64)
                x1 = q3[:, :, 0:8]
                x2 = q3[:, :, 8:16]
                cosb = rope_ap[:, 0:8].unsqueeze(1).to_broadcast([128, 16, 8])
                sinb = rope_ap[:, 8:16].unsqueeze(1).to_broadcast([128, 16, 8])
                rtmp = rtmp_ring.next()
                tt("pool", rtmp.t[:, 0], x1, cosb, ALU.mult, [qkf.r, rrope], [rtmp.r])
                tt("pool", rtmp.t[:, 1], x2, sinb, ALU.mult, [qkf.r, rrope], [rtmp.r])
                tt("pool", rtmp.t[:, 2], x2, cosb, ALU.mult, [qkf.r, rrope], [rtmp.r])
                tt("pool", rtmp.t[:, 3], x1, sinb, ALU.mult, [qkf.r, rrope], [rtmp.r])
                tt("pool", x1, rtmp.t[:, 0], rtmp.t[:, 1], ALU.subtract, [rtmp.r], [qkf.r])
                tt("pool", x2, rtmp.t[:, 2], rtmp.t[:, 3], ALU.add, [rtmp.r], [qkf.r])
                dma("sp", kout, qkf.t[:, 512:1024], [qkf.r], [], qkf.r)
                yield
                qkb = qkb_ring.next()
                cp("dve", qkb.t[:], qkf.t[:], [qkf.r], [qkb.r])
                yield
                vB = bank_bf(B_TB)
                for g in range(8):
                    tr(vB[:, g, :], qkb.t[:, g * 128:(g + 1) * 128], [qkb.r], [rb[B_TB]])
                if prompt_tg is None:
                    cp("act", qT_smp.t[:], vB[:, 0:4, :], [rb[B_TB]], [qT_smp.r])
                    cp("act", kT_smp.t[:], vB[:, 4:8, :], [rb[B_TB]], [kT_smp.r])
                else:
                    qkT = qkT_ring.next()
                    cp("act", qkT.t[:], vB[:, :, :], [rb[B_TB]], [qkT.r])
                    cols = slice(prompt_tg * 128, (prompt_tg + 1) * 128)
                    dma("sp", QT_s[:, :, cols].rearrange("h p t -> p h t"), qkT.t[:, 0:4, :], [qkT.r],
                        [r_qk_blk[prompt_tg // 4]], qkT.r)
                    dma("sp", KT_s[:, :, cols].rearrange("h p t -> p h t"), qkT.t[:, 4:8, :], [qkT.r],
                        [r_qk_blk[prompt_tg // 4]], qkT.r)
                vA = bank_bf(B_TA)
                for g in range(2):
                    tr(vA[:, g, :], qkb.t[:, 1024 + g * 128:1024 + (g + 1) * 128], [qkb.r], [rb[B_TA]])
                cp("act", qmT_dst, vA[:, 0:2, :], [rb[B_TA]], [r_qm])

            lx_p = sbuf(sa, [128, 2, 1, 3 + BT], F32, "lx_p")
            lx_s = sbuf(sa, [128, 2, 4, 3 + 32], F32, "lx_s")
            r_lx_p = [Res("lxp0"), Res("lxp1")]
            r_lx_s = [Res("lxs0"), Res("lxs1")]
            hprev = sbuf(sa, [128, 2], F32, "hprev")
            r_hprev = [Res("hp0"), Res("hp1")]
            h0s = sbuf(sa, [128, 2, 4], F32, "h0s")
            memset("pool", lx_p.t[:, :, :, 0:3], 0.0, r_lx_p)
            memset("pool", hprev.t[:], 0.0, r_hprev)
            for cc in range(2):
                for sb_i in range(4):
                    dma("sp", lx_s.t[:, cc, sb_i, 0:3], sconv[sb_i, :, cc * 128:(cc + 1) * 128].rearrange("j p -> p j"),
                        [], [r_lx_s[cc]], r_lx_s[cc])
                dma("sp", h0s.t[:, cc, :], sh[:, cc * 128:(cc + 1) * 128].rearrange("b p -> p b"), [], [h0s.r], h0s.r)
            fring = Ring(sa, 1, [128, 5, BT], F32, "fw")
            hl_s = sbuf(sa, [128, 2, 4], F32, "hl_s")
            xcb_ring = Ring(sa, 1, [128, BT], BF16, "xcb")
            slg_ring = Ring(sa, 1, [128, 2, BT], BF16, "slg")
            smg_ring = Ring(sa, 1, [128, 2, BT], BF16, "smg")
            gate_ring = Ring(sa, 1, [128, 4, BT], BF16, "gate")
            mbc_ring = Ring(sa, 1, [128, 4, BT], BF16, "mbc")
            e_ring = Ring(sa, 2, [128, BT], BF16, "eT")
            mo_ring = Ring(sa, 1, [128, 1, BT], F32, "mo")
            fbank = [0]

            def next_fbank():
                fbank[0] += 1
                return B_F0 if fbank[0] % 2 else B_F1

            def feat_gen(part, NT, k2, nseg, L, lxb, r_lx, hinit_fn, gate_b, mbc_b, r_mbcB, qmT_ap, rqm, memgroups,
                         last, outs):
                hn = hnT2[k2]
                rhs_ = r_hn2[k2][0:max(1, NT // 128)]

                def proj(col, bk=None):
                    if bk is None:
                        bk = next_fbank()
                    for dc in range(8):
                        mm(ps[:, bk, 0:NT], win.t[:, dc, col:col + 128], hn.t[:, dc, 0:NT], dc == 0, dc == 7,
                           [win.r] + rhs_, [rb[bk]])
                    return bk

                if part == "A":
                    smg = smg_ring.next()
                    for i in range(4):
                        bk = proj(1536 + i * 128)
                        act(gate_b.t[:, i, 0:NT], ps[:, bk, 0:NT], AF.Silu, [rb[bk]], [gate_b.r])
                    for hp_ in range(2):
                        bk = proj(2816 + hp_ * 128)
                        act(smg.t[:, hp_, 0:NT], ps[:, bk, 0:NT], AF.Silu, [rb[bk]], [smg.r])
                    yield
                    for _ in mem_part(NT, mbc_b, smg, qmT_ap, rqm, memgroups):
                        yield
                    return
                slg = slg_ring.next()
                for cc in range(2):
                    proj(2048 + cc * 128, B_TA)
                    cp("act", lxb.t[:, cc, :, 3:3 + L], ps[:, B_TA, 0:NT].rearrange("p (s l) -> p s l", l=L),
                       [rb[B_TA]], [r_lx[cc]])
                    yield
                for cc in range(2):
                    proj(2304 + cc * 128, B_TB)
                    act(slg.t[:, cc, 0:NT], ps[:, B_TB, 0:NT], AF.Silu, [rb[B_TB]], [slg.r])
                    yield
                for cc in range(2):
                    fw = fring.next()
                    xc = fw.t[:, 0, 0:NT]
                    xc3 = xc.rearrange("p (s l) -> p s l", l=L)
                    ts("pool", xc3, lxb.t[:, cc, :, 0:L], lv.t[:, cc, 0:1], lv.t[:, cc, 4:5], ALU.mult, ALU.add,
                       [r_lx[cc], lv.r], [fw.r])
                    for j in range(1, 4):
                        stt("dve", xc3, lxb.t[:, cc, :, j:j + L], lv.t[:, cc, j:j + 1], xc3, ALU.mult, ALU.add,
                            [r_lx[cc], lv.r, fw.r], [fw.r])
                    xcb = xcb_ring.next()
                    cp("dve", xcb.t[:, 0:NT], xc, [fw.r], [xcb.r])
                    yield
                    bka, bkx = B_TA, B_TB
                    mm(ps[:, bka, 0:NT], wbd.t[:, 0, cc, :], xcb.t[:, 0:NT], True, True, [wbd.r, xcb.r], [rb[bka]])
                    mm(ps[:, bkx, 0:NT], wbd.t[:, 1, cc, :], xcb.t[:, 0:NT], True, True, [wbd.r, xcb.r], [rb[bkx]])
                    act(fw.t[:, 1, 0:NT], ps[:, bka, 0:NT], AF.Sigmoid, [rb[bka], lv.r], [fw.r], bias=lv.t[:, cc, 5:6])
                    act(fw.t[:, 2, 0:NT], ps[:, bkx, 0:NT], AF.Sigmoid, [rb[bkx], lv.r], [fw.r], bias=lv.t[:, cc, 6:7])
                    yield
                    a_ = fw.t[:, 3, 0:NT]
                    tmp = fw.t[:, 4, 0:NT]
                    h_ = fw.t[:, 1, 0:NT]
                    act(a_, fw.t[:, 1, 0:NT], AF.Exp, [fw.r, lv.r], [fw.r], scale=lv.t[:, cc, 10:11])
                    tt("pool", tmp, a_, a_, ALU.mult, [fw.r], [fw.r])
                    act(tmp, tmp, AF.Ln, [fw.r, cst.r], [fw.r], bias=one_t, scale=-1.0)
                    act(tmp, tmp, AF.Exp, [fw.r], [fw.r], scale=0.5)
                    tt("dve", fw.t[:, 2, 0:NT], fw.t[:, 2, 0:NT], fw.t[:, 0, 0:NT], ALU.mult, [fw.r], [fw.r])
                    tt("dve", tmp, tmp, fw.t[:, 2, 0:NT], ALU.mult, [fw.r], [fw.r])
                    yield
                    for s_ in range(nseg):
                        scan(h_[:, s_ * L:(s_ + 1) * L], a_[:, s_ * L:(s_ + 1) * L], tmp[:, s_ * L:(s_ + 1) * L],
                             hinit_fn(cc, s_), [fw.r, r_hprev[cc], h0s.r], [fw.r])
                    tt("dve", mbc_b.t[:, cc, 0:NT], h_, slg.t[:, cc, 0:NT], ALU.mult, [fw.r, slg.r], [r_mbcB])
                    if nseg == 1:
                        cp("pool", hprev.t[:, cc:cc + 1], h_[:, L - 1:L], [fw.r], [r_hprev[cc]])
                        if last:
                            dma("sp", outs["conv"][:, cc * 128:(cc + 1) * 128].rearrange("j p -> p j"),
                                lxb.t[:, cc, 0, L:L + 3], [r_lx[cc]], [], r_lx[cc])
                            dma("sp", outs["h"][cc * 128:(cc + 1) * 128].rearrange("(p o) -> p o", o=1),
                                hprev.t[:, cc:cc + 1], [r_hprev[cc]], [], r_hprev[cc])
                        else:
                            cp("pool", lxb.t[:, cc, 0, 0:3], lxb.t[:, cc, 0, L:L + 3], [r_lx[cc]], [r_lx[cc]])
                    else:
                        for sb_i in range(nseg):
                            dma("sp", outs["conv"][sb_i, :, cc * 128:(cc + 1) * 128].rearrange("j p -> p j"),
                                lxb.t[:, cc, sb_i, L:L + 3], [r_lx[cc]], [], r_lx[cc])
                        hlast = hl_s.t[:, cc, 0:nseg]
                        cp("pool", hlast, h_.rearrange("p (s l) -> p s l", l=L)[:, :, L - 1], [fw.r], [hl_s.r])
                        dma("sp", outs["h"][:, cc * 128:(cc + 1) * 128].rearrange("b p -> p b"), hlast, [hl_s.r], [],
                            hl_s.r)
                    yield
            def mem_part(NT, mbc_b, smg, qmT_ap, rqm, memgroups):
                for hp_ in range(2):
                    mo = mo_ring.next()
                    its = []
                    for (c0, n, mkt_fn, mv_fn, rmk, rmv) in memgroups:
                        subs = [(c0 + k_ * 256, 256) for k_ in range(n // 256)] if n > 256 else [(c0, n)]
                        for (cc0, nn) in subs:
                            for hh in range(2):
                                its.append((cc0, nn, hh, mkt_fn, mv_fn, rmk, rmv))
                    ebuf = {}

                    def mS(k_, hp_=hp_, its=its, ebuf=ebuf):
                        cc0, nn, hh, mkt_fn, mv_fn, rmk, rmv = its[k_]
                        bk = next_fbank()
                        for mt in range(2):
                            mm(ps[:, bk, mt * nn:(mt + 1) * nn], mkt_fn(hp_, mt)[hh * 64:(hh + 1) * 64, :],
                               qmT_ap[hh * 64:(hh + 1) * 64, hp_, cc0:cc0 + nn], True, True, [rmk, rqm], [rb[bk]])
                        eb = e_ring.next()
                        act(eb.t[:, 0:2 * nn], ps[:, bk, 0:2 * nn], AF.Exp, [rb[bk], sv.r], [eb.r], bias=nshift_m,
                            scale=0.125)
                        ebuf[k_] = eb

                    def mPV(k_, hp_=hp_, its=its, ebuf=ebuf):
                        cc0, nn, hh, mkt_fn, mv_fn, rmk, rmv = its[k_]
                        h = 2 * hp_ + hh
                        eb = ebuf.pop(k_)
                        for mt in range(2):
                            mm(ps[hh * 64:(hh + 1) * 64, B_MO, cc0:cc0 + nn], mv_fn(mt)[:, h * 64:(h + 1) * 64],
                               eb.t[:, mt * nn:(mt + 1) * nn], mt == 0, mt == 1, [rmv, eb.r], [rb[B_MO]])
                        for mt in range(2):
                            mm(ps[hh * 64:(hh + 1) * 64, B_MR, cc0:cc0 + nn], onesb.t[:, 0:64],
                               eb.t[:, mt * nn:(mt + 1) * nn], mt == 0, mt == 1, [onesb.r, eb.r], [rb[B_MR]])

                    mS(0)
                    for k_ in range(len(its)):
                        if k_ + 1 < len(its):
                            mS(k_ + 1)
                        mPV(k_)
                        yield
                    recip(mo.t[:, 0, 0:NT], ps[:, B_MR, 0:NT], [rb[B_MR]], [mo.r])
                    tt("dve", mo.t[:, 0, 0:NT], ps[:, B_MO, 0:NT], mo.t[:, 0, 0:NT], ALU.mult, [rb[B_MO], mo.r], [mo.r])
                    tt("dve", mbc_b.t[:, 2 + hp_, 0:NT], mo.t[:, 0, 0:NT], smg.t[:, hp_, 0:NT], ALU.mult,
                       [mo.r, smg.r], [mbc_b.r])
                    yield

            tiles = []
            for bi in range(NBLK):
                for ti in range(4):
                    tiles.append((bi, ti))
            tiles.append((NBLK, 0))
            tile_i = [0]
            active = []
            tiles_left = {bi: 4 for bi in range(NBLK)}
            tiles_left[NBLK] = 1
            feat_q = []
            feat_done = set()
            cur_feat = [None]
            FEAT_STEPS = 1

            def make_tile(bi, ti):
                k2 = bi % 2
                if bi < NBLK:
                    tg = bi * 4 + ti
                    rp_ = rope_ring.next()
                    dma("sp", rp_.t[:], ropep[:, tg * 16:(tg + 1) * 16], [], [rp_.r], rp_.r)
                    return tile_gen(xp[tg * 128:(tg + 1) * 128, :], k2, ti * 128, rp_.t[:, :], rp_.r,
                                    kp[tg * 128:(tg + 1) * 128, :], vp[tg * 128:(tg + 1) * 128, :],
                                    qmT2[k2].t[:, :, ti * 128:(ti + 1) * 128], qmT2[k2].r, tg)
                return tile_gen(xs, k2, 0, rope_s.t[:, :], rope_s.r, kso, vso, qmT2[k2].t[:, :, 0:128], qmT2[k2].r, None)

            def make_feat(bi):
                k2 = bi % 2
                r_mbcB = Res("mbcB%d" % bi)
                if bi < NBLK:
                    gate_b = gate_ring.next(); mbc_b = mbc_ring.next()
                    cols = slice(bi * BT, (bi + 1) * BT)

                    def post(gate_b=gate_b, mbc_b=mbc_b, cols=cols, bi=bi, r_mbcB=r_mbcB):
                        dma("sp", G_s[:, :, cols].rearrange("h p t -> p h t"), gate_b.t[:], [gate_b.r], [r_g_blk[bi]],
                            gate_b.r)
                        dma("sp", MBC_s[:, :, cols].rearrange("h p t -> p h t"), mbc_b.t[:], [mbc_b.r, r_mbcB],
                            [r_mbc_blk[bi]], mbc_b.r)

                    args = (BT, k2, 1, BT, lx_p, r_lx_p, lambda cc, s_: hprev.t[:, cc:cc + 1], gate_b, mbc_b, r_mbcB,
                            qmT2[k2].t, qmT2[k2].r,
                            [(0, BT, lambda hp_, mt: MKT_p.t[:, hp_, mt * 128:(mt + 1) * 128],
                              lambda mt: MV_p.t[:, mt, :], MKT_p.r, MV_p.r)],
                            bi == NBLK - 1, {"conv": cpo, "h": hpo})
                    return [bi, feat_gen("A", *args), feat_gen("B", *args), post, mbc_b, r_mbcB]

                def mk_grp(b):
                    return (32 * b, 32, lambda hp_, mt: MKT_s.t[:, b, hp_, mt * 128:(mt + 1) * 128],
                            lambda mt: MV_s.t[:, b, mt, :], MKT_s.r, MV_s.r)

                args = (128, k2, 4, 32, lx_s, r_lx_s, lambda cc, s_: h0s.t[:, cc, s_:s_ + 1], gate_smp, mbc_smp, r_mbcB,
                        qmT2[k2].t, qmT2[k2].r, [mk_grp(b) for b in range(4)], True, {"conv": cso, "h": hso})
                return [bi, feat_gen("A", *args), feat_gen("B", *args), None, mbc_smp, r_mbcB]

            while tile_i[0] < len(tiles) or active or feat_q or cur_feat[0] is not None:
                while len(active) < 4 and tile_i[0] < len(tiles):
                    bi, ti = tiles[tile_i[0]]
                    if bi >= 2 and (bi - 2) not in feat_done:
                        break
                    if active and min(a_[2][0] for a_ in active) < 2:
                        break
                    active.append((bi, make_tile(bi, ti), [0]))
                    tile_i[0] += 1
                for item in list(active):
                    bi, g, cnt_ = item
                    cnt_[0] += 1
                    try:
                        next(g)
                    except StopIteration:
                        active.remove(item)
                        tiles_left[bi] -= 1
                        if tiles_left[bi] == 0:
                            feat_q.append(bi)
                if cur_feat[0] is None and feat_q:
                    cur_feat[0] = make_feat(feat_q.pop(0))
                if cur_feat[0] is not None:
                    cf = cur_feat[0]
                    for gi in (1, 2):
                        if cf[gi] is not None:
                            try:
                                next(cf[gi])
                            except StopIteration:
                                cf[gi] = None
                    if cf[1] is None and cf[2] is None:
                        if cf[3] is not None:
                            cf[3]()
                        feat_done.add(cf[0])
                        cur_feat[0] = None
            S.barrier()

        with ExitStack() as sb_:
            KT_ring = Ring(sb_, 2, [128, T], BF16, "KTh")
            V_ring = Ring(sb_, 2, [128, T // 128, 128], BF16, "Vh")
            QT_ring = Ring(sb_, 4, [128, BT], BF16, "QTb")
            E_ring = Ring(sb_, 3, [128, 2, 32], BF16, "E")
            ep_ring = Ring(sb_, 2, [128, 4, 32], F32, "ep")
            ot_ring = Ring(sb_, 2, [128, 32], BF16, "ot")
            SB = [(0, 1), (2, 3)]
            B_O = (4, 5)
            B_R = (6, 7)

            acc = sbuf(sb_, [128, 2, 32], F32, "acc")
            r_acc = [Res("acc0"), Res("acc1")]
            hl_ring = Ring(sb_, 2, [128, 2, 2, 32], BF16, "hl")
            r_hl = {}

            def attn_unit(steps, kt_fn, v_fn, qt_fn, rk, rv, rq, NQ, out_ap, rout):
                n = len(steps)
                ebs = {}
                ceng = ("dve", "pool")

                def issue_S(j):
                    ksz, q0, _, _ = steps[j]
                    pair = SB[j % 2]
                    for c in range(2):
                        mm(ps[0:ksz, pair[c], q0:NQ], kt_fn(j, c), qt_fn(c)[:, q0:NQ], True, True,
                           [rk, rq], [rb[pair[c]]])
                    eb = E_ring.next()
                    act(eb.t[0:ksz, :, q0:NQ], ps[0:ksz, pair[0]:pair[1] + 1, q0:NQ], AF.Exp,
                        [rb[pair[0]], rb[pair[1]], sv.r], [eb.r], bias=nshift_d[0:ksz, :], scale=0.125)
                    ebs[j] = eb

                def issue_PV(j):
                    ksz, q0, mq0, pm = steps[j]
                    eb = ebs.pop(j)
                    if mq0 is not None:
                        memset("pool", eb.t[64:128, :, mq0:mq0 + 64], 0.0, [eb.r])
                    for (p0, p1) in pm:
                        memset("pool", eb.t[p0:p1, :, q0:NQ], 0.0, [eb.r])
                    for c in range(2):
                        mm(ps[:, B_O[c], q0:NQ], v_fn(j), eb.t[0:ksz, c, q0:NQ], j == 0, j == n - 1,
                           [rv, eb.r], [rb[B_O[c]]])
                    if j == 0:
                        cp("dve", acc.t[:, 0, 0:NQ], eb.t[:, 0, 0:NQ], [eb.r], [r_acc[0]])
                    else:
                        tt("dve", acc.t[:, 0, q0:NQ], acc.t[:, 0, q0:NQ], eb.t[:, 0, q0:NQ], ALU.add,
                           [eb.r, r_acc[0]], [r_acc[0]])
                    mm(ps[:, B_R[1], q0:NQ], onesb.t[0:ksz, :], eb.t[0:ksz, 1, q0:NQ], j == 0, j == n - 1,
                       [onesb.r, eb.r], [rb[B_R[1]]])

                issue_S(0)
                for j in range(n):
                    if j + 1 < n:
                        issue_S(j + 1)
                    issue_PV(j)
                hl = hl_ring.next()
                if id(hl) not in r_hl:
                    r_hl[id(hl)] = [Res("hl0"), Res("hl1")]
                rh = r_hl[id(hl)]
                for c in range(1):
                    cp(ceng[c], hl.t[:, c, 0, 0:NQ], acc.t[:, c, 0:NQ], [r_acc[c]], [rh[c]])
                    tt(ceng[c], hl.t[:, c, 1, 0:NQ], acc.t[:, c, 0:NQ], hl.t[:, c, 0, 0:NQ], ALU.subtract,
                       [r_acc[c], rh[c]], [rh[c]])
                    mm(ps[:, B_R[c], 0:NQ], onesb.t[:], hl.t[:, c, 0, 0:NQ], True, False, [onesb.r, rh[c]], [rb[B_R[c]]])
                    mm(ps[:, B_R[c], 0:NQ], onesb.t[:], hl.t[:, c, 1, 0:NQ], False, True, [onesb.r, rh[c]], [rb[B_R[c]]])
                ep = ep_ring.next()
                cp("dve", ep.t[:, 0, 0:NQ], ps[:, B_O[0], 0:NQ], [rb[B_O[0]]], [ep.r])
                cp("dve", ep.t[:, 1, 0:NQ], ps[:, B_O[1], 0:NQ], [rb[B_O[1]]], [ep.r])
                cp("dve", ep.t[:, 2, 0:NQ], ps[:, B_R[1], 0:NQ], [rb[B_R[1]]], [ep.r])
                recip(ep.t[:, 3, 0:NQ], ps[:, B_R[0], 0:NQ], [rb[B_R[0]]], [ep.r])
                recip(ep.t[:, 2, 0:NQ], ep.t[:, 2, 0:NQ], [ep.r], [ep.r])
                tt("dve", ep.t[:, 0, 0:NQ], ep.t[:, 0, 0:NQ], ep.t[:, 3, 0:NQ], ALU.mult, [ep.r], [ep.r])
                tt("dve", ep.t[:, 1, 0:NQ], ep.t[:, 1, 0:NQ], ep.t[:, 2, 0:NQ], ALU.mult, [ep.r], [ep.r])
                stt("dve", out_ap, ep.t[:, 1, 0:NQ], neglam, ep.t[:, 0, 0:NQ], ALU.mult, ALU.add, [ep.r, sv.r], [rout])

            ckst = Ring(sb_, 3, [128, 512], F32, "ckst")
            ckb_ring = Ring(sb_, 2, [128, 512], BF16, "ckb")
            KTs_ring = Ring(sb_, 2, [128, 4, PAST], BF16, "KTs")
            Vs_ring = Ring(sb_, 2, [128, PAST // 128, 512], BF16, "Vs")
            for b in range(4):
                KTs = KTs_ring.next()
                Vs = Vs_ring.next()
                for kt in range(PAST // 128):
                    cf = ckst.next()
                    dma("sp", cf.t[:], ck[b, kt * 128:(kt + 1) * 128, :], [], [cf.r], cf.r)
                    cb = ckb_ring.next()
                    cp("dve", cb.t[:], cf.t[:], [cf.r], [cb.r])
                    bk = SB[kt % 2][0]
                    vw = bank_bf(bk)
                    for h in range(4):
                        tr(vw[:, h, :], cb.t[:, h * 128:(h + 1) * 128], [cb.r], [rb[bk]])
                    cp("dve", KTs.t[:, :, kt * 128:(kt + 1) * 128], vw[:, 0:4, :], [rb[bk]], [KTs.r])
                    cf2 = ckst.next()
                    dma("sp", cf2.t[:], cv[b, kt * 128:(kt + 1) * 128, :], [], [cf2.r], cf2.r)
                    cp("pool", Vs.t[:, kt, :], cf2.t[:], [cf2.r], [Vs.r])
                for h in range(4):
                    steps = [(128, 0, None, []) for _ in range(PAST // 128)]
                    pm = [(32 * qd, 32 * qd + 32) for qd in range(4) if qd != b]
                    steps.append((128, 0, None, pm))

                    def kt_fn(j, c, KTs=KTs, h=h):
                        if j < PAST // 128:
                            return KTs.t[c * 64:(c + 1) * 64, h, j * 128:(j + 1) * 128]
                        return kT_smp.t[c * 64:(c + 1) * 64, h, :]

                    def v_fn(j, Vs=Vs, h=h):
                        if j < PAST // 128:
                            return Vs.t[:, j, h * 128:(h + 1) * 128]
                        return vb_smp.t[:, h * 128:(h + 1) * 128]

                    def qt_fn(c, h=h, b=b):
                        return qT_smp.t[c * 64:(c + 1) * 64, h, 32 * b:32 * b + 32]

                    attn_unit(steps, kt_fn, v_fn, qt_fn, KTs.r, Vs.r, qT_smp.r, 32,
                              oT_smp.t[:, h, 32 * b:32 * b + 32], oT_smp.r)

            QB = 256
            NQB = T // QB
            SP3 = [(0, 1), (2, 3), (4, 5)]
            OB = 6
            RB = 7
            E4_ring = Ring(sb_, 8, [128, 2, 2, QB], BF16, "E4")
            acc2 = sbuf(sb_, [128, 2, QB], F32, "acc2")
            hl2_ring = Ring(sb_, 2, [128, 2, 2, QB], BF16, "hl2")
            ep2_ring = Ring(sb_, 2, [128, 2, 2 * QB], F32, "ep2")
            ot2_ring = Ring(sb_, 3, [128, QB], BF16, "ot2")
            heads = {}

            def load_head(h):
                KTh = KT_ring.next(); Vh = V_ring.next()
                dma("sp", KTh.t[:], KT_s[h], r_qk_blk, [KTh.r], KTh.r)
                for part in range(4):
                    dma("sp", Vh.t[:, part * 16:(part + 1) * 16, :],
                        V_s[part * 2048:(part + 1) * 2048, h * 128:(h + 1) * 128].rearrange("(t p) e -> p t e", p=128),
                        r_v_blk, [Vh.r], Vh.r)
                heads[h] = (KTh, Vh)

            gsteps = []
            for h in range(4):
                for i in range(NQB):
                    for p in range(i + 1):
                        gsteps.append((h, i, p))
            qblk = {}

            def ensure_q(h, i):
                key = (h, i // 2)
                if key not in qblk:
                    qb = QT_ring.next()
                    dma("sp", qb.t[:], QT_s[h, :, (i // 2) * BT:(i // 2 + 1) * BT], [r_qk_blk[i // 2]], [qb.r], qb.r)
                    qblk[key] = qb
                return qblk[key]

            ebs = {}

            def issue_S2(g):
                h, i, p = gsteps[g]
                if p == 0:
                    ensure_q(h, i)
                    if i + 2 < NQB:
                        ensure_q(h, i + 2)
                KTh, Vh = heads[h]
                qb = ensure_q(h, i)
                pair = SP3[g % 3]
                qo = (i % 2) * QB
                for t in range(2):
                    kt = 2 * p + t
                    for c in range(2):
                        mm(ps[:, pair[c], t * QB:(t + 1) * QB], KTh.t[c * 64:(c + 1) * 64, kt * 128:(kt + 1) * 128],
                           qb.t[c * 64:(c + 1) * 64, qo:qo + QB], True, True, [KTh.r, qb.r], [rb[pair[c]]])
                eb = E4_ring.next()
                act(eb.t[:].rearrange("p c t q -> p c (t q)"), ps[:, pair[0]:pair[1] + 1, :], AF.Exp,
                    [rb[pair[0]], rb[pair[1]], sv.r], [eb.r], bias=nshift_d, scale=0.125)
                ebs[g] = eb

            def issue_PV2(g):
                h, i, p = gsteps[g]
                KTh, Vh = heads[h]
                eb = ebs.pop(g)
                if p == i:
                    memset("pool", eb.t[64:128, :, 0, 0:64], 0.0, [eb.r])
                    memset("pool", eb.t[:, :, 1, 0:128], 0.0, [eb.r])
                    memset("pool", eb.t[64:128, :, 1, 128:192], 0.0, [eb.r])
                for t in range(2):
                    kt = 2 * p + t
                    for c in range(2):
                        mm(ps[:, OB, c * QB:(c + 1) * QB], Vh.t[:, kt, :], eb.t[:, c, t, :],
                           p == 0 and t == 0 and c == 0, p == i and t == 1, [Vh.r, eb.r], [rb[OB]], skip=True)
                for t in range(2):
                    mm(ps[:, RB, 0:QB], onesb.t[:], eb.t[:, 1, t, :], p == 0 and t == 0, False, [onesb.r, eb.r], [rb[RB]],
                       skip=True)
                if p == 0:
                    cp("dve", acc2.t[:, :, :], eb.t[:, 0, :, :], [eb.r], [acc2.r])
                else:
                    tt("dve", acc2.t[:, :, :], acc2.t[:, :, :], eb.t[:, 0, :, :], ALU.add, [eb.r, acc2.r], [acc2.r])
                if p == i:
                    flush_pending(NG)
                    ep = ep2_ring.next()
                    cp("dve", ep.t[:, 0, :], ps[:, OB, :], [rb[OB]], [ep.r])
                    cp("dve", ep.t[:, 1, 0:QB], ps[:, RB, 0:QB], [rb[RB]], [ep.r])
                    recip(ep.t[:, 1, 0:QB], ep.t[:, 1, 0:QB], [ep.r], [ep.r])
                    hl = hl2_ring.next()
                    cp("dve", hl.t[:, 0, :, :], acc2.t[:, :, :], [acc2.r], [hl.r])
                    tt("dve", hl.t[:, 1, :, :], acc2.t[:, :, :], hl.t[:, 0, :, :], ALU.subtract, [acc2.r, hl.r], [hl.r])

                    def rest(ep=ep, hl=hl, h=h, i=i):
                        for k_ in range(2):
                            for t in range(2):
                                mm(ps[:, RB, QB:2 * QB], onesb.t[:], hl.t[:, k_, t, :], False, k_ == 1 and t == 1,
                                   [onesb.r, hl.r], [rb[RB]], skip=True)
                        cp("dve", ep.t[:, 1, QB:2 * QB], ps[:, RB, QB:2 * QB], [rb[RB]], [ep.r])
                        memset("dve", ps[:, RB, QB:2 * QB], 0.0, [rb[RB]])
                        recip(ep.t[:, 1, QB:2 * QB], ep.t[:, 1, QB:2 * QB], [ep.r], [ep.r])
                        tt("pool", ep.t[:, 0, 0:QB], ep.t[:, 0, 0:QB], ep.t[:, 1, QB:2 * QB], ALU.mult, [ep.r], [ep.r])
                        tt("pool", ep.t[:, 0, QB:2 * QB], ep.t[:, 0, QB:2 * QB], ep.t[:, 1, 0:QB], ALU.mult, [ep.r], [ep.r])
                        ot = ot2_ring.next()
                        stt("dve", ot.t[:], ep.t[:, 0, QB:2 * QB], neglam, ep.t[:, 0, 0:QB], ALU.mult, ALU.add,
                            [ep.r, sv.r], [ot.r])
                        dma("sp", OT_s[h, :, i * QB:(i + 1) * QB], ot.t[:], [ot.r], [r_ot[h][i // 2]], ot.r)

                    pending.append((g + 2, rest))
                    if i == NQB - 1 and h + 2 < 4:
                        load_head(h + 2)

            pending = []

            def flush_pending(gnow):
                while pending and pending[0][0] <= gnow:
                    pending.pop(0)[1]()

            memset("dve", ps[:, RB, QB:2 * QB], 0.0, [rb[RB]])
            load_head(0)
            load_head(1)
            NG = len(gsteps)
            issue_S2(0)
            issue_S2(1)
            for g in range(NG):
                if g + 2 < NG:
                    issue_S2(g + 2)
                issue_PV2(g)
                flush_pending(g)
            flush_pending(NG + 10)
            S.barrier()

        with ExitStack() as sc_:
            wout = sbuf(sc_, [128, 8, D], BF16, "wout")
            with ExitStack() as s0:
                wst = Ring(s0, 2, [128, D], F32, "wst2")
                for dc in range(8):
                    b = wst.next()
                    dma("sp", b.t[:], w_out[dc * 128:(dc + 1) * 128, :], [], [b.r], b.r)
                    cp("dve" if dc % 2 == 0 else "pool", wout.t[:, dc, :], b.t[:], [b.r], [wout.r])
                S.barrier()
            o_ring = Ring(sc_, 2, [128, 4, BT], BF16, "o3")
            g_ring = Ring(sc_, 2, [128, 4, BT], BF16, "g3")
            m_ring = Ring(sc_, 3, [128, 4, BT], BF16, "m3")
            oa_ring = Ring(sc_, 3, [128, 4, BT], BF16, "oa3")
            osq_ring = Ring(sc_, 2, [128, BT], BF16, "osq")
            rs_ring = Ring(sc_, 2, [128, 2, BT], F32, "rs3")
            x_ring = Ring(sc_, 3, [128, D], F32, "x3")
            y_ring = Ring(sc_, 3, [128, D], F32, "y3")
            B_ST = (0, 1)
            B_Y = ((2, 3), (4, 5), (6, 7))
            cnt = [0, 0]

            def out_stats(NT, o_b, g_b):
                oa = oa_ring.next()
                for h in range(4):
                    osq = osq_ring.next()
                    tt("pool", osq.t[:, 0:NT], o_b.t[:, h, 0:NT], o_b.t[:, h, 0:NT], ALU.mult, [o_b.r], [osq.r])
                    bk = B_ST[cnt[0] % 2]
                    cnt[0] += 1
                    mm(ps[:, bk, 0:NT], onesm.t[:], osq.t[:, 0:NT], True, True, [onesm.r, osq.r], [rb[bk]])
                    rs = rs_ring.next()
                    act(rs.t[:, 0, 0:NT], ps[:, bk, 0:NT], AF.Ln, [rb[bk], cst.r], [rs.r], bias=eps_t)
                    act(rs.t[:, 1, 0:NT], rs.t[:, 0, 0:NT], AF.Exp, [rs.r], [rs.r], scale=-0.5)
                    tt("dve", rs.t[:, 0, 0:NT], o_b.t[:, h, 0:NT], rs.t[:, 1, 0:NT], ALU.mult, [o_b.r, rs.r], [rs.r])
                    stt("dve", oa.t[:, h, 0:NT], rs.t[:, 0, 0:NT], subg_t, g_b.t[:, h, 0:NT], ALU.mult, ALU.mult,
                        [rs.r, sg.r, g_b.r], [oa.r])
                return oa

            def out_tiles(NT, oa, m_b, x_dram, y_dram):
                for ti in range(NT // 128):
                    xb_ = x_ring.next()
                    dma("sp", xb_.t[:], x_dram[ti * 128:(ti + 1) * 128, :], [], [xb_.r], xb_.r)
                    pair = B_Y[cnt[1] % 3]
                    cnt[1] += 1
                    for half in range(2):
                        bk = pair[half]
                        for dc in range(8):
                            lhs = oa.t[:, dc, ti * 128:(ti + 1) * 128] if dc < 4 else m_b.t[:, dc - 4, ti * 128:(ti + 1) * 128]
                            mm(ps[:, bk, :], lhs, wout.t[:, dc, half * 512:(half + 1) * 512], dc == 0, dc == 7,
                               [oa.r, m_b.r, wout.r], [rb[bk]])
                    yb = y_ring.next()
                    for half in range(2):
                        tt("dve", yb.t[:, half * 512:(half + 1) * 512], ps[:, pair[half], :],
                           xb_.t[:, half * 512:(half + 1) * 512], ALU.add, [rb[pair[half]], xb_.r], [yb.r])
                    dma("act", y_dram[ti * 128:(ti + 1) * 128, :], yb.t[:], [yb.r], [], yb.r)

            def load_blk(bi):
                cols = slice(bi * BT, (bi + 1) * BT)
                o_b = o_ring.next(); g_b = g_ring.next(); m_b = m_ring.next()
                dma("sp", o_b.t[:], OT_s[:, :, cols].rearrange("h p t -> p h t"), [r_ot[h][bi] for h in range(4)],
                    [o_b.r], o_b.r)
                dma("sp", g_b.t[:], G_s[:, :, cols].rearrange("h p t -> p h t"), [r_g_blk[bi]], [g_b.r], g_b.r)
                dma("sp", m_b.t[:], MBC_s[:, :, cols].rearrange("h p t -> p h t"), [r_mbc_blk[bi]], [m_b.r], m_b.r)
                return o_b, g_b, m_b

            oa_s = out_stats(128, oT_smp, gate_smp)
            blk = load_blk(0)
            oa_n = out_stats(BT, blk[0], blk[1])
            out_tiles(128, oa_s, mbc_smp, xs, ys)
            for bi in range(NBLK):
                cur_oa, cur_m = oa_n, blk[2]
                if bi + 1 < NBLK:
                    blk = load_blk(bi + 1)
                    oa_n = out_stats(BT, blk[0], blk[1])
                out_tiles(BT, cur_oa, cur_m, xp[bi * BT:(bi + 1) * BT, :], yp[bi * BT:(bi + 1) * BT, :])

        S.emit()
    return nc


_NC_CACHE = {}


def _rope_table(pos):
    half = 8
    inv = ROPE_THETA ** (-np.arange(0, 16, 2, dtype=np.float32) / np.float32(16))
    ang = pos.astype(np.float32)[:, None] * inv[None, :].astype(np.float32)
    return np.concatenate([np.cos(ang), np.sin(ang)], axis=1).astype(np.float32)


def kernel(x_prompt, x_sample, mem_prompt, cache_diff_k, cache_diff_v, cache_mem_k, cache_mem_v,
           state_lru_conv, state_lru_h, norm_g, w_in, da_q_norm_g, da_k_norm_g, lambda_q1, lambda_k1,
           lambda_q2, lambda_k2, da_subln_g, lru_conv_w, lru_conv_b, lru_w_a, lru_b_a, lru_w_x, lru_b_x,
           lru_lambda, mem_norm_g, w_mem_kv, mx_q_norm_g, mx_k_norm_g, w_out):
    f = lambda a: np.ascontiguousarray(np.asarray(a, dtype=np.float32))
    if "nc" not in _NC_CACHE:
        _NC_CACHE["nc"] = build_nc()
    nc = _NC_CACHE["nc"]
    shared = {
        "w_in": f(w_in[0]), "w_out": f(w_out[0]), "w_mkv": f(w_mem_kv[0]),
        "norm_g": f(norm_g[0]), "mem_norm_g": f(mem_norm_g[0]),
        "qg": f(da_q_norm_g[0]), "kg": f(da_k_norm_g[0]), "mqg": f(mx_q_norm_g[0]), "mkg": f(mx_k_norm_g[0]),
        "lams": f(np.stack([np.asarray(lambda_q1[0]), np.asarray(lambda_k1[0]), np.asarray(lambda_q2[0]),
                            np.asarray(lambda_k2[0])])),
        "subg": f(da_subln_g[0]), "convw": f(lru_conv_w[0]), "convb": f(lru_conv_b[0]),
        "b_a": f(lru_b_a[0]), "b_x": f(lru_b_x[0]), "lrul": f(lru_lambda[0]),
        "w_a": f(lru_w_a[0]), "w_x": f(lru_w_x[0]),
        "ident": np.eye(128, dtype=np.float32),
        "ropep": np.ascontiguousarray(_rope_table(np.arange(T)).reshape(T // 128, 128, 16).transpose(1, 0, 2).reshape(128, -1)),
        "ropes": _rope_table(PAST + (np.arange(128) % 32)),
    }
    xp_ = np.asarray(x_prompt); xs_ = np.asarray(x_sample); mem_ = np.asarray(mem_prompt)
    ck_ = np.asarray(cache_diff_k)[0]; cv_ = np.asarray(cache_diff_v)[0]
    cmk_ = np.asarray(cache_mem_k)[0]; cmv_ = np.asarray(cache_mem_v)[0]
    sc_ = np.asarray(state_lru_conv)[0]; sh_ = np.asarray(state_lru_h)[0]
    in_maps = []
    for c in range(8):
        sl = slice(4 * c, 4 * c + 4)
        m = dict(shared)
        m["xp"] = f(xp_[c]); m["xs"] = f(xs_[sl].reshape(128, D)); m["mem"] = f(mem_[c])
        m["ck"] = f(ck_[sl].reshape(4, PAST, 512)); m["cv"] = f(cv_[sl].reshape(4, PAST, 512))
        m["cmk"] = f(cmk_[sl].reshape(4, 256, 256)); m["cmv"] = f(cmv_[sl].reshape(4, 256, 256))
        m["sconv"] = f(sc_[sl]); m["sh"] = f(sh_[sl])
        in_maps.append(m)
    res = run_bass_kernel_spmd(nc, in_maps, core_ids=list(range(8)))
    R = res.results
    cat = lambda k: np.stack([np.asarray(r[k]) for r in R])
    y_p = cat("yp")
    y_s = cat("ys").reshape(32, 32, D)
    k_p = cat("kp").reshape(1, 8, T, 4, 2, 64)
    v_p = cat("vp").reshape(1, 8, T, 4, 128)
    mk_p = cat("mkp").reshape(1, 8, 256, 4, 64)
    mv_p = cat("mvp").reshape(1, 8, 256, 4, 64)
    c_p = cat("cpo").reshape(1, 8, 3, 256)
    h_p = cat("hpo").reshape(1, 8, 256)
    k_s = cat("kso").reshape(1, 32, 32, 4, 2, 64)
    v_s = cat("vso").reshape(1, 32, 32, 4, 128)
    c_s = cat("cso").reshape(1, 32, 3, 256)
    h_s = cat("hso").reshape(1, 32, 256)
    return (y_p, y_s, k_p, v_p, mk_p, mv_p, c_p, h_p, k_s, v_s, c_s, h_s)
```

```python
import math
from contextlib import ExitStack

import numpy as np
import concourse.bass as bass
import concourse.mybir as mybir
from concourse.bass_utils import run_bass_kernel_spmd

F32 = mybir.dt.float32
BF16 = mybir.dt.bfloat16
ALU = mybir.AluOpType
AF = mybir.ActivationFunctionType
AX = mybir.AxisListType

T = 8192
D = 1024
BT = 512
NBLK = T // BT
PAST = 1024
LAMBDA_INIT = 0.8 - 0.6 * math.exp(0.0)
EPS = 1e-6
ROPE_THETA = 500000.0


class Res:
    __slots__ = ("name", "w", "rs", "sem", "ndma", "excl")

    def __init__(self, name, excl=False):
        self.name = name
        self.w = None
        self.rs = []
        self.sem = None
        self.ndma = 0
        self.excl = excl


class Op:
    __slots__ = ("eng", "fn", "deps", "signal", "kind", "sem", "val")


class Sched:
    ENGS = ("pe", "act", "dve", "pool", "sp")

    def __init__(self, nc, stack):
        self.nc = nc
        self.stack = stack
        self.streams = {e: [] for e in self.ENGS}
        self.esem = {e: stack.enter_context(nc.semaphore("es_" + e)) for e in self.ENGS}
        self.nsem = len(self.ENGS)
        self.last_dma = {}

    def _add(self, eng, fn, reads, writes, kind):
        op = Op()
        op.eng = eng
        op.fn = fn
        op.kind = kind
        op.signal = False
        op.sem = None
        op.val = None
        deps = {}
        for r in reads:
            if r.excl:
                continue
            if r.w is not None:
                deps[id(r.w)] = (r.w, "raw")
        for w in list(writes) + [r for r in reads if r.excl]:
            if w.w is not None:
                deps[id(w.w)] = (w.w, "waw")
            for rd in w.rs:
                if id(rd) not in deps:
                    deps[id(rd)] = (rd, "war")
        op.deps = []
        for p, typ in deps.values():
            if p.kind == "c" and kind == "c" and p.eng == eng:
                if eng == "pe" or typ == "war":
                    continue
            op.deps.append(p)
            if p.kind == "c":
                p.signal = True
        for r in reads:
            if r.excl:
                r.w = op
                r.rs = []
            else:
                r.rs.append(op)
        for w in writes:
            w.w = op
            w.rs = []
        self.streams[eng].append(op)
        return op

    def op(self, eng, fn, reads=(), writes=()):
        return self._add(eng, fn, reads, writes, "c")

    def dma(self, q, out, in_, reads=(), writes=(), semres=None, **kw):
        if semres.sem is None:
            semres.sem = self.stack.enter_context(self.nc.semaphore("ds_%d" % self.nsem))
            self.nsem += 1
        op = self._add(q, lambda e: e.dma_start(out=out, in_=in_, **kw), reads, writes, "d")
        semres.ndma += 1
        op.sem = semres.sem
        op.val = 16 * semres.ndma
        self.last_dma[id(op.sem)] = op
        return op

    def barrier(self):
        lasts = []
        for e in self.ENGS:
            for op in reversed(self.streams[e]):
                if op.kind == "c" and op.fn is not None:
                    lasts.append(op)
                    break
        dmas = list(self.last_dma.values())
        for e in self.ENGS:
            op = Op()
            op.eng = e
            op.fn = None
            op.kind = "c"
            op.signal = False
            op.sem = None
            op.val = None
            op.deps = [p for p in lasts if p.eng != e] + dmas
            for p in op.deps:
                if p.kind == "c":
                    p.signal = True
            self.streams[e].append(op)

    def emit(self):
        nc = self.nc
        for e in self.ENGS:
            cnt = 0
            for op in self.streams[e]:
                if op.kind == "c" and op.signal:
                    cnt += 1
                    op.sem = self.esem[e]
                    op.val = cnt
        last_dma = self.last_dma
        streams = self.streams
        esem = self.esem

        def run(e, eng):
            waited = {}
            for op in streams[e]:
                for p in op.deps:
                    k = id(p.sem)
                    if waited.get(k, 0) >= p.val:
                        continue
                    waited[k] = p.val
                    eng.wait_ge(p.sem, p.val)
                if op.fn is None:
                    continue
                ins = op.fn(eng)
                if op.kind == "d":
                    ins.then_inc(op.sem, 16)
                elif op.signal:
                    ins.then_inc(esem[e], 1)
            if e == "sp":
                for p in last_dma.values():
                    if waited.get(id(p.sem), 0) < p.val:
                        eng.wait_ge(p.sem, p.val)

        with nc.Block() as block:
            @block.tensor
            def _(eng):
                run("pe", eng)

            @block.scalar
            def _(eng):
                run("act", eng)

            @block.vector
            def _(eng):
                run("dve", eng)

            @block.gpsimd
            def _(eng):
                run("pool", eng)

            @block.sync
            def _(eng):
                run("sp", eng)


class Buf:
    __slots__ = ("t", "r")

    def __init__(self, t, r):
        self.t = t
        self.r = r


def build_nc():
    nc = bass.Bass("TRN2", target_bir_lowering=False)

    def din(name, shape, dt=F32):
        return nc.dram_tensor(name, list(shape), dt, kind="ExternalInput").ap()

    def dout(name, shape):
        return nc.dram_tensor(name, list(shape), F32, kind="ExternalOutput").ap()

    def dscr(name, shape, dt=BF16):
        return nc.dram_tensor(name, list(shape), dt).ap()

    xp = din("xp", [T, D]); xs = din("xs", [128, D]); mem = din("mem", [256, D])
    ck = din("ck", [4, PAST, 512]); cv = din("cv", [4, PAST, 512])
    cmk = din("cmk", [4, 256, 256]); cmv = din("cmv", [4, 256, 256])
    sconv = din("sconv", [4, 3, 256]); sh = din("sh", [4, 256])
    w_in = din("w_in", [D, 3072]); w_out = din("w_out", [D, D]); w_mkv = din("w_mkv", [D, 512])
    norm_g = din("norm_g", [D]); mem_norm_g = din("mem_norm_g", [D])
    qg = din("qg", [64]); kg = din("kg", [64]); mqg = din("mqg", [64]); mkg = din("mkg", [64])
    lams = din("lams", [4, 64]); subg = din("subg", [128])
    convw = din("convw", [4, 256]); convb = din("convb", [256]); b_a = din("b_a", [256]); b_x = din("b_x", [256])
    lrul = din("lrul", [256]); w_a = din("w_a", [4, 64, 64]); w_x = din("w_x", [4, 64, 64])
    ident = din("ident", [128, 128]); ropep = din("ropep", [128, (T // 128) * 16]); ropes = din("ropes", [128, 16])

    yp = dout("yp", [T, D]); ys = dout("ys", [128, D])
    kp = dout("kp", [T, 512]); vp = dout("vp", [T, 512])
    mkp = dout("mkp", [256, 256]); mvp = dout("mvp", [256, 256])
    cpo = dout("cpo", [3, 256]); hpo = dout("hpo", [256])
    kso = dout("kso", [128, 512]); vso = dout("vso", [128, 512])
    cso = dout("cso", [4, 3, 256]); hso = dout("hso", [4, 256])

    QT_s = dscr("QT_s", [4, 128, T]); KT_s = dscr("KT_s", [4, 128, T]); V_s = dscr("V_s", [T, 512])
    OT_s = dscr("OT_s", [4, 128, T]); G_s = dscr("G_s", [4, 128, T]); MBC_s = dscr("MBC_s", [4, 128, T])

    gst = ExitStack()
    with gst:
        S = Sched(nc, gst)
        uid = [0]

        def sbuf(st, shape, dt, name=None):
            uid[0] += 1
            nm = "%s_%d" % (name or "b", uid[0])
            return Buf(st.enter_context(nc.sbuf_tensor(nm, list(shape), dt)), Res(nm))

        class Ring:
            def __init__(self, st, n, shape, dt, name):
                self.bufs = [sbuf(st, shape, dt, name) for _ in range(n)]
                self.i = 0

            def next(self):
                b = self.bufs[self.i % len(self.bufs)]
                self.i += 1
                return b

        ps = gst.enter_context(nc.psum_tensor("ps", [128, 8, 512], F32))
        rb = [Res("bank%d" % i, excl=True) for i in range(8)]

        def bank_bf(b):
            return ps[:, b, :].bitcast(BF16).rearrange("p (a c) -> p a c", c=128)

        def mm(out, lhsT, rhs, start, stop, r, w, skip=False):
            if skip:
                S.op("pe", lambda e: e.matmul(out=out, lhsT=lhsT, rhs=rhs, start=start, stop=stop,
                                              skip_group_check=True), r, w)
            else:
                S.op("pe", lambda e: e.matmul(out=out, lhsT=lhsT, rhs=rhs, start=start, stop=stop), r, w)

        def tr(out, in_, r, w):
            S.op("pe", lambda e: e.transpose(out=out, in_=in_, identity=identb.t[:]), list(r) + [identb.r], w)

        def tt(eng, out, in0, in1, op, r, w):
            S.op(eng, lambda e: e.tensor_tensor(out=out, in0=in0, in1=in1, op=op), r, w)

        def ts(eng, out, in0, s1, s2, op0, op1, r, w):
            if s2 is None:
                S.op(eng, lambda e: e.tensor_scalar(out=out, in0=in0, scalar1=s1, scalar2=None, op0=op0), r, w)
            else:
                S.op(eng, lambda e: e.tensor_scalar(out=out, in0=in0, scalar1=s1, scalar2=s2, op0=op0, op1=op1), r, w)

        def stt(eng, out, in0, scalar, in1, op0, op1, r, w, accum_out=None):
            if accum_out is None:
                S.op(eng, lambda e: e.scalar_tensor_tensor(out=out, in0=in0, scalar=scalar, in1=in1, op0=op0, op1=op1), r, w)
            else:
                S.op(eng, lambda e: e.scalar_tensor_tensor(out=out, in0=in0, scalar=scalar, in1=in1, op0=op0, op1=op1,
                                                           accum_out=accum_out), r, w)

        def act(out, in_, func, r, w, bias=None, scale=1.0, accum_out=None):
            if accum_out is not None:
                S.op("act", lambda e: e.activation(out=out, in_=in_, func=func, scale=scale, accum_out=accum_out), r, w)
            elif bias is None:
                S.op("act", lambda e: e.activation(out=out, in_=in_, func=func, scale=scale), r, w)
            else:
                S.op("act", lambda e: e.activation(out=out, in_=in_, func=func, bias=bias, scale=scale), r, w)

        def cp(eng, out, in_, r, w):
            if eng == "act":
                S.op("act", lambda e: e.copy(out=out, in_=in_), r, w)
            else:
                S.op(eng, lambda e: e.tensor_copy(out=out, in_=in_), r, w)

        def memset(eng, ap, val, w):
            S.op(eng, lambda e: e.memset(ap, val), [], w)

        def red(eng, out, in_, op, r, w, absval=False):
            if absval:
                S.op(eng, lambda e: e.tensor_reduce(out=out, in_=in_, axis=AX.X, op=op, apply_absolute_value=True), r, w)
            else:
                S.op(eng, lambda e: e.tensor_reduce(out=out, in_=in_, axis=AX.X, op=op), r, w)

        def recip(out, in_, r, w):
            S.op("dve", lambda e: e.reciprocal(out=out, in_=in_), r, w)

        def scan(out, d0, d1, init, r, w):
            S.op("dve", lambda e: e.tensor_tensor_scan(out=out, data0=d0, data1=d1, initial=init,
                                                       op0=ALU.mult, op1=ALU.add), r, w)

        def dma(q, out, in_, r, w, semres, **kw):
            S.dma(q, out, in_, reads=r, writes=w, semres=semres, **kw)

        nc_ctx = nc.allow_non_contiguous_dma(reason="small strided parameter / state transfers")
        gst.enter_context(nc_ctx)

        identf = sbuf(gst, [128, 128], F32, "identf")
        identb = sbuf(gst, [128, 128], BF16, "identb")
        dma("sp", identf.t[:], ident, [], [identf.r], identf.r)
        cp("dve", identb.t[:], identf.t[:], [identf.r], [identb.r])
        onesb = sbuf(gst, [128, 128], BF16, "onesb")
        memset("pool", onesb.t[:], 1.0, [onesb.r])
        onesm = sbuf(gst, [128, 128], BF16, "onesm")
        memset("pool", onesm.t[:], 1.0 / 128.0, [onesm.r])
        cst = sbuf(gst, [128, 4], F32, "cst")
        memset("pool", cst.t[:, 0:1], EPS, [cst.r])
        memset("pool", cst.t[:, 1:2], 1.0, [cst.r])
        eps_t = cst.t[:, 0:1]
        one_t = cst.t[:, 1:2]

        gv = sbuf(gst, [128, 4, 64], F32, "gv")
        for i, v in enumerate((qg, kg, mqg, mkg)):
            dma("sp", gv.t[:, i, :], v.partition_broadcast(128), [], [gv.r], gv.r)
        gfull = sbuf(gst, [128, 20, 64], F32, "gfull")
        cp("pool", gfull.t[:, 0:8, :], gv.t[:, 0:1, :].to_broadcast([128, 8, 64]), [gv.r], [gfull.r])
        cp("pool", gfull.t[:, 8:16, :], gv.t[:, 1:2, :].to_broadcast([128, 8, 64]), [gv.r], [gfull.r])
        cp("pool", gfull.t[:, 16:20, :], gv.t[:, 2:3, :].to_broadcast([128, 4, 64]), [gv.r], [gfull.r])
        gmk = sbuf(gst, [128, 4, 64], F32, "gmk")
        cp("pool", gmk.t[:, :, :], gv.t[:, 3:4, :].to_broadcast([128, 4, 64]), [gv.r], [gmk.r])
        sv = sbuf(gst, [128, 16], F32, "sv")
        red("dve", sv.t[:, 0:4], gv.t[:, :, :], ALU.max, [gv.r], [sv.r], absval=True)
        tt("dve", sv.t[:, 4:5], sv.t[:, 0:1], sv.t[:, 1:2], ALU.mult, [sv.r], [sv.r])
        tt("dve", sv.t[:, 5:6], sv.t[:, 2:3], sv.t[:, 3:4], ALU.mult, [sv.r], [sv.r])
        ts("dve", sv.t[:, 6:8], sv.t[:, 4:6], -8.0, None, ALU.mult, None, [sv.r], [sv.r])
        nshift_d = sv.t[:, 6:7]
        nshift_m = sv.t[:, 7:8]
        lb = sbuf(gst, [128, 4, 64], F32, "lb")
        dma("sp", lb.t[:, :, :], lams.partition_broadcast(128), [], [lb.r], lb.r)
        lp = sbuf(gst, [128, 2, 64], F32, "lp")
        tt("dve", lp.t[:, 0, :], lb.t[:, 0, :], lb.t[:, 1, :], ALU.mult, [lb.r], [lp.r])
        tt("dve", lp.t[:, 1, :], lb.t[:, 2, :], lb.t[:, 3, :], ALU.mult, [lb.r], [lp.r])
        red("dve", sv.t[:, 8:10], lp.t[:, :, :], ALU.add, [lp.r], [sv.r])
        act(sv.t[:, 10:12], sv.t[:, 8:10], AF.Exp, [sv.r], [sv.r])
        tt("dve", sv.t[:, 12:13], sv.t[:, 11:12], sv.t[:, 10:11], ALU.subtract, [sv.r], [sv.r])
        ts("dve", sv.t[:, 13:14], sv.t[:, 12:13], -LAMBDA_INIT, None, ALU.add, None, [sv.r], [sv.r])
        neglam = sv.t[:, 13:14]
        sg = sbuf(gst, [128, 2], F32, "sg")
        dma("sp", sg.t[:, 0:1], subg.rearrange("(p o) -> p o", o=1), [], [sg.r], sg.r)
        ts("dve", sg.t[:, 1:2], sg.t[:, 0:1], 1.0 - LAMBDA_INIT, None, ALU.mult, None, [sg.r], [sg.r])
        subg_t = sg.t[:, 1:2]
        lv = sbuf(gst, [128, 2, 12], F32, "lv")
        for cc in range(2):
            dma("sp", lv.t[:, cc, 0:4], convw[:, cc * 128:(cc + 1) * 128].rearrange("j p -> p j"), [], [lv.r], lv.r)
            for i, v in enumerate((convb, b_a, b_x, lrul)):
                dma("sp", lv.t[:, cc, 4 + i:5 + i], v[cc * 128:(cc + 1) * 128].rearrange("(p o) -> p o", o=1),
                    [], [lv.r], lv.r)
        act(lv.t[:, :, 8:9], lv.t[:, :, 7:8], AF.Exp, [lv.r], [lv.r], scale=-1.0)
        act(lv.t[:, :, 9:10], lv.t[:, :, 8:9], AF.Ln, [lv.r, cst.r], [lv.r], bias=one_t)
        ts("dve", lv.t[:, :, 10:11], lv.t[:, :, 9:10], -8.0, None, ALU.mult, None, [lv.r], [lv.r])
        wbdf = sbuf(gst, [128, 2, 2, 128], F32, "wbdf")
        memset("pool", wbdf.t[:], 0.0, [wbdf.r])
        for wi, wsrc in enumerate((w_a, w_x)):
            for n in range(4):
                cc, hf = n // 2, n % 2
                dma("sp", wbdf.t[hf * 64:(hf + 1) * 64, wi, cc, hf * 64:(hf + 1) * 64], wsrc[n], [], [wbdf.r], wbdf.r)
        wbd = sbuf(gst, [128, 2, 2, 128], BF16, "wbd")
        cp("dve", wbd.t[:], wbdf.t[:], [wbdf.r], [wbd.r])
        rope_ring = Ring(gst, 4, [128, 16], F32, "rope_t")
        rope_s = sbuf(gst, [128, 16], F32, "rope_s")
        dma("sp", rope_s.t[:], ropes, [], [rope_s.r], rope_s.r)

        qT_smp = sbuf(gst, [128, 4, 128], BF16, "qT_smp")
        kT_smp = sbuf(gst, [128, 4, 128], BF16, "kT_smp")
        qmT_smp = sbuf(gst, [128, 2, 128], BF16, "qmT_smp")
        vb_smp = sbuf(gst, [128, 512], BF16, "vb_smp")
        oT_smp = sbuf(gst, [128, 4, 128], BF16, "oT_smp")
        gate_smp = sbuf(gst, [128, 4, 128], BF16, "gate_smp")
        mbc_smp = sbuf(gst, [128, 4, 128], BF16, "mbc_smp")
        MKT_p = sbuf(gst, [128, 2, 256], BF16, "MKT_p")
        MV_p = sbuf(gst, [128, 2, 256], BF16, "MV_p")
        MKT_s = sbuf(gst, [128, 4, 2, 256], BF16, "MKT_s")
        MV_s = sbuf(gst, [128, 4, 2, 256], BF16, "MV_s")

        r_qk_blk = [Res("qkblk%d" % i) for i in range(NBLK)]
        r_v_blk = [Res("vblk%d" % i) for i in range(NBLK)]
        r_g_blk = [Res("gblk%d" % i) for i in range(NBLK)]
        r_mbc_blk = [Res("mbcblk%d" % i) for i in range(NBLK)]
        r_ot = [[Res("ot%d_%d" % (h, i)) for i in range(NBLK)] for h in range(4)]

        with ExitStack() as sa:
            win = sbuf(sa, [128, 8, 3072], BF16, "win")
            xring = Ring(sa, 4, [128, D], F32, "x")
            xbring = Ring(sa, 2, [128, D], BF16, "xb")
            string = Ring(sa, 4, [128, 4], F32, "st")
            hnT2 = [sbuf(sa, [128, 8, BT], BF16, "hnT") for _ in range(2)]
            r_hn2 = [[Res("hn%d_%d" % (k, i)) for i in range(4)] for k in range(2)]
            qkf_ring = Ring(sa, 4, [128, 1280], F32, "qkf")
            vf_ring = Ring(sa, 4, [128, 512], F32, "vf")
            sq_ring = Ring(sa, 4, [128, 1280], BF16, "sq")
            st20_ring = Ring(sa, 3, [128, 3, 20], F32, "st20")
            qkb_ring = Ring(sa, 4, [128, 1280], BF16, "qkb")
            vb_ring = Ring(sa, 2, [128, 512], BF16, "vb")
            rtmp_ring = Ring(sa, 1, [128, 4, 16, 8], F32, "rtmp")
            qkT_ring = Ring(sa, 2, [128, 8, 128], BF16, "qkT")
            qmT2 = [sbuf(sa, [128, 2, BT], BF16, "qmT") for _ in range(2)]
            ngt = sbuf(sa, [128, 2, 8], F32, "ngt")
            dma("sp", ngt.t[:, 0, :], norm_g.rearrange("(c p) -> p c", p=128), [], [ngt.r], ngt.r)
            dma("sp", ngt.t[:, 1, :], mem_norm_g.rearrange("(c p) -> p c", p=128), [], [ngt.r], ngt.r)

            B_TA, B_TB, B_Q, B_K, B_F0, B_F1, B_MO, B_MR = range(8)

            def norm_a(x_dram):
                xb_ = xring.next()
                dma("sp", xb_.t[:], x_dram, [], [xb_.r], xb_.r)
                return xb_

            def norm_b(xb_):
                st_ = string.next()
                xbf = xbring.next()
                act(xbf.t[:], xb_.t[:], AF.Square, [xb_.r], [st_.r, xbf.r], accum_out=st_.t[:, 0:1])
                act(st_.t[:, 1:2], st_.t[:, 0:1], AF.Ln, [st_.r, cst.r], [st_.r], bias=eps_t, scale=1.0 / D)
                act(st_.t[:, 2:3], st_.t[:, 1:2], AF.Exp, [st_.r], [st_.r], scale=-0.5)
                ts("dve", xbf.t[:], xb_.t[:], st_.t[:, 2:3], None, ALU.mult, None, [xb_.r, st_.r], [xbf.r])
                return xbf

            def norm_c(xbf, hn_dst, rdeps):
                va = bank_bf(B_TA)
                for dc in range(8):
                    tr(va[:, dc, :], xbf.t[:, dc * 128:(dc + 1) * 128], [xbf.r], [rb[B_TA]])
                cp("act", hn_dst, va[:, :, :], [rb[B_TA]], rdeps)

            def gnorm(src, G, gap, rsrc, rg, sq=None):
                s3 = src.rearrange("p (g d) -> p g d", d=64)
                if sq is None:
                    sq = sq_ring.next()
                    tt("pool", sq.t[:, 0:G * 64], src, src, ALU.mult, [rsrc], [sq.r])
                s20 = st20_ring.next()
                red("dve", s20.t[:, 0, 0:G], sq.t[:, 0:G * 64].rearrange("p (g d) -> p g d", d=64), ALU.add,
                    [sq.r], [s20.r])
                act(s20.t[:, 1, 0:G], s20.t[:, 0, 0:G], AF.Ln, [s20.r, cst.r], [s20.r], bias=eps_t, scale=1.0 / 64)
                act(s20.t[:, 2, 0:G], s20.t[:, 1, 0:G], AF.Exp, [s20.r], [s20.r], scale=-0.5)
                tt("dve", s3, s3, s20.t[:, 2, 0:G].unsqueeze(2).to_broadcast([128, G, 64]), ALU.mult,
                   [rsrc, s20.r], [rsrc])
                tt("dve", s3, s3, gap, ALU.mult, [rsrc, rg], [rsrc])

            with ExitStack() as s0:
                wmkv = sbuf(s0, [128, 8, 512], BF16, "wmkv")
                wst = Ring(s0, 2, [128, 1024], F32, "wst")
                for dc in range(8):
                    b = wst.next()
                    dma("sp", b.t[:, 0:512], w_mkv[dc * 128:(dc + 1) * 128, :], [], [b.r], b.r)
                    ts("dve", wmkv.t[:, dc, :], b.t[:, 0:512], ngt.t[:, 1, dc:dc + 1], None, ALU.mult, None,
                       [b.r, ngt.r], [wmkv.r])
                wst2 = Ring(s0, 3, [128, 1024], F32, "wstb")
                win_todo = [(dc, pc) for dc in range(8) for pc in range(3)]

                def stage_win(n):
                    for _ in range(n):
                        if not win_todo:
                            return
                        dc, pc = win_todo.pop(0)
                        b = wst2.next()
                        dma("sp", b.t[:], w_in[dc * 128:(dc + 1) * 128, pc * 1024:(pc + 1) * 1024], [], [b.r], b.r)
                        ts("dve", win.t[:, dc, pc * 1024:(pc + 1) * 1024], b.t[:], ngt.t[:, 0, dc:dc + 1], None,
                           ALU.mult, None, [b.r, ngt.r], [win.r])

                hnT = hnT2[0]
                for mt in range(2):
                    xb_ = norm_a(mem[mt * 128:(mt + 1) * 128, :])
                    xbf = norm_b(xb_)
                    norm_c(xbf, hnT.t[:, :, 0:128], [r_hn2[0][0]])
                    for dc in range(8):
                        mm(ps[:, B_Q, :], hnT.t[:, dc, 0:128], wmkv.t[:, dc, :], dc == 0, dc == 7,
                           [r_hn2[0][0], wmkv.r], [rb[B_Q]])
                    kvf = vf_ring.next()
                    cp("act", kvf.t[:], ps[:, B_Q, :], [rb[B_Q]], [kvf.r])
                    dma("act", mvp[mt * 128:(mt + 1) * 128, :], kvf.t[:, 256:512], [kvf.r], [], kvf.r)
                    cp("dve", MV_p.t[:, mt, :], kvf.t[:, 256:512], [kvf.r], [MV_p.r])
                    gnorm(kvf.t[:, 0:256], 4, gmk.t[:, :, :], kvf.r, gmk.r)
                    dma("act", mkp[mt * 128:(mt + 1) * 128, :], kvf.t[:, 0:256], [kvf.r], [], kvf.r)
                    kb = qkb_ring.next()
                    cp("dve", kb.t[:, 0:256], kvf.t[:, 0:256], [kvf.r], [kb.r])
                    vb_ = bank_bf(B_TB)
                    for hp_ in range(2):
                        tr(vb_[:, hp_, :], kb.t[:, hp_ * 128:(hp_ + 1) * 128], [kb.r], [rb[B_TB]])
                    cp("act", MKT_p.t[:, :, mt * 128:(mt + 1) * 128], vb_[:, 0:2, :], [rb[B_TB]], [MKT_p.r])
                    stage_win(3)
                for b in range(4):
                    for mt in range(2):
                        cf = vf_ring.next()
                        dma("sp", cf.t[:, 0:256], cmk[b, mt * 128:(mt + 1) * 128, :], [], [cf.r], cf.r)
                        dma("sp", cf.t[:, 256:512], cmv[b, mt * 128:(mt + 1) * 128, :], [], [cf.r], cf.r)
                        kb = qkb_ring.next()
                        cp("dve", kb.t[:, 0:256], cf.t[:, 0:256], [cf.r], [kb.r])
                        cp("pool", MV_s.t[:, b, mt, :], cf.t[:, 256:512], [cf.r], [MV_s.r])
                        vb_ = bank_bf(B_TB)
                        for hp_ in range(2):
                            tr(vb_[:, hp_, :], kb.t[:, hp_ * 128:(hp_ + 1) * 128], [kb.r], [rb[B_TB]])
                        cp("act", MKT_s.t[:, b, :, mt * 128:(mt + 1) * 128], vb_[:, 0:2, :], [rb[B_TB]], [MKT_s.r])
                        stage_win(2)
                stage_win(24)
                S.barrier()

            def tile_gen(x_dram, k2, col0, rope_ap, rrope, kout, vout, qmT_dst, r_qm, prompt_tg):
                hn = hnT2[k2]
                rh = r_hn2[k2][col0 // 128]
                xb_ = norm_a(x_dram)
                yield
                xbf = norm_b(xb_)
                yield
                norm_c(xbf, hn.t[:, :, col0:col0 + 128], [rh])
                yield
                qkf = qkf_ring.next()
                vf = vf_ring.next()
                for (bk, c0) in ((B_Q, 0), (B_K, 512)):
                    for dc in range(8):
                        mm(ps[:, bk, :], hn.t[:, dc, col0:col0 + 128], win.t[:, dc, c0:c0 + 512], dc == 0, dc == 7,
                           [rh, win.r], [rb[bk]])
                cp("act", qkf.t[:, 0:1024].rearrange("p (a c) -> p a c", c=512), ps[:, B_Q:B_K + 1, :],
                   [rb[B_Q], rb[B_K]], [qkf.r])
                sq = sq_ring.next()
                act(sq.t[:, 0:1024].rearrange("p (a c) -> p a c", c=512), ps[:, B_Q:B_K + 1, :], AF.Square,
                    [rb[B_Q], rb[B_K]], [sq.r])
                yield
                for (bk, c0, n) in ((B_Q, 1024, 512), (B_K, 2560, 256)):
                    for dc in range(8):
                        mm(ps[:, bk, 0:n], hn.t[:, dc, col0:col0 + 128], win.t[:, dc, c0:c0 + n], dc == 0, dc == 7,
                           [rh, win.r], [rb[bk]])
                cp("act", vf.t[:], ps[:, B_Q, :], [rb[B_Q]], [vf.r])
                cp("act", qkf.t[:, 1024:1280], ps[:, B_K, 0:256], [rb[B_K]], [qkf.r])
                act(sq.t[:, 1024:1280], ps[:, B_K, 0:256], AF.Square, [rb[B_K]], [sq.r])
                dma("sp", vout, vf.t[:], [vf.r], [], vf.r)
                if prompt_tg is None:
                    cp("dve", vb_smp.t[:], vf.t[:], [vf.r], [vb_smp.r])
                else:
                    vbb = vb_ring.next()
                    cp("dve", vbb.t[:], vf.t[:], [vf.r], [vbb.r])
                    dma("sp", V_s[prompt_tg * 128:(prompt_tg + 1) * 128, :], vbb.t[:], [vbb.r],
                        [r_v_blk[prompt_tg // 4]], vbb.r)
                yield
                gnorm(qkf.t[:, 0:1280], 20, gfull.t[:, :, :], qkf.r, gfull.r, sq)
                yield
                q3 = qkf.t[:, 0:1024].rearrange("p (g d) -> p g d", d=64)
                x1 = q3[:, :, 0:8]
                x2 = q3[:, :, 8:16]
                cosb = rope_ap[:, 0:8].unsqueeze(1).to_broadcast([128, 16, 8])
                sinb = rope_ap[:, 8:16].unsqueeze(1).to_broadcast([128, 16, 8])
                rtmp = rtmp_ring.next()
                tt("pool", rtmp.t[:, 0], x1, cosb, ALU.mult, [qkf.r, rrope], [rtmp.r])
                tt("pool", rtmp.t[:, 1], x2, sinb, ALU.mult, [qkf.r, rrope], [rtmp.r])
                tt("pool", rtmp.t[:, 2], x2, cosb, ALU.mult, [qkf.r, rrope], [rtmp.r])
                tt("pool", rtmp.t[:, 3], x1, sinb, ALU.mult, [qkf.r, rrope], [rtmp.r])
                tt("pool", x1, rtmp.t[:, 0], rtmp.t[:, 1], ALU.subtract, [rtmp.r], [qkf.r])
                tt("pool", x2, rtmp.t[:, 2], rtmp.t[:, 3], ALU.add, [rtmp.r], [qkf.r])
                dma("sp", kout, qkf.t[:, 512:1024], [qkf.r], [], qkf.r)
                yield
                qkb = qkb_ring.next()
                cp("dve", qkb.t[:], qkf.t[:], [qkf.r], [qkb.r])
                yield
                vB = bank_bf(B_TB)
                for g in range(8):
                    tr(vB[:, g, :], qkb.t[:, g * 128:(g + 1) * 128], [qkb.r], [rb[B_TB]])
                if prompt_tg is None:
                    cp("act", qT_smp.t[:], vB[:, 0:4, :], [rb[B_TB]], [qT_smp.r])
                    cp("act", kT_smp.t[:], vB[:, 4:8, :], [rb[B_TB]], [kT_smp.r])
                else:
                    qkT = qkT_ring.next()
                    cp("dve", qkT.t[:], vB[:, :, :], [rb[B_TB]], [qkT.r])
                    cols = slice(prompt_tg * 128, (prompt_tg + 1) * 128)
                    dma("sp", QT_s[:, :, cols].rearrange("h p t -> p h t"), qkT.t[:, 0:4, :], [qkT.r],
                        [r_qk_blk[prompt_tg // 4]], qkT.r)
                    dma("sp", KT_s[:, :, cols].rearrange("h p t -> p h t"), qkT.t[:, 4:8, :], [qkT.r],
                        [r_qk_blk[prompt_tg // 4]], qkT.r)
                vA = bank_bf(B_TA)
                for g in range(2):
                    tr(vA[:, g, :], qkb.t[:, 1024 + g * 128:1024 + (g + 1) * 128], [qkb.r], [rb[B_TA]])
                cp("act", qmT_dst, vA[:, 0:2, :], [rb[B_TA]], [r_qm])

            lx_p = sbuf(sa, [128, 2, 1, 3 + BT], F32, "lx_p")
            lx_s = sbuf(sa, [128, 2, 4, 3 + 32], F32, "lx_s")
            r_lx_p = [Res("lxp0"), Res("lxp1")]
            r_lx_s = [Res("lxs0"), Res("lxs1")]
            hprev = sbuf(sa, [128, 2], F32, "hprev")
            r_hprev = [Res("hp0"), Res("hp1")]
            h0s = sbuf(sa, [128, 2, 4], F32, "h0s")
            memset("pool", lx_p.t[:, :, :, 0:3], 0.0, r_lx_p)
            memset("pool", hprev.t[:], 0.0, r_hprev)
            for cc in range(2):
                for sb_i in range(4):
                    dma("sp", lx_s.t[:, cc, sb_i, 0:3], sconv[sb_i, :, cc * 128:(cc + 1) * 128].rearrange("j p -> p j"),
                        [], [r_lx_s[cc]], r_lx_s[cc])
                dma("sp", h0s.t[:, cc, :], sh[:, cc * 128:(cc + 1) * 128].rearrange("b p -> p b"), [], [h0s.r], h0s.r)
            fring = Ring(sa, 1, [128, 5, BT], F32, "fw")
            hl_s = sbuf(sa, [128, 2, 4], F32, "hl_s")
            xcb_ring = Ring(sa, 1, [128, BT], BF16, "xcb")
            slg_ring = Ring(sa, 1, [128, 2, BT], BF16, "slg")
            smg_ring = Ring(sa, 1, [128, 2, BT], BF16, "smg")
            gate_ring = Ring(sa, 1, [128, 4, BT], BF16, "gate")
            mbc_ring = Ring(sa, 1, [128, 4, BT], BF16, "mbc")
            e_ring = Ring(sa, 2, [128, BT], BF16, "eT")
            mo_ring = Ring(sa, 1, [128, 1, BT], F32, "mo")
            fbank = [0]

            def next_fbank():
                fbank[0] += 1
                return B_F0 if fbank[0] % 2 else B_F1

            def feat_gen(part, NT, k2, nseg, L, lxb, r_lx, hinit_fn, gate_b, mbc_b, r_mbcB, qmT_ap, rqm, memgroups,
                         last, outs):
                hn = hnT2[k2]
                rhs_ = r_hn2[k2][0:max(1, NT // 128)]

                def proj(col, bk=None):
                    if bk is None:
                        bk = next_fbank()
                    for dc in range(8):
                        mm(ps[:, bk, 0:NT], win.t[:, dc, col:col + 128], hn.t[:, dc, 0:NT], dc == 0, dc == 7,
                           [win.r] + rhs_, [rb[bk]])
                    return bk

                if part == "A":
                    smg = smg_ring.next()
                    for i in range(4):
                        bk = proj(1536 + i * 128)
                        act(gate_b.t[:, i, 0:NT], ps[:, bk, 0:NT], AF.Silu, [rb[bk]], [gate_b.r])
                    for hp_ in range(2):
                        bk = proj(2816 + hp_ * 128)
                        act(smg.t[:, hp_, 0:NT], ps[:, bk, 0:NT], AF.Silu, [rb[bk]], [smg.r])
                    yield
                    for _ in mem_part(NT, mbc_b, smg, qmT_ap, rqm, memgroups):
                        yield
                    return
                slg = slg_ring.next()
                for cc in range(2):
                    proj(2048 + cc * 128, B_TA)
                    cp("act", lxb.t[:, cc, :, 3:3 + L], ps[:, B_TA, 0:NT].rearrange("p (s l) -> p s l", l=L),
                       [rb[B_TA]], [r_lx[cc]])
                    yield
                for cc in range(2):
                    proj(2304 + cc * 128, B_TB)
                    act(slg.t[:, cc, 0:NT], ps[:, B_TB, 0:NT], AF.Silu, [rb[B_TB]], [slg.r])
                    yield
                for cc in range(2):
                    fw = fring.next()
                    xc = fw.t[:, 0, 0:NT]
                    xc3 = xc.rearrange("p (s l) -> p s l", l=L)
                    ts("pool", xc3, lxb.t[:, cc, :, 0:L], lv.t[:, cc, 0:1], lv.t[:, cc, 4:5], ALU.mult, ALU.add,
                       [r_lx[cc], lv.r], [fw.r])
                    for j in range(1, 4):
                        stt("dve", xc3, lxb.t[:, cc, :, j:j + L], lv.t[:, cc, j:j + 1], xc3, ALU.mult, ALU.add,
                            [r_lx[cc], lv.r, fw.r], [fw.r])
                    xcb = xcb_ring.next()
                    cp("dve", xcb.t[:, 0:NT], xc, [fw.r], [xcb.r])
                    yield
                    bka, bkx = B_TA, B_TB
                    mm(ps[:, bka, 0:NT], wbd.t[:, 0, cc, :], xcb.t[:, 0:NT], True, True, [wbd.r, xcb.r], [rb[bka]])
                    mm(ps[:, bkx, 0:NT], wbd.t[:, 1, cc, :], xcb.t[:, 0:NT], True, True, [wbd.r, xcb.r], [rb[bkx]])
                    act(fw.t[:, 1, 0:NT], ps[:, bka, 0:NT], AF.Sigmoid, [rb[bka], lv.r], [fw.r], bias=lv.t[:, cc, 5:6])
                    act(fw.t[:, 2, 0:NT], ps[:, bkx, 0:NT], AF.Sigmoid, [rb[bkx], lv.r], [fw.r], bias=lv.t[:, cc, 6:7])
                    yield
                    a_ = fw.t[:, 3, 0:NT]
                    tmp = fw.t[:, 4, 0:NT]
                    h_ = fw.t[:, 1, 0:NT]
                    act(a_, fw.t[:, 1, 0:NT], AF.Exp, [fw.r, lv.r], [fw.r], scale=lv.t[:, cc, 10:11])
                    tt("pool", tmp, a_, a_, ALU.mult, [fw.r], [fw.r])
                    act(tmp, tmp, AF.Ln, [fw.r, cst.r], [fw.r], bias=one_t, scale=-1.0)
                    act(tmp, tmp, AF.Exp, [fw.r], [fw.r], scale=0.5)
                    tt("dve", fw.t[:, 2, 0:NT], fw.t[:, 2, 0:NT], fw.t[:, 0, 0:NT], ALU.mult, [fw.r], [fw.r])
                    tt("dve", tmp, tmp, fw.t[:, 2, 0:NT], ALU.mult, [fw.r], [fw.r])
                    yield
                    for s_ in range(nseg):
                        scan(h_[:, s_ * L:(s_ + 1) * L], a_[:, s_ * L:(s_ + 1) * L], tmp[:, s_ * L:(s_ + 1) * L],
                             hinit_fn(cc, s_), [fw.r, r_hprev[cc], h0s.r], [fw.r])
                    tt("dve", mbc_b.t[:, cc, 0:NT], h_, slg.t[:, cc, 0:NT], ALU.mult, [fw.r, slg.r], [r_mbcB])
                    if nseg == 1:
                        cp("pool", hprev.t[:, cc:cc + 1], h_[:, L - 1:L], [fw.r], [r_hprev[cc]])
                        if last:
                            dma("sp", outs["conv"][:, cc * 128:(cc + 1) * 128].rearrange("j p -> p j"),
                                lxb.t[:, cc, 0, L:L + 3], [r_lx[cc]], [], r_lx[cc])
                            dma("sp", outs["h"][cc * 128:(cc + 1) * 128].rearrange("(p o) -> p o", o=1),
                                hprev.t[:, cc:cc + 1], [r_hprev[cc]], [], r_hprev[cc])
                        else:
                            cp("pool", lxb.t[:, cc, 0, 0:3], lxb.t[:, cc, 0, L:L + 3], [r_lx[cc]], [r_lx[cc]])
                    else:
                        for sb_i in range(nseg):
                            dma("sp", outs["conv"][sb_i, :, cc * 128:(cc + 1) * 128].rearrange("j p -> p j"),
                                lxb.t[:, cc, sb_i, L:L + 3], [r_lx[cc]], [], r_lx[cc])
                        hlast = hl_s.t[:, cc, 0:nseg]
                        cp("pool", hlast, h_.rearrange("p (s l) -> p s l", l=L)[:, :, L - 1], [fw.r], [hl_s.r])
                        dma("sp", outs["h"][:, cc * 128:(cc + 1) * 128].rearrange("b p -> p b"), hlast, [hl_s.r], [],
                            hl_s.r)
                    yield
            def mem_part(NT, mbc_b, smg, qmT_ap, rqm, memgroups):
                for hp_ in range(2):
                    mo = mo_ring.next()
                    its = []
                    for (c0, n, mkt_fn, mv_fn, rmk, rmv) in memgroups:
                        subs = [(c0 + k_ * 256, 256) for k_ in range(n // 256)] if n > 256 else [(c0, n)]
                        for (cc0, nn) in subs:
                            for hh in range(2):
                                its.append((cc0, nn, hh, mkt_fn, mv_fn, rmk, rmv))
                    ebuf = {}

                    def mS(k_, hp_=hp_, its=its, ebuf=ebuf):
                        cc0, nn, hh, mkt_fn, mv_fn, rmk, rmv = its[k_]
                        bk = next_fbank()
                        for mt in range(2):
                            mm(ps[:, bk, mt * nn:(mt + 1) * nn], mkt_fn(hp_, mt)[hh * 64:(hh + 1) * 64, :],
                               qmT_ap[hh * 64:(hh + 1) * 64, hp_, cc0:cc0 + nn], True, True, [rmk, rqm], [rb[bk]])
                        eb = e_ring.next()
                        act(eb.t[:, 0:2 * nn], ps[:, bk, 0:2 * nn], AF.Exp, [rb[bk], sv.r], [eb.r], bias=nshift_m,
                            scale=0.125)
                        ebuf[k_] = eb

                    def mPV(k_, hp_=hp_, its=its, ebuf=ebuf):
                        cc0, nn, hh, mkt_fn, mv_fn, rmk, rmv = its[k_]
                        h = 2 * hp_ + hh
                        eb = ebuf.pop(k_)
                        for mt in range(2):
                            mm(ps[hh * 64:(hh + 1) * 64, B_MO, cc0:cc0 + nn], mv_fn(mt)[:, h * 64:(h + 1) * 64],
                               eb.t[:, mt * nn:(mt + 1) * nn], mt == 0, mt == 1, [rmv, eb.r], [rb[B_MO]])
                        for mt in range(2):
                            mm(ps[hh * 64:(hh + 1) * 64, B_MR, cc0:cc0 + nn], onesb.t[:, 0:64],
                               eb.t[:, mt * nn:(mt + 1) * nn], mt == 0, mt == 1, [onesb.r, eb.r], [rb[B_MR]])

                    mS(0)
                    for k_ in range(len(its)):
                        if k_ + 1 < len(its):
                            mS(k_ + 1)
                        mPV(k_)
                        yield
                    recip(mo.t[:, 0, 0:NT], ps[:, B_MR, 0:NT], [rb[B_MR]], [mo.r])
                    tt("dve", mo.t[:, 0, 0:NT], ps[:, B_MO, 0:NT], mo.t[:, 0, 0:NT], ALU.mult, [rb[B_MO], mo.r], [mo.r])
                    tt("dve", mbc_b.t[:, 2 + hp_, 0:NT], mo.t[:, 0, 0:NT], smg.t[:, hp_, 0:NT], ALU.mult,
                       [mo.r, smg.r], [mbc_b.r])
                    yield

            tiles = []
            for bi in range(NBLK):
                for ti in range(4):
                    tiles.append((bi, ti))
            tiles.append((NBLK, 0))
            tile_i = [0]
            active = []
            tiles_left = {bi: 4 for bi in range(NBLK)}
            tiles_left[NBLK] = 1
            feat_q = []
            feat_done = set()
            cur_feat = [None]
            FEAT_STEPS = 1

            def make_tile(bi, ti):
                k2 = bi % 2
                if bi < NBLK:
                    tg = bi * 4 + ti
                    rp_ = rope_ring.next()
                    dma("sp", rp_.t[:], ropep[:, tg * 16:(tg + 1) * 16], [], [rp_.r], rp_.r)
                    return tile_gen(xp[tg * 128:(tg + 1) * 128, :], k2, ti * 128, rp_.t[:, :], rp_.r,
                                    kp[tg * 128:(tg + 1) * 128, :], vp[tg * 128:(tg + 1) * 128, :],
                                    qmT2[k2].t[:, :, ti * 128:(ti + 1) * 128], qmT2[k2].r, tg)
                return tile_gen(xs, k2, 0, rope_s.t[:, :], rope_s.r, kso, vso, qmT2[k2].t[:, :, 0:128], qmT2[k2].r, None)

            def make_feat(bi):
                k2 = bi % 2
                r_mbcB = Res("mbcB%d" % bi)
                if bi < NBLK:
                    gate_b = gate_ring.next(); mbc_b = mbc_ring.next()
                    cols = slice(bi * BT, (bi + 1) * BT)

                    def post(gate_b=gate_b, mbc_b=mbc_b, cols=cols, bi=bi, r_mbcB=r_mbcB):
                        dma("sp", G_s[:, :, cols].rearrange("h p t -> p h t"), gate_b.t[:], [gate_b.r], [r_g_blk[bi]],
                            gate_b.r)
                        dma("sp", MBC_s[:, :, cols].rearrange("h p t -> p h t"), mbc_b.t[:], [mbc_b.r, r_mbcB],
                            [r_mbc_blk[bi]], mbc_b.r)

                    args = (BT, k2, 1, BT, lx_p, r_lx_p, lambda cc, s_: hprev.t[:, cc:cc + 1], gate_b, mbc_b, r_mbcB,
                            qmT2[k2].t, qmT2[k2].r,
                            [(0, BT, lambda hp_, mt: MKT_p.t[:, hp_, mt * 128:(mt + 1) * 128],
                              lambda mt: MV_p.t[:, mt, :], MKT_p.r, MV_p.r)],
                            bi == NBLK - 1, {"conv": cpo, "h": hpo})
                    return [bi, feat_gen("A", *args), feat_gen("B", *args), post, mbc_b, r_mbcB]

                def mk_grp(b):
                    return (32 * b, 32, lambda hp_, mt: MKT_s.t[:, b, hp_, mt * 128:(mt + 1) * 128],
                            lambda mt: MV_s.t[:, b, mt, :], MKT_s.r, MV_s.r)

                args = (128, k2, 4, 32, lx_s, r_lx_s, lambda cc, s_: h0s.t[:, cc, s_:s_ + 1], gate_smp, mbc_smp, r_mbcB,
                        qmT2[k2].t, qmT2[k2].r, [mk_grp(b) for b in range(4)], True, {"conv": cso, "h": hso})
                return [bi, feat_gen("A", *args), feat_gen("B", *args), None, mbc_smp, r_mbcB]

            while tile_i[0] < len(tiles) or active or feat_q or cur_feat[0] is not None:
                while len(active) < 4 and tile_i[0] < len(tiles):
                    bi, ti = tiles[tile_i[0]]
                    if bi >= 2 and (bi - 2) not in feat_done:
                        break
                    if active and min(a_[2][0] for a_ in active) < 2:
                        break
                    active.append((bi, make_tile(bi, ti), [0]))
                    tile_i[0] += 1
                for item in list(active):
                    bi, g, cnt_ = item
                    cnt_[0] += 1
                    try:
                        next(g)
                    except StopIteration:
                        active.remove(item)
                        tiles_left[bi] -= 1
                        if tiles_left[bi] == 0:
                            feat_q.append(bi)
                if cur_feat[0] is None and feat_q:
                    cur_feat[0] = make_feat(feat_q.pop(0))
                if cur_feat[0] is not None:
                    cf = cur_feat[0]
                    for gi in (1, 2):
                        if cf[gi] is not None:
                            try:
                                next(cf[gi])
                            except StopIteration:
                                cf[gi] = None
                    if cf[1] is None and cf[2] is None:
                        if cf[3] is not None:
                            cf[3]()
                        feat_done.add(cf[0])
                        cur_feat[0] = None
            S.barrier()

        with ExitStack() as sb_:
            KT_ring = Ring(sb_, 2, [128, T], BF16, "KTh")
            V_ring = Ring(sb_, 2, [128, T // 128, 128], BF16, "Vh")
            QT_ring = Ring(sb_, 4, [128, BT], BF16, "QTb")
            E_ring = Ring(sb_, 3, [128, 2, 32], BF16, "E")
            ep_ring = Ring(sb_, 2, [128, 4, 32], F32, "ep")
            ot_ring = Ring(sb_, 2, [128, 32], BF16, "ot")
            SB = [(0, 1), (2, 3)]
            B_O = (4, 5)
            B_R = (6, 7)

            acc = sbuf(sb_, [128, 2, 32], F32, "acc")
            r_acc = [Res("acc0"), Res("acc1")]
            hl_ring = Ring(sb_, 2, [128, 2, 2, 32], BF16, "hl")
            r_hl = {}

            def attn_unit(steps, kt_fn, v_fn, qt_fn, rk, rv, rq, NQ, out_ap, rout):
                n = len(steps)
                ebs = {}
                ceng = ("dve", "pool")

                def issue_S(j):
                    ksz, q0, _, _ = steps[j]
                    pair = SB[j % 2]
                    for c in range(2):
                        mm(ps[0:ksz, pair[c], q0:NQ], kt_fn(j, c), qt_fn(c)[:, q0:NQ], True, True,
                           [rk, rq], [rb[pair[c]]])
                    eb = E_ring.next()
                    act(eb.t[0:ksz, :, q0:NQ], ps[0:ksz, pair[0]:pair[1] + 1, q0:NQ], AF.Exp,
                        [rb[pair[0]], rb[pair[1]], sv.r], [eb.r], bias=nshift_d[0:ksz, :], scale=0.125)
                    ebs[j] = eb

                def issue_PV(j):
                    ksz, q0, mq0, pm = steps[j]
                    eb = ebs.pop(j)
                    if mq0 is not None:
                        memset("pool", eb.t[64:128, :, mq0:mq0 + 64], 0.0, [eb.r])
                    for (p0, p1) in pm:
                        memset("pool", eb.t[p0:p1, :, q0:NQ], 0.0, [eb.r])
                    for c in range(2):
                        mm(ps[:, B_O[c], q0:NQ], v_fn(j), eb.t[0:ksz, c, q0:NQ], j == 0, j == n - 1,
                           [rv, eb.r], [rb[B_O[c]]])
                    if j == 0:
                        cp("dve", acc.t[:, 0, 0:NQ], eb.t[:, 0, 0:NQ], [eb.r], [r_acc[0]])
                    else:
                        tt("dve", acc.t[:, 0, q0:NQ], acc.t[:, 0, q0:NQ], eb.t[:, 0, q0:NQ], ALU.add,
                           [eb.r, r_acc[0]], [r_acc[0]])
                    mm(ps[:, B_R[1], q0:NQ], onesb.t[0:ksz, :], eb.t[0:ksz, 1, q0:NQ], j == 0, j == n - 1,
                       [onesb.r, eb.r], [rb[B_R[1]]])

                issue_S(0)
                for j in range(n):
                    if j + 1 < n:
                        issue_S(j + 1)
                    issue_PV(j)
                hl = hl_ring.next()
                if id(hl) not in r_hl:
                    r_hl[id(hl)] = [Res("hl0"), Res("hl1")]
                rh = r_hl[id(hl)]
                for c in range(1):
                    cp(ceng[c], hl.t[:, c, 0, 0:NQ], acc.t[:, c, 0:NQ], [r_acc[c]], [rh[c]])
                    tt(ceng[c], hl.t[:, c, 1, 0:NQ], acc.t[:, c, 0:NQ], hl.t[:, c, 0, 0:NQ], ALU.subtract,
                       [r_acc[c], rh[c]], [rh[c]])
                    mm(ps[:, B_R[c], 0:NQ], onesb.t[:], hl.t[:, c, 0, 0:NQ], True, False, [onesb.r, rh[c]], [rb[B_R[c]]])
                    mm(ps[:, B_R[c], 0:NQ], onesb.t[:], hl.t[:, c, 1, 0:NQ], False, True, [onesb.r, rh[c]], [rb[B_R[c]]])
                ep = ep_ring.next()
                cp("dve", ep.t[:, 0, 0:NQ], ps[:, B_O[0], 0:NQ], [rb[B_O[0]]], [ep.r])
                cp("dve", ep.t[:, 1, 0:NQ], ps[:, B_O[1], 0:NQ], [rb[B_O[1]]], [ep.r])
                cp("dve", ep.t[:, 2, 0:NQ], ps[:, B_R[1], 0:NQ], [rb[B_R[1]]], [ep.r])
                recip(ep.t[:, 3, 0:NQ], ps[:, B_R[0], 0:NQ], [rb[B_R[0]]], [ep.r])
                recip(ep.t[:, 2, 0:NQ], ep.t[:, 2, 0:NQ], [ep.r], [ep.r])
                tt("dve", ep.t[:, 0, 0:NQ], ep.t[:, 0, 0:NQ], ep.t[:, 3, 0:NQ], ALU.mult, [ep.r], [ep.r])
                tt("dve", ep.t[:, 1, 0:NQ], ep.t[:, 1, 0:NQ], ep.t[:, 2, 0:NQ], ALU.mult, [ep.r], [ep.r])
                stt("dve", out_ap, ep.t[:, 1, 0:NQ], neglam, ep.t[:, 0, 0:NQ], ALU.mult, ALU.add, [ep.r, sv.r], [rout])

            ckst = Ring(sb_, 3, [128, 512], F32, "ckst")
            ckb_ring = Ring(sb_, 2, [128, 512], BF16, "ckb")
            KTs_ring = Ring(sb_, 2, [128, 4, PAST], BF16, "KTs")
            Vs_ring = Ring(sb_, 2, [128, PAST // 128, 512], BF16, "Vs")
            for b in range(4):
                KTs = KTs_ring.next()
                Vs = Vs_ring.next()
                for kt in range(PAST // 128):
                    cf = ckst.next()
                    dma("sp", cf.t[:], ck[b, kt * 128:(kt + 1) * 128, :], [], [cf.r], cf.r)
                    cb = ckb_ring.next()
                    cp("dve", cb.t[:], cf.t[:], [cf.r], [cb.r])
                    bk = SB[kt % 2][0]
                    vw = bank_bf(bk)
                    for h in range(4):
                        tr(vw[:, h, :], cb.t[:, h * 128:(h + 1) * 128], [cb.r], [rb[bk]])
                    cp("dve", KTs.t[:, :, kt * 128:(kt + 1) * 128], vw[:, 0:4, :], [rb[bk]], [KTs.r])
                    cf2 = ckst.next()
                    dma("sp", cf2.t[:], cv[b, kt * 128:(kt + 1) * 128, :], [], [cf2.r], cf2.r)
                    cp("pool", Vs.t[:, kt, :], cf2.t[:], [cf2.r], [Vs.r])
                for h in range(4):
                    steps = [(128, 0, None, []) for _ in range(PAST // 128)]
                    pm = [(32 * qd, 32 * qd + 32) for qd in range(4) if qd != b]
                    steps.append((128, 0, None, pm))

                    def kt_fn(j, c, KTs=KTs, h=h):
                        if j < PAST // 128:
                            return KTs.t[c * 64:(c + 1) * 64, h, j * 128:(j + 1) * 128]
                        return kT_smp.t[c * 64:(c + 1) * 64, h, :]

                    def v_fn(j, Vs=Vs, h=h):
                        if j < PAST // 128:
                            return Vs.t[:, j, h * 128:(h + 1) * 128]
                        return vb_smp.t[:, h * 128:(h + 1) * 128]

                    def qt_fn(c, h=h, b=b):
                        return qT_smp.t[c * 64:(c + 1) * 64, h, 32 * b:32 * b + 32]

                    attn_unit(steps, kt_fn, v_fn, qt_fn, KTs.r, Vs.r, qT_smp.r, 32,
                              oT_smp.t[:, h, 32 * b:32 * b + 32], oT_smp.r)

            QB = 256
            NQB = T // QB
            SP3 = [(0, 1), (2, 3), (4, 5)]
            OB = 6
            RB = 7
            E4_ring = Ring(sb_, 8, [128, 2, 2, QB], BF16, "E4")
            acc2 = sbuf(sb_, [128, 2, QB], F32, "acc2")
            hl2_ring = Ring(sb_, 2, [128, 2, 2, QB], BF16, "hl2")
            ep2_ring = Ring(sb_, 2, [128, 2, 2 * QB], F32, "ep2")
            ot2_ring = Ring(sb_, 3, [128, QB], BF16, "ot2")
            heads = {}

            def load_head(h):
                KTh = KT_ring.next(); Vh = V_ring.next()
                dma("sp", KTh.t[:], KT_s[h], r_qk_blk, [KTh.r], KTh.r)
                for part in range(4):
                    dma("sp", Vh.t[:, part * 16:(part + 1) * 16, :],
                        V_s[part * 2048:(part + 1) * 2048, h * 128:(h + 1) * 128].rearrange("(t p) e -> p t e", p=128),
                        r_v_blk, [Vh.r], Vh.r)
                heads[h] = (KTh, Vh)

            gsteps = []
            for h in range(4):
                for i in range(NQB):
                    for p in range(i + 1):
                        gsteps.append((h, i, p))
            qblk = {}

            def ensure_q(h, i):
                key = (h, i // 2)
                if key not in qblk:
                    qb = QT_ring.next()
                    dma("sp", qb.t[:], QT_s[h, :, (i // 2) * BT:(i // 2 + 1) * BT], [r_qk_blk[i // 2]], [qb.r], qb.r)
                    qblk[key] = qb
                return qblk[key]

            ebs = {}

            def issue_S2(g):
                h, i, p = gsteps[g]
                if p == 0:
                    ensure_q(h, i)
                    if i + 2 < NQB:
                        ensure_q(h, i + 2)
                KTh, Vh = heads[h]
                qb = ensure_q(h, i)
                pair = SP3[g % 3]
                qo = (i % 2) * QB
                for t in range(2):
                    kt = 2 * p + t
                    for c in range(2):
                        mm(ps[:, pair[c], t * QB:(t + 1) * QB], KTh.t[c * 64:(c + 1) * 64, kt * 128:(kt + 1) * 128],
                           qb.t[c * 64:(c + 1) * 64, qo:qo + QB], True, True, [KTh.r, qb.r], [rb[pair[c]]])
                eb = E4_ring.next()
                act(eb.t[:].rearrange("p c t q -> p c (t q)"), ps[:, pair[0]:pair[1] + 1, :], AF.Exp,
                    [rb[pair[0]], rb[pair[1]], sv.r], [eb.r], bias=nshift_d, scale=0.125)
                ebs[g] = eb

            def issue_PV2(g):
                h, i, p = gsteps[g]
                KTh, Vh = heads[h]
                eb = ebs.pop(g)
                if p == i:
                    memset("pool", eb.t[64:128, :, 0, 0:64], 0.0, [eb.r])
                    memset("pool", eb.t[:, :, 1, 0:128], 0.0, [eb.r])
                    memset("pool", eb.t[64:128, :, 1, 128:192], 0.0, [eb.r])
                for t in range(2):
                    kt = 2 * p + t
                    for c in range(2):
                        mm(ps[:, OB, c * QB:(c + 1) * QB], Vh.t[:, kt, :], eb.t[:, c, t, :],
                           p == 0 and t == 0 and c == 0, p == i and t == 1, [Vh.r, eb.r], [rb[OB]], skip=True)
                for t in range(2):
                    mm(ps[:, RB, 0:QB], onesb.t[:], eb.t[:, 1, t, :], p == 0 and t == 0, False, [onesb.r, eb.r], [rb[RB]],
                       skip=True)
                if p == 0:
                    cp("dve", acc2.t[:, :, :], eb.t[:, 0, :, :], [eb.r], [acc2.r])
                else:
                    tt("dve", acc2.t[:, :, :], acc2.t[:, :, :], eb.t[:, 0, :, :], ALU.add, [eb.r, acc2.r], [acc2.r])
                if p == i:
                    flush_pending(NG)
                    ep = ep2_ring.next()
                    cp("dve", ep.t[:, 0, :], ps[:, OB, :], [rb[OB]], [ep.r])
                    cp("dve", ep.t[:, 1, 0:QB], ps[:, RB, 0:QB], [rb[RB]], [ep.r])
                    recip(ep.t[:, 1, 0:QB], ep.t[:, 1, 0:QB], [ep.r], [ep.r])
                    hl = hl2_ring.next()
                    cp("dve", hl.t[:, 0, :, :], acc2.t[:, :, :], [acc2.r], [hl.r])
                    tt("dve", hl.t[:, 1, :, :], acc2.t[:, :, :], hl.t[:, 0, :, :], ALU.subtract, [acc2.r, hl.r], [hl.r])

                    def rest(ep=ep, hl=hl, h=h, i=i):
                        for k_ in range(2):
                            for t in range(2):
                                mm(ps[:, RB, QB:2 * QB], onesb.t[:], hl.t[:, k_, t, :], False, k_ == 1 and t == 1,
                                   [onesb.r, hl.r], [rb[RB]], skip=True)
                        cp("dve", ep.t[:, 1, QB:2 * QB], ps[:, RB, QB:2 * QB], [rb[RB]], [ep.r])
                        memset("dve", ps[:, RB, QB:2 * QB], 0.0, [rb[RB]])
                        recip(ep.t[:, 1, QB:2 * QB], ep.t[:, 1, QB:2 * QB], [ep.r], [ep.r])
                        tt("pool", ep.t[:, 0, 0:QB], ep.t[:, 0, 0:QB], ep.t[:, 1, QB:2 * QB], ALU.mult, [ep.r], [ep.r])
                        tt("pool", ep.t[:, 0, QB:2 * QB], ep.t[:, 0, QB:2 * QB], ep.t[:, 1, 0:QB], ALU.mult, [ep.r], [ep.r])
                        ot = ot2_ring.next()
                        stt("dve", ot.t[:], ep.t[:, 0, QB:2 * QB], neglam, ep.t[:, 0, 0:QB], ALU.mult, ALU.add,
                            [ep.r, sv.r], [ot.r])
                        dma("sp", OT_s[h, :, i * QB:(i + 1) * QB], ot.t[:], [ot.r], [r_ot[h][i // 2]], ot.r)

                    pending.append((g + 2, rest))
                    if i == NQB - 1 and h + 2 < 4:
                        load_head(h + 2)

            pending = []

            def flush_pending(gnow):
                while pending and pending[0][0] <= gnow:
                    pending.pop(0)[1]()

            memset("dve", ps[:, RB, QB:2 * QB], 0.0, [rb[RB]])
            load_head(0)
            load_head(1)
            NG = len(gsteps)
            issue_S2(0)
            issue_S2(1)
            for g in range(NG):
                if g + 2 < NG:
                    issue_S2(g + 2)
                issue_PV2(g)
                flush_pending(g)
            flush_pending(NG + 10)
            S.barrier()

        with ExitStack() as sc_:
            wout = sbuf(sc_, [128, 8, D], BF16, "wout")
            with ExitStack() as s0:
                wst = Ring(s0, 2, [128, D], F32, "wst2")
                for dc in range(8):
                    b = wst.next()
                    dma("sp", b.t[:], w_out[dc * 128:(dc + 1) * 128, :], [], [b.r], b.r)
                    cp("dve" if dc % 2 == 0 else "pool", wout.t[:, dc, :], b.t[:], [b.r], [wout.r])
                S.barrier()
            o_ring = Ring(sc_, 2, [128, 4, BT], BF16, "o3")
            g_ring = Ring(sc_, 2, [128, 4, BT], BF16, "g3")
            m_ring = Ring(sc_, 3, [128, 4, BT], BF16, "m3")
            oa_ring = Ring(sc_, 3, [128, 4, BT], BF16, "oa3")
            osq_ring = Ring(sc_, 2, [128, BT], BF16, "osq")
            rs_ring = Ring(sc_, 2, [128, 2, BT], F32, "rs3")
            x_ring = Ring(sc_, 3, [128, D], F32, "x3")
            y_ring = Ring(sc_, 3, [128, D], F32, "y3")
            B_ST = (0, 1)
            B_Y = ((2, 3), (4, 5), (6, 7))
            cnt = [0, 0]

            def out_stats(NT, o_b, g_b):
                oa = oa_ring.next()
                for h in range(4):
                    osq = osq_ring.next()
                    tt("pool", osq.t[:, 0:NT], o_b.t[:, h, 0:NT], o_b.t[:, h, 0:NT], ALU.mult, [o_b.r], [osq.r])
                    bk = B_ST[cnt[0] % 2]
                    cnt[0] += 1
                    mm(ps[:, bk, 0:NT], onesm.t[:], osq.t[:, 0:NT], True, True, [onesm.r, osq.r], [rb[bk]])
                    rs = rs_ring.next()
                    act(rs.t[:, 0, 0:NT], ps[:, bk, 0:NT], AF.Ln, [rb[bk], cst.r], [rs.r], bias=eps_t)
                    act(rs.t[:, 1, 0:NT], rs.t[:, 0, 0:NT], AF.Exp, [rs.r], [rs.r], scale=-0.5)
                    tt("dve", rs.t[:, 0, 0:NT], o_b.t[:, h, 0:NT], rs.t[:, 1, 0:NT], ALU.mult, [o_b.r, rs.r], [rs.r])
                    stt("dve", oa.t[:, h, 0:NT], rs.t[:, 0, 0:NT], subg_t, g_b.t[:, h, 0:NT], ALU.mult, ALU.mult,
                        [rs.r, sg.r, g_b.r], [oa.r])
                return oa

            def out_tiles(NT, oa, m_b, x_dram, y_dram):
                for ti in range(NT // 128):
                    xb_ = x_ring.next()
                    dma("sp", xb_.t[:], x_dram[ti * 128:(ti + 1) * 128, :], [], [xb_.r], xb_.r)
                    pair = B_Y[cnt[1] % 3]
                    cnt[1] += 1
                    for half in range(2):
                        bk = pair[half]
                        for dc in range(8):
                            lhs = oa.t[:, dc, ti * 128:(ti + 1) * 128] if dc < 4 else m_b.t[:, dc - 4, ti * 128:(ti + 1) * 128]
                            mm(ps[:, bk, :], lhs, wout.t[:, dc, half * 512:(half + 1) * 512], dc == 0, dc == 7,
                               [oa.r, m_b.r, wout.r], [rb[bk]])
                    yb = y_ring.next()
                    for half in range(2):
                        tt("dve", yb.t[:, half * 512:(half + 1) * 512], ps[:, pair[half], :],
                           xb_.t[:, half * 512:(half + 1) * 512], ALU.add, [rb[pair[half]], xb_.r], [yb.r])
                    dma("act", y_dram[ti * 128:(ti + 1) * 128, :], yb.t[:], [yb.r], [], yb.r)

            def load_blk(bi):
                cols = slice(bi * BT, (bi + 1) * BT)
                o_b = o_ring.next(); g_b = g_ring.next(); m_b = m_ring.next()
                dma("sp", o_b.t[:], OT_s[:, :, cols].rearrange("h p t -> p h t"), [r_ot[h][bi] for h in range(4)],
                    [o_b.r], o_b.r)
                dma("sp", g_b.t[:], G_s[:, :, cols].rearrange("h p t -> p h t"), [r_g_blk[bi]], [g_b.r], g_b.r)
                dma("sp", m_b.t[:], MBC_s[:, :, cols].rearrange("h p t -> p h t"), [r_mbc_blk[bi]], [m_b.r], m_b.r)
                return o_b, g_b, m_b

            oa_s = out_stats(128, oT_smp, gate_smp)
            blk = load_blk(0)
            oa_n = out_stats(BT, blk[0], blk[1])
            out_tiles(128, oa_s, mbc_smp, xs, ys)
            for bi in range(NBLK):
                cur_oa, cur_m = oa_n, blk[2]
                if bi + 1 < NBLK:
                    blk = load_blk(bi + 1)
                    oa_n = out_stats(BT, blk[0], blk[1])
                out_tiles(BT, cur_oa, cur_m, xp[bi * BT:(bi + 1) * BT, :], yp[bi * BT:(bi + 1) * BT, :])

        S.emit()
    return nc


_NC_CACHE = {}


def _rope_table(pos):
    half = 8
    inv = ROPE_THETA ** (-np.arange(0, 16, 2, dtype=np.float32) / np.float32(16))
    ang = pos.astype(np.float32)[:, None] * inv[None, :].astype(np.float32)
    return np.concatenate([np.cos(ang), np.sin(ang)], axis=1).astype(np.float32)


def kernel(x_prompt, x_sample, mem_prompt, cache_diff_k, cache_diff_v, cache_mem_k, cache_mem_v,
           state_lru_conv, state_lru_h, norm_g, w_in, da_q_norm_g, da_k_norm_g, lambda_q1, lambda_k1,
           lambda_q2, lambda_k2, da_subln_g, lru_conv_w, lru_conv_b, lru_w_a, lru_b_a, lru_w_x, lru_b_x,
           lru_lambda, mem_norm_g, w_mem_kv, mx_q_norm_g, mx_k_norm_g, w_out):
    f = lambda a: np.ascontiguousarray(np.asarray(a, dtype=np.float32))
    if "nc" not in _NC_CACHE:
        _NC_CACHE["nc"] = build_nc()
    nc = _NC_CACHE["nc"]
    shared = {
        "w_in": f(w_in[0]), "w_out": f(w_out[0]), "w_mkv": f(w_mem_kv[0]),
        "norm_g": f(norm_g[0]), "mem_norm_g": f(mem_norm_g[0]),
        "qg": f(da_q_norm_g[0]), "kg": f(da_k_norm_g[0]), "mqg": f(mx_q_norm_g[0]), "mkg": f(mx_k_norm_g[0]),
        "lams": f(np.stack([np.asarray(lambda_q1[0]), np.asarray(lambda_k1[0]), np.asarray(lambda_q2[0]),
                            np.asarray(lambda_k2[0])])),
        "subg": f(da_subln_g[0]), "convw": f(lru_conv_w[0]), "convb": f(lru_conv_b[0]),
        "b_a": f(lru_b_a[0]), "b_x": f(lru_b_x[0]), "lrul": f(lru_lambda[0]),
        "w_a": f(lru_w_a[0]), "w_x": f(lru_w_x[0]),
        "ident": np.eye(128, dtype=np.float32),
        "ropep": np.ascontiguousarray(_rope_table(np.arange(T)).reshape(T // 128, 128, 16).transpose(1, 0, 2).reshape(128, -1)),
        "ropes": _rope_table(PAST + (np.arange(128) % 32)),
    }
    xp_ = np.asarray(x_prompt); xs_ = np.asarray(x_sample); mem_ = np.asarray(mem_prompt)
    ck_ = np.asarray(cache_diff_k)[0]; cv_ = np.asarray(cache_diff_v)[0]
    cmk_ = np.asarray(cache_mem_k)[0]; cmv_ = np.asarray(cache_mem_v)[0]
    sc_ = np.asarray(state_lru_conv)[0]; sh_ = np.asarray(state_lru_h)[0]
    in_maps = []
    for c in range(8):
        sl = slice(4 * c, 4 * c + 4)
        m = dict(shared)
        m["xp"] = f(xp_[c]); m["xs"] = f(xs_[sl].reshape(128, D)); m["mem"] = f(mem_[c])
        m["ck"] = f(ck_[sl].reshape(4, PAST, 512)); m["cv"] = f(cv_[sl].reshape(4, PAST, 512))
        m["cmk"] = f(cmk_[sl].reshape(4, 256, 256)); m["cmv"] = f(cmv_[sl].reshape(4, 256, 256))
        m["sconv"] = f(sc_[sl]); m["sh"] = f(sh_[sl])
        in_maps.append(m)
    res = run_bass_kernel_spmd(nc, in_maps, core_ids=list(range(8)))
    R = res.results
    cat = lambda k: np.stack([np.asarray(r[k]) for r in R])
    y_p = cat("yp")
    y_s = cat("ys").reshape(32, 32, D)
    k_p = cat("kp").reshape(1, 8, T, 4, 2, 64)
    v_p = cat("vp").reshape(1, 8, T, 4, 128)
    mk_p = cat("mkp").reshape(1, 8, 256, 4, 64)
    mv_p = cat("mvp").reshape(1, 8, 256, 4, 64)
    c_p = cat("cpo").reshape(1, 8, 3, 256)
    h_p = cat("hpo").reshape(1, 8, 256)
    k_s = cat("kso").reshape(1, 32, 32, 4, 2, 64)
    v_s = cat("vso").reshape(1, 32, 32, 4, 128)
    c_s = cat("cso").reshape(1, 32, 3, 256)
    h_s = cat("hso").reshape(1, 32, 256)
    return (y_p, y_s, k_p, v_p, mk_p, mv_p, c_p, h_p, k_s, v_s, c_s, h_s)
```
